# Optimizing a Trainium2 kernel written in Bass

```python
import jax, jax.numpy as jnp
from jax import lax
import numpy as np

D_MODEL = 1024
BATCH = 8
SEQ = 4096
DEPTH = 2
DEC_BATCH = 16
DEC_SEQ = 2048
PAST_LEN = 128

D_FF = 2816
BLOCK = 128
SGU_WIDTH = 512
SGU_GROUPS = 4
SGU_GROUP_DIM = SGU_WIDTH // SGU_GROUPS
HEAD_DIM = 64
ATT_HEADS = 8
ATT_KV_HEADS = 2
ATT_REP = ATT_HEADS // ATT_KV_HEADS
WINDOW = 128
ROPE_THETA = 500000.0
ROPE_DIM = HEAD_DIM // 4
AB_IN = 2 * SGU_WIDTH + (ATT_HEADS + 2 * ATT_KV_HEADS) * HEAD_DIM
AB_OUT = SGU_WIDTH + ATT_HEADS * HEAD_DIM
NA_HEADS = 16
NA_WIDTH = NA_HEADS * HEAD_DIM
NA_KH = 8
NA_KW = 16
GRID_W = 64
EPS = 1e-6
NEG_INF = -1e30

kernel_name = "hybrid_bidir_encoder_gmlp_swa_natten_macaron"


def rmsnorm(x, g):
    x32 = x.astype(jnp.float32)
    y = x32 * lax.rsqrt(jnp.mean(x32 * x32, axis=-1, keepdims=True) + EPS)
    return y.astype(x.dtype) * g


def layer_norm(x, g, b):
    x32 = x.astype(jnp.float32)
    mu = jnp.mean(x32, axis=-1, keepdims=True)
    var = jnp.mean(jnp.square(x32 - mu), axis=-1, keepdims=True)
    y = (x32 - mu) * lax.rsqrt(var + EPS)
    return y.astype(x.dtype) * g + b


def swiglu(h, wg, wu, wd):
    return (jax.nn.silu(h @ wg) * (h @ wu)) @ wd


def rope_partial(x):
    L = x.shape[1]
    inv = jnp.power(ROPE_THETA, -jnp.arange(0, ROPE_DIM, 2, dtype=jnp.float32) / ROPE_DIM)
    ang = jnp.arange(L, dtype=jnp.float32)[:, None] * inv[None, :]
    cos = jnp.cos(ang)[None, :, None, :]
    sin = jnp.sin(ang)[None, :, None, :]
    xr = x[..., :ROPE_DIM].astype(jnp.float32)
    x1, x2 = xr[..., :ROPE_DIM // 2], xr[..., ROPE_DIM // 2:]
    rot = jnp.concatenate([x1 * cos - x2 * sin, x2 * cos + x1 * sin], axis=-1).astype(x.dtype)
    return jnp.concatenate([rot, x[..., ROPE_DIM:]], axis=-1)


def window_gqa(q, k, v, sink):
    B, L = q.shape[0], q.shape[1]
    nb = L // BLOCK
    qb = q.reshape(B, nb, BLOCK, ATT_KV_HEADS, ATT_REP, HEAD_DIM)
    pad = ((0, 0), (BLOCK, BLOCK), (0, 0), (0, 0))
    kp = jnp.pad(k, pad).reshape(B, nb + 2, BLOCK, ATT_KV_HEADS, HEAD_DIM)
    vp = jnp.pad(v, pad).reshape(B, nb + 2, BLOCK, ATT_KV_HEADS, HEAD_DIM)
    kw = jnp.concatenate([kp[:, :-2], kp[:, 1:-1], kp[:, 2:]], axis=2)
    vw = jnp.concatenate([vp[:, :-2], vp[:, 1:-1], vp[:, 2:]], axis=2)
    s = jnp.einsum('bnqgrd,bnkgd->bngrqk', qb, kw).astype(jnp.float32) * (HEAD_DIM ** -0.5)
    qpos = jnp.arange(nb)[:, None] * BLOCK + jnp.arange(BLOCK)[None, :]
    kpos = (jnp.arange(nb)[:, None] - 1) * BLOCK + jnp.arange(3 * BLOCK)[None, :]
    rel = kpos[:, None, :] - qpos[:, :, None]
    valid = (jnp.abs(rel) <= WINDOW) & (kpos[:, None, :] >= 0) & (kpos[:, None, :] < L)
    s = jnp.where(valid[None, :, None, None], s, NEG_INF)
    sink_b = sink.astype(jnp.float32).reshape(ATT_KV_HEADS, ATT_REP)[None, None, :, :, None, None]
    sink_b = jnp.broadcast_to(sink_b, s.shape[:-1] + (1,))
    p = jax.nn.softmax(jnp.concatenate([s, sink_b], axis=-1), axis=-1)[..., :-1]
    o = jnp.einsum('bngrqk,bnkgd->bnqgrd', p.astype(v.dtype), vw)
    return o.reshape(B, L, ATT_HEADS * HEAD_DIM)


def neighbourhood_attn(q, k, v, rpb):
    B, L = q.shape[0], q.shape[1]
    rows = L // GRID_W
    kh = min(NA_KH, rows)
    qg = q.reshape(B, rows, GRID_W, NA_HEADS, HEAD_DIM)
    kg = k.reshape(B, rows, GRID_W, NA_HEADS, HEAD_DIM)
    vg = v.reshape(B, rows, GRID_W, NA_HEADS, HEAD_DIM)
    col_start = jnp.clip(jnp.arange(GRID_W) - NA_KW // 2, 0, GRID_W - NA_KW)
    col_idx = col_start[:, None] + jnp.arange(NA_KW)[None, :]
    dc = col_idx - jnp.arange(GRID_W)[:, None] + (NA_KW - 1)
    row_start = jnp.clip(jnp.arange(rows) - kh // 2, 0, rows - kh)

    def one_row(args):
        r, rs, q_row = args
        k_band = lax.dynamic_slice_in_dim(kg, rs, kh, axis=1)
        v_band = lax.dynamic_slice_in_dim(vg, rs, kh, axis=1)
        k_win = k_band[:, :, col_idx]
        v_win = v_band[:, :, col_idx]
        dr = rs + jnp.arange(kh) - r + (NA_KH - 1)
        bias = rpb[:, dr][:, :, dc]
        s = jnp.einsum('bchd,bacjhd->bhcaj', q_row, k_win).astype(jnp.float32) * (HEAD_DIM ** -0.5)
        s = s + jnp.transpose(bias, (0, 2, 1, 3)).astype(jnp.float32)[None]
        p = jax.nn.softmax(s.reshape(B, NA_HEADS, GRID_W, kh * NA_KW), axis=-1)
        p = p.reshape(B, NA_HEADS, GRID_W, kh, NA_KW).astype(v.dtype)
        return jnp.einsum('bhcaj,bacjhd->bchd', p, v_win)

    out = lax.map(one_row, (jnp.arange(rows), row_start, jnp.transpose(qg, (1, 0, 2, 3, 4))))
    return jnp.transpose(out, (1, 0, 2, 3, 4)).reshape(B, L, NA_WIDTH)


def mixer_ab(h, w_in, sgu_ln_g, sgu_ln_b, sgu_w, sgu_b, sink, w_out):
    B, L, _ = h.shape
    z = h @ w_in
    zs = jax.nn.gelu(z[..., :2 * SGU_WIDTH], approximate=False)
    u, vh = zs[..., :SGU_WIDTH], zs[..., SGU_WIDTH:]
    vh = layer_norm(vh, sgu_ln_g, sgu_ln_b)
    nc = L // BLOCK
    vh = vh.reshape(B, nc, BLOCK, SGU_GROUPS, SGU_GROUP_DIM)
    vmix = jnp.einsum('gij,bnjgc->bnigc', sgu_w, vh) + jnp.transpose(sgu_b)[None, None, :, :, None]
    a_out = u * vmix.reshape(B, L, SGU_WIDTH)
    o = 2 * SGU_WIDTH
    q = z[..., o:o + ATT_HEADS * HEAD_DIM].reshape(B, L, ATT_HEADS, HEAD_DIM)
    o = o + ATT_HEADS * HEAD_DIM
    k = z[..., o:o + ATT_KV_HEADS * HEAD_DIM].reshape(B, L, ATT_KV_HEADS, HEAD_DIM)
    o = o + ATT_KV_HEADS * HEAD_DIM
    v = z[..., o:o + ATT_KV_HEADS * HEAD_DIM].reshape(B, L, ATT_KV_HEADS, HEAD_DIM)
    b_out = window_gqa(rope_partial(q), rope_partial(k), v, sink)
    return jnp.concatenate([a_out, b_out], axis=-1) @ w_out


def mixer_c(h, w_qkv, rpb, w_out):
    B, L, _ = h.shape
    qkv = (h @ w_qkv).reshape(B, L, 3, NA_HEADS, HEAD_DIM)
    o = neighbourhood_attn(qkv[:, :, 0], qkv[:, :, 1], qkv[:, :, 2], rpb)
    return o @ w_out


def even_layer(x, f1_n, f1_g, f1_u, f1_d, mix_n, w_in, sgu_ln_g, sgu_ln_b, sgu_w, sgu_b, sink, w_out,
               f2_n, f2_g, f2_u, f2_d):
    x = x + 0.5 * swiglu(rmsnorm(x, f1_n), f1_g, f1_u, f1_d)
    x = x + mixer_ab(rmsnorm(x, mix_n), w_in, sgu_ln_g, sgu_ln_b, sgu_w, sgu_b, sink, w_out)
    x = x + 0.5 * swiglu(rmsnorm(x, f2_n), f2_g, f2_u, f2_d)
    return x


def odd_layer(x, f1_n, f1_g, f1_u, f1_d, mix_n, w_qkv, rpb, w_out, f2_n, f2_g, f2_u, f2_d):
    x = x + 0.5 * swiglu(rmsnorm(x, f1_n), f1_g, f1_u, f1_d)
    x = x + mixer_c(rmsnorm(x, mix_n), w_qkv, rpb, w_out)
    x = x + 0.5 * swiglu(rmsnorm(x, f2_n), f2_g, f2_u, f2_d)
    return x


def trunk(x, layer_params, final_norm):
    for layer in range(DEPTH):
        if layer % 2 == 0:
            x = even_layer(x, *layer_params[layer])
        else:
            x = odd_layer(x, *layer_params[layer])
    return rmsnorm(x, final_norm)


def _w(key, shape, fan_in):
    return jax.random.normal(key, shape, jnp.float32) * (fan_in ** -0.5)


def _gain(key, shape):
    return 1.0 + 0.02 * jax.random.normal(key, shape, jnp.float32)


def setup_inputs(seed: int = 0) -> dict:
    key = jax.random.key(seed)
    ks = jax.random.split(key, 40)
    d = {}
    d['x_prompt'] = jax.random.normal(ks[0], (BATCH, SEQ, D_MODEL), jnp.float32)
    d['x_sample'] = jax.random.normal(ks[1], (DEC_BATCH, DEC_SEQ, D_MODEL), jnp.float32)
    d['l0_ffn1_norm'] = _gain(ks[2], (D_MODEL,))
    d['l0_ffn1_w_gate'] = _w(ks[3], (D_MODEL, D_FF), D_MODEL)
    d['l0_ffn1_w_up'] = _w(ks[4], (D_MODEL, D_FF), D_MODEL)
    d['l0_ffn1_w_down'] = _w(ks[5], (D_FF, D_MODEL), D_FF)
    d['l0_mix_norm'] = _gain(ks[6], (D_MODEL,))
    d['l0_w_in'] = _w(ks[7], (D_MODEL, AB_IN), D_MODEL)
    d['l0_sgu_ln_g'] = _gain(ks[8], (SGU_WIDTH,))
    d['l0_sgu_ln_b'] = 0.02 * jax.random.normal(ks[9], (SGU_WIDTH,), jnp.float32)
    d['l0_sgu_w'] = _w(ks[10], (SGU_GROUPS, BLOCK, BLOCK), BLOCK)
    d['l0_sgu_b'] = _gain(ks[11], (SGU_GROUPS, BLOCK))
    d['l0_sink'] = 0.5 * jax.random.normal(ks[12], (ATT_HEADS,), jnp.float32)
    d['l0_w_out'] = _w(ks[13], (AB_OUT, D_MODEL), AB_OUT)
    d['l0_ffn2_norm'] = _gain(ks[14], (D_MODEL,))
    d['l0_ffn2_w_gate'] = _w(ks[15], (D_MODEL, D_FF), D_MODEL)
    d['l0_ffn2_w_up'] = _w(ks[16], (D_MODEL, D_FF), D_MODEL)
    d['l0_ffn2_w_down'] = _w(ks[17], (D_FF, D_MODEL), D_FF)
    d['l1_ffn1_norm'] = _gain(ks[18], (D_MODEL,))
    d['l1_ffn1_w_gate'] = _w(ks[19], (D_MODEL, D_FF), D_MODEL)
    d['l1_ffn1_w_up'] = _w(ks[20], (D_MODEL, D_FF), D_MODEL)
    d['l1_ffn1_w_down'] = _w(ks[21], (D_FF, D_MODEL), D_FF)
    d['l1_mix_norm'] = _gain(ks[22], (D_MODEL,))
    d['l1_w_qkv'] = _w(ks[23], (D_MODEL, 3 * NA_WIDTH), D_MODEL)
    d['l1_rpb'] = 0.5 * jax.random.normal(ks[24], (NA_HEADS, 2 * NA_KH - 1, 2 * NA_KW - 1), jnp.float32)
    d['l1_w_out'] = _w(ks[25], (NA_WIDTH, D_MODEL), NA_WIDTH)
    d['l1_ffn2_norm'] = _gain(ks[26], (D_MODEL,))
    d['l1_ffn2_w_gate'] = _w(ks[27], (D_MODEL, D_FF), D_MODEL)
    d['l1_ffn2_w_up'] = _w(ks[28], (D_MODEL, D_FF), D_MODEL)
    d['l1_ffn2_w_down'] = _w(ks[29], (D_FF, D_MODEL), D_FF)
    d['final_norm'] = _gain(ks[30], (D_MODEL,))
    return d


def reference(x_prompt, x_sample,
              l0_ffn1_norm, l0_ffn1_w_gate, l0_ffn1_w_up, l0_ffn1_w_down,
              l0_mix_norm, l0_w_in, l0_sgu_ln_g, l0_sgu_ln_b, l0_sgu_w, l0_sgu_b, l0_sink, l0_w_out,
              l0_ffn2_norm, l0_ffn2_w_gate, l0_ffn2_w_up, l0_ffn2_w_down,
              l1_ffn1_norm, l1_ffn1_w_gate, l1_ffn1_w_up, l1_ffn1_w_down,
              l1_mix_norm, l1_w_qkv, l1_rpb, l1_w_out,
              l1_ffn2_norm, l1_ffn2_w_gate, l1_ffn2_w_up, l1_ffn2_w_down,
              final_norm):
    p0 = (l0_ffn1_norm, l0_ffn1_w_gate, l0_ffn1_w_up, l0_ffn1_w_down,
          l0_mix_norm, l0_w_in, l0_sgu_ln_g, l0_sgu_ln_b, l0_sgu_w, l0_sgu_b, l0_sink, l0_w_out,
          l0_ffn2_norm, l0_ffn2_w_gate, l0_ffn2_w_up, l0_ffn2_w_down)
    p1 = (l1_ffn1_norm, l1_ffn1_w_gate, l1_ffn1_w_up, l1_ffn1_w_down,
          l1_mix_norm, l1_w_qkv, l1_rpb, l1_w_out,
          l1_ffn2_norm, l1_ffn2_w_gate, l1_ffn2_w_up, l1_ffn2_w_down)
    layer_params = [p0, p1]
    y_prompt = trunk(x_prompt, layer_params, final_norm)
    y_sample = trunk(x_sample, layer_params, final_norm)
    return (y_prompt, y_sample)
```

```python
import numpy as np
from contextlib import ExitStack
import concourse.bass as bass
import concourse.mybir as mybir
from concourse.bass_utils import run_bass_kernel_spmd

F32 = mybir.dt.float32
BF16 = mybir.dt.bfloat16
AF = mybir.ActivationFunctionType
ALU = mybir.AluOpType

D = 1024
DFF = 2816
NFC = DFF // 128
EPS = 1e-6
SEQS = (4096, 2048, 2048)

WSPECS = [
    ("l0_ffn1_norm", (1024,)), ("l0_ffn1_w_gate", (1024, 2816)), ("l0_ffn1_w_up", (1024, 2816)),
    ("l0_ffn1_w_down", (2816, 1024)), ("l0_mix_norm", (1024,)), ("l0_w_in", (1024, 1792)),
    ("l0_sgu_ln_g", (512,)), ("l0_sgu_ln_b", (512,)), ("l0_sgu_w", (4, 128, 128)), ("l0_sgu_b", (4, 128)),
    ("l0_sink", (8,)), ("l0_w_out", (1024, 1024)), ("l0_ffn2_norm", (1024,)),
    ("l0_ffn2_w_gate", (1024, 2816)), ("l0_ffn2_w_up", (1024, 2816)), ("l0_ffn2_w_down", (2816, 1024)),
    ("l1_ffn1_norm", (1024,)), ("l1_ffn1_w_gate", (1024, 2816)), ("l1_ffn1_w_up", (1024, 2816)),
    ("l1_ffn1_w_down", (2816, 1024)), ("l1_mix_norm", (1024,)), ("l1_w_qkv", (1024, 3072)),
    ("l1_rpb", (16, 15, 31)), ("l1_w_out", (1024, 1024)), ("l1_ffn2_norm", (1024,)),
    ("l1_ffn2_w_gate", (1024, 2816)), ("l1_ffn2_w_up", (1024, 2816)), ("l1_ffn2_w_down", (2816, 1024)),
    ("final_norm", (1024,)),
]


class Res:
    __slots__ = ("name", "w", "r", "dsem", "dcnt")

    def __init__(self, name):
        self.name = name
        self.w = None
        self.r = {}
        self.dsem = None
        self.dcnt = 0


class Sched:
    ENGS = ("pe", "act", "dve", "pool", "sp")

    def __init__(self, nc, stack):
        self.nc = nc
        self.stack = stack
        self.q = {e: [] for e in self.ENGS}
        self.cnt = {e: 0 for e in self.ENGS}
        self.sem = {}
        self.seen = {e: {} for e in self.ENGS}
        self.nsem = 0
        for e in ("pe", "act", "dve", "pool"):
            self.sem[e] = self.new_sem("c_" + e)
        self.out_tokens = []
        self.free_d = []
        self.live_d = []
        self.dma_hi = {}

    def new_sem(self, name):
        self.nsem += 1
        return self.stack.enter_context(self.nc.semaphore(f"{name}_{self.nsem}"))

    def _waits(self, eng, reads, writes, nowaw=False):
        waits = {}

        def need(tok):
            if tok is None:
                return
            sem, val = tok
            if eng == "pe" and sem is self.sem["pe"]:
                return
            k = id(sem)
            if self.seen[eng].get(k, 0) >= val:
                return
            if k not in waits or waits[k][1] < val:
                waits[k] = (sem, val)

        for r in reads:
            need(r.w)
        for w in writes:
            if not nowaw:
                need(w.w)
            for tok in w.r.values():
                need(tok)
        for k, (sem, val) in waits.items():
            self.seen[eng][k] = val
        return list(waits.values())

    def _commit(self, tok, reads, writes):
        k = id(tok[0])
        for r in reads:
            old = r.r.get(k)
            if old is None or old[1] < tok[1]:
                r.r[k] = tok
        for w in writes:
            w.w = tok
            w.r = {}

    tag = ""

    def op(self, eng, fn, reads=(), writes=()):
        import os
        if self.tag and self.tag in os.environ.get("SKIP", "").split(","):
            return None
        waits = self._waits(eng, reads, writes)
        self.cnt[eng] += 1
        tok = (self.sem[eng], self.cnt[eng])
        self._commit(tok, reads, writes)
        self.q[eng].append((waits, fn, tok, 1))
        return tok

    def dma(self, eng, fn, reads=(), writes=(), sem_res=None, is_output=False, nowaw=False):
        import os
        if self.tag and self.tag in os.environ.get("SKIP", "").split(","):
            return None
        waits = self._waits(eng, reads, writes, nowaw)
        if sem_res is None:
            sem_res = writes[0]
        if sem_res.dsem is None:
            if self.free_d:
                sem_res.dsem, sem_res.dcnt = self.free_d.pop()
            else:
                sem_res.dsem, sem_res.dcnt = self.new_sem("d"), 0
            self.live_d.append(sem_res)
        sem_res.dcnt += 16
        tok = (sem_res.dsem, sem_res.dcnt)
        self.dma_hi[id(tok[0])] = tok
        self._commit(tok, reads, writes)
        self.q[eng].append((waits, fn, tok, 16))
        if is_output:
            self.out_tokens.append(tok)
        return tok

    def barrier(self):
        toks = [(self.sem[e], self.cnt[e]) for e in ("pe", "act", "dve", "pool") if self.cnt[e] > 0]
        toks += list(self.dma_hi.values())
        for e in self.ENGS:
            ws = []
            for sem, val in toks:
                if self.seen[e].get(id(sem), 0) < val:
                    ws.append((sem, val))
                    self.seen[e][id(sem)] = val
            if ws:
                self.q[e].append((ws, None, None, 0))
        for r in self.live_d:
            self.free_d.append((r.dsem, r.dcnt))
            r.dsem = None
        self.live_d = []

    def flush(self, final=False):
        nc = self.nc
        q = self.q
        fin = []
        if final:
            d = {}
            for sem, val in self.out_tokens:
                if id(sem) not in d or d[id(sem)][1] < val:
                    d[id(sem)] = (sem, val)
            fin = list(d.values())
        csem = {id(self.sem[e]): e for e in ("pe", "act", "dve", "pool")}
        needed = {e: set() for e in csem.values()}
        for e in self.ENGS:
            for waits, fn, tok, inc in q[e]:
                for sem, val in waits:
                    if id(sem) in csem:
                        needed[csem[id(sem)]].add(val)
        if not hasattr(self, "base"):
            self.base = {e: 0 for e in csem.values()}
        remap = {}
        for e, vals in needed.items():
            for rank, v in enumerate(sorted(vals)):
                remap[(e, v)] = self.base[e] + rank + 1
            self.base[e] += len(vals)

        def tr(sem, val):
            if id(sem) in csem:
                return remap[(csem[id(sem)], val)]
            return val

        def run(e, items, extra=()):
            for waits, fn, tok, inc in items:
                for sem, val in waits:
                    e.wait_ge(sem, tr(sem, val))
                if fn is not None:
                    ins = fn(e)
                    if inc == 16:
                        ins.then_inc(tok[0], 16)
                    elif (csem[id(tok[0])], tok[1]) in remap:
                        ins.then_inc(tok[0], 1)
            for sem, val in extra:
                e.wait_ge(sem, val)

        with nc.Block() as block:
            @block.sync
            def _(e):
                run(e, q["sp"], fin)

            @block.gpsimd
            def _(e):
                run(e, q["pool"])

            @block.tensor
            def _(e):
                run(e, q["pe"])

            @block.scalar
            def _(e):
                run(e, q["act"])

            @block.vector
            def _(e):
                run(e, q["dve"])
        self.q = {e: [] for e in self.ENGS}
        self.n_inc = getattr(self, "n_inc", 0) + sum(len(v) for v in needed.values())


def mm_group(S, out_ap, pairs, reads, writes):
    def f(q, pairs=pairs, out_ap=out_ap):
        n = len(pairs)
        ins = None
        for i, (l, r) in enumerate(pairs):
            ins = q.matmul(out_ap, lhsT=l, rhs=r, start=(i == 0), stop=(i == n - 1))
        return ins
    S.op("pe", f, reads, writes)


class Ctx:
    pass


_UC = [0]


def U(n):
    _UC[0] += 1
    return f"{n}_{_UC[0]}"


def rmsnorm_block(S, C, x_ap, x_res, gt, gt_res, hb, hb_res, st, st_res, col):
    S.op("act", lambda q: q.activation(out=C.junk[:], in_=x_ap, func=AF.Square, scale=1.0 / 32.0,
                                       accum_out=st[:, col:col + 1]), reads=[x_res], writes=[st_res])
    S.op("act", lambda q: q.activation(out=st[:, col:col + 1], in_=st[:, col:col + 1], func=AF.Ln,
                                       bias=C.eps[:, 0:1], scale=1.0), reads=[st_res, C.const_r], writes=[st_res])
    S.op("act", lambda q: q.activation(out=st[:, col:col + 1], in_=st[:, col:col + 1], func=AF.Exp, scale=-0.5), reads=[st_res], writes=[st_res])
    S.op("dve", lambda q: q.scalar_tensor_tensor(out=hb, in0=x_ap, scalar=st[:, col:col + 1], in1=gt[:],
                                                 op0=ALU.mult, op1=ALU.mult),
         reads=[x_res, st_res, gt_res], writes=[hb_res])


def transpose_block(S, C, hb, hb_res, ptr, ptr_res, dst3, dst_res, eng="act"):
    def f(q):
        ins = None
        for kc in range(8):
            ins = q.transpose(out=ptr[:, kc * 128:(kc + 1) * 128], in_=hb[:, kc * 128:(kc + 1) * 128], identity=C.ident[:])
        return ins
    S.op("pe", f, reads=[hb_res, C.const_r], writes=[ptr_res])
    src = ptr[:, 0:1024].rearrange("p (k t) -> p k t", k=8)
    if eng == "act":
        S.op("act", lambda q: q.copy(out=dst3, in_=src), reads=[ptr_res], writes=[dst_res])
    else:
        S.op("dve", lambda q: q.tensor_copy(out=dst3, in_=src), reads=[ptr_res], writes=[dst_res])


def load_gain(S, C, ph, nc, g_ap, name):
    gt = ph.enter_context(nc.sbuf_tensor(U(name), [128, 1024], F32))
    gt_r = Res(name)
    S.dma("sp", lambda q: q.dma_start(out=C.vrow[:], in_=g_ap.unsqueeze(0)), writes=[C.vrow_r])
    def f(q):
        q.matmul(C.pbc[:, 0:512], lhsT=C.ones_row[:], rhs=C.vrow[:, 0:512], start=True, stop=True)
        return q.matmul(C.pbc[:, 512:1024], lhsT=C.ones_row[:], rhs=C.vrow[:, 512:1024], start=True, stop=True)
    S.op("pe", f, reads=[C.vrow_r, C.const_r], writes=[C.pbc_r])
    S.op("act", lambda q: q.copy(out=gt[:], in_=C.pbc[:]), reads=[C.pbc_r], writes=[gt_r])
    return gt, gt_r


def ffn_phase(S, C, nc, x_src, src_res, x_dst, dst_res, g_ap, wg, wu, wd, ntok, final_g=None, is_output=False):
    TM = 1024
    NB = TM // 128
    NX = 4
    NO = 3
    groups = [(0, 4), (4, 4), (8, 4), (12, 4), (16, 4), (20, 2)]
    with ExitStack() as ph:
        sb = lambda n, sh, dt: ph.enter_context(nc.sbuf_tensor(U(n), sh, dt))
        pst = lambda n, sh, dt: ph.enter_context(nc.psum_tensor(U(n), sh, dt))
        C.pbc = pst("pbc", [128, 1024], F32); C.pbc_r = Res("pbc")
        gt, gt_r = load_gain(S, C, ph, nc, g_ap, "gt")
        if final_g is not None:
            fgt, fgt_r = load_gain(S, C, ph, nc, final_g, "fgt")
        xt = sb("xt", [128, NX, 1024], F32); xt_r = [Res(f"xt{b}") for b in range(NX)]
        hb = [sb(f"hb{i}", [128, 1024], BF16) for i in range(2)]; hb_r = [Res(f"hb{i}") for i in range(2)]
        hT = [sb(f"hT{i}", [128, 8, TM], BF16) for i in range(2)]
        hT_r = [[Res(f"hT{i}_{b}") for b in range(NB)] for i in range(2)]
        aT = sb("aT", [128, NFC, TM], BF16); aT_r = [[Res(f"aT{j}_{s}") for s in range(2)] for j in range(NFC)]
        wdt = sb("wdt", [128, NFC, 1024], BF16); wd_r = [Res(f"wd{g}") for g in range(len(groups))]
        wgt = [sb(f"wgt{i}", [128, 8, 512], BF16) for i in range(2)]; wg_r = [Res(f"wg{i}") for i in range(2)]
        wut = [sb(f"wut{i}", [128, 8, 512], BF16) for i in range(2)]; wu_r = [Res(f"wu{i}") for i in range(2)]
        sg = [sb(f"sg{i}", [128, 512], F32) for i in range(2)]; sg_r = [Res(f"sg{i}") for i in range(2)]
        ost = [sb(f"ost{i}", [128, 1024], F32) for i in range(NO)]; ost_r = [Res(f"ost{i}") for i in range(NO)]
        stt = sb("stt", [128, 16], F32); stt_r = [Res(f"stt{b}") for b in range(16)]
        pg = [pst(f"pg{i}", [128, 512], F32) for i in range(2)]; pg_r = [Res(f"pg{i}") for i in range(2)]
        pu = [pst(f"pu{i}", [128, 512], F32) for i in range(2)]; pu_r = [Res(f"pu{i}") for i in range(2)]
        ptr = C.pbc.bitcast(BF16); ptr_r = C.pbc_r
        py = [pst(f"py{i}", [128, 512], F32) for i in range(2)]; py_r = [Res(f"py{i}") for i in range(2)]

        wg_v = wg.rearrange("(kc p) f -> p kc f", p=128)
        wu_v = wu.rearrange("(kc p) f -> p kc f", p=128)
        wd_v = wd.rearrange("(fc p) d -> p fc d", p=128)

        def issue_gu(gi):
            f0, nf = groups[gi]
            sl = gi % 2
            S.dma("pool", lambda q: q.dma_start(out=wgt[sl][:, :, 0:nf * 128], in_=wg_v[:, :, f0 * 128:(f0 + nf) * 128]), writes=[wg_r[sl]])
            S.dma("pool", lambda q: q.dma_start(out=wut[sl][:, :, 0:nf * 128], in_=wu_v[:, :, f0 * 128:(f0 + nf) * 128]), writes=[wu_r[sl]])

        def issue_wd(gi):
            f0, nf = groups[gi]
            S.dma("pool", lambda q: q.dma_start(out=wdt[:, f0:f0 + nf, :], in_=wd_v[:, f0:f0 + nf, :]), writes=[wd_r[gi]])

        nmt = ntok // TM
        pcnt = 0
        ycnt = 0
        xcnt = [0]

        def stage1_front(mt, b):
            blk = (mt * TM // 128) + b
            xs = xcnt[0] % NX; xcnt[0] += 1
            sl = b % 2
            S.dma("sp", lambda q: q.dma_start(out=xt[:, xs, :], in_=x_src[blk * 128:(blk + 1) * 128, :]), reads=[src_res[blk]], writes=[xt_r[xs]])
            rmsnorm_block(S, C, xt[:, xs, :], xt_r[xs], gt, gt_r, hb[sl][:], hb_r[sl], stt, stt_r[b], b)

        def stage1_back(mt, b):
            sl = b % 2
            hs = mt % 2
            transpose_block(S, C, hb[sl], hb_r[sl], ptr, ptr_r, hT[hs][:, :, b * 128:(b + 1) * 128], hT_r[hs][b])

        def load_res(mt, tb):
            blk = (mt * TM // 128) + tb
            o = blk % NO
            S.dma("sp", lambda q: q.dma_start(out=ost[o][:], in_=x_src[blk * 128:(blk + 1) * 128, :]), reads=[src_res[blk]], writes=[ost_r[o]])

        issue_gu(0); issue_gu(1)
        for b in range(NB):
            stage1_front(0, b)
            stage1_back(0, b)
        for mt in range(nmt):
            t0 = mt * TM
            hs = mt % 2
            if mt > 0:
                pass
            for gi, (f0, nf) in enumerate(groups):
                sl = gi % 2
                for jl in range(nf):
                    j = f0 + jl
                    for s in range(2):
                        p = pcnt % 2; pcnt += 1
                        hrd = hT_r[hs][s * 4:(s + 1) * 4]
                        mm_group(S, pg[p][:], [(wgt[sl][:, kc, jl * 128:(jl + 1) * 128], hT[hs][:, kc, s * 512:(s + 1) * 512]) for kc in range(8)],
                                 reads=hrd + [wg_r[sl]], writes=[pg_r[p]])
                        mm_group(S, pu[p][:], [(wut[sl][:, kc, jl * 128:(jl + 1) * 128], hT[hs][:, kc, s * 512:(s + 1) * 512]) for kc in range(8)],
                                 reads=hrd + [wu_r[sl]], writes=[pu_r[p]])
                        S.op("act", lambda q, p=p: q.activation(out=sg[p][:], in_=pg[p][:], func=AF.Silu), reads=[pg_r[p]], writes=[sg_r[p]])
                        S.op("dve", lambda q, p=p, j=j, s=s: q.tensor_tensor(out=aT[:, j, s * 512:(s + 1) * 512], in0=sg[p][:], in1=pu[p][:], op=ALU.mult),
                             reads=[sg_r[p], pu_r[p]], writes=[aT_r[j][s]])
                issue_wd(gi)
                if gi + 2 < len(groups):
                    issue_gu(gi + 2)
            if mt + 1 < nmt:
                issue_gu(0); issue_gu(1)
            load_res(mt, 0)
            for tb in range(NB):
                blk = (t0 // 128) + tb
                o = blk % NO
                if tb + 1 < NB:
                    load_res(mt, tb + 1)
                if mt + 1 < nmt:
                    stage1_front(mt + 1, tb)
                    if tb >= 1:
                        stage1_back(mt + 1, tb - 1)
                for dh in range(2):
                    y = ycnt % 2; ycnt += 1
                    mm_group(S, py[y][:], [(aT[:, j, tb * 128:(tb + 1) * 128], wdt[:, j, dh * 512:(dh + 1) * 512]) for j in range(NFC)],
                             reads=[aT_r[j][tb // 4] for j in range(NFC)] + wd_r, writes=[py_r[y]])
                    S.op("dve", lambda q, y=y, o=o, dh=dh: q.scalar_tensor_tensor(
                        out=ost[o][:, dh * 512:(dh + 1) * 512], in0=py[y][:], scalar=0.5, in1=ost[o][:, dh * 512:(dh + 1) * 512],
                        op0=ALU.mult, op1=ALU.add), reads=[py_r[y], ost_r[o]], writes=[ost_r[o]])
                if final_g is not None:
                    rmsnorm_block(S, C, ost[o][:], ost_r[o], fgt, fgt_r, ost[o][:], ost_r[o], stt, stt_r[8 + tb], 8 + tb)
                S.dma("sp", lambda q, o=o, blk=blk: q.dma_start(out=x_dst[blk * 128:(blk + 1) * 128, :], in_=ost[o][:]),
                      reads=[ost_r[o]], writes=[dst_res[blk]], sem_res=ost_r[o], is_output=is_output)
            if mt + 1 < nmt:
                stage1_back(mt + 1, NB - 1)
        S.barrier()
        S.flush()


def mixer_ab_phase(S, C, nc, W, x_src, src_res, x_dst, dst_res, seqs):
    with ExitStack() as ph:
        sb = lambda n, sh, dt: ph.enter_context(nc.sbuf_tensor(U(n), sh, dt))
        pst = lambda n, sh, dt: ph.enter_context(nc.psum_tensor(U(n), sh, dt))
        C.pbc = pst("pbc", [128, 1024], F32); C.pbc_r = Res("pbc")
        gt, gt_r = load_gain(S, C, ph, nc, W["l0_mix_norm"], "gt")
        T0 = pst("T0", [128, 1024], BF16); T0_r = Res("T0")
        A0 = pst("A0", [128, 512], F32); A0_r = Res("A0")
        A1 = pst("A1", [128, 512], F32); A1_r = Res("A1")
        ST = [pst(f"ST{i}", [128, 512], F32) for i in range(2)]; ST_r = [Res(f"ST{i}") for i in range(2)]
        PO = pst("PO", [128, 512], F32); PO_r = Res("PO")
        PQ = C.pbc; PQ_r = C.pbc_r

        win = sb("win", [128, 8, 1792], BF16); win_r = Res("win")
        win_v = W["l0_w_in"].rearrange("(kc p) f -> p kc f", p=128)
        for h in range(2):
            S.dma("pool", lambda q, h=h: q.dma_start(out=win[:, :, h * 896:(h + 1) * 896], in_=win_v[:, :, h * 896:(h + 1) * 896]),
                  writes=[win_r], sem_res=Res("winl"))
        woA = sb("woA", [128, 4, 1024], BF16); woA_r = Res("woA")
        woB = sb("woB", [64, 8, 1024], BF16); woB_r = Res("woB")
        S.dma("pool", lambda q: q.dma_start(out=woA[:], in_=W["l0_w_out"][0:512, :].rearrange("(c p) d -> p c d", p=128)), writes=[woA_r])
        S.dma("pool", lambda q: q.dma_start(out=woB[:], in_=W["l0_w_out"][512:1024, :].rearrange("(h p) d -> p h d", p=64)), writes=[woB_r])
        S.tag = "wmask"
        wmask = sb("wmask", [128, 2, 512], BF16); wmask_r = Res("wmask")
        S.dma("pool", lambda q: q.dma_start(out=wmask[:], in_=C.wmask_in), writes=[wmask_r])
        S.tag = "rope"
        rope = sb("rope", [128, 32, 32], F32); rope_r = Res("rope")
        S.dma("sp", lambda q: q.dma_start(out=rope[:], in_=C.rope_in.rearrange("(b p) c -> p b c", p=128)), writes=[rope_r])
        S.tag = "tr"
        sw = sb("sw", [128, 4, 128], F32); sw_r = Res("sw")
        S.dma("sp", lambda q: q.dma_start(out=sw[:], in_=W["l0_sgu_w"].rearrange("g i j -> i g j")), writes=[sw_r])
        WT = sb("WT", [128, 4, 128], BF16); WT_r = Res("WT")
        WTf = sb("WTf", [128, 4, 128], F32); WTf_r = Res("WTf")
        def ftr(q):
            ins = None
            for g in range(4):
                ins = q.transpose(out=A0[:, g * 128:(g + 1) * 128], in_=sw[:, g, :], identity=C.identf[:])
            return ins
        S.op("pe", ftr, reads=[sw_r, C.const_r], writes=[A0_r])
        S.op("act", lambda q: q.copy(out=WTf[:], in_=A0[:].rearrange("p (g i) -> p g i", g=4)), reads=[A0_r], writes=[WTf_r])
        S.op("dve", lambda q: q.tensor_copy(out=WT[:], in_=WTf[:]), reads=[WTf_r], writes=[WT_r])
        S.tag = "k2"
        k2l = sb("k2l", [2, 4, 128], F32); k2l_r = Res("k2l")
        k2r = sb("k2r", [2, 4, 128], F32); k2r_r = Res("k2r")
        S.op("dve", lambda q: q.memset(k2l[:], 1.0), writes=[k2l_r])
        S.dma("sp", lambda q: q.dma_start(out=k2l[0:1, :, :], in_=W["l0_sgu_ln_b"].rearrange("(o g c) -> o g c", o=1, g=4)), writes=[k2l_r])
        S.dma("sp", lambda q: q.dma_start(out=k2r[1:2, :, :], in_=W["l0_sgu_b"].unsqueeze(0)), writes=[k2r_r])
        S.op("pe", lambda q: q.matmul(A1[0:1, :], lhsT=C.ones_col[:, 0:1], rhs=WTf[:].rearrange("p g i -> p (g i)"), start=True, stop=True),
             reads=[WTf_r, C.const_r], writes=[A1_r])
        S.op("act", lambda q: q.copy(out=k2r[0:1, :, :], in_=A1[0:1, :].rearrange("p (g i) -> p g i", g=4)), reads=[A1_r], writes=[k2r_r])
        B2 = sb("B2", [128, 4, 128], F32); B2_r = Res("B2")
        def fb2(q):
            ins = None
            for g in range(4):
                ins = q.matmul(A0[:, g * 128:(g + 1) * 128], lhsT=k2l[:, g, :], rhs=k2r[:, g, :], start=True, stop=True)
            return ins
        S.op("pe", fb2, reads=[k2l_r, k2r_r], writes=[A0_r])
        S.op("act", lambda q: q.copy(out=B2[:], in_=A0[:].rearrange("p (g i) -> p g i", g=4)), reads=[A0_r], writes=[B2_r])
        S.tag = "lg"
        lg = sb("lg", [128, 4], F32); lg_r = Res("lg")
        S.dma("sp", lambda q: q.dma_start(out=lg[:], in_=W["l0_sgu_ln_g"].rearrange("(g c) -> c g", g=4), allow_slow_non_contiguous=True), writes=[lg_r])
        S.tag = "es"
        S.dma("sp", lambda q: q.dma_start(out=C.vrow[:, 0:8], in_=W["l0_sink"].unsqueeze(0)), writes=[C.vrow_r])
        S.op("pe", lambda q: q.matmul(A1[:, 0:8], lhsT=C.ones_row[:], rhs=C.vrow[:, 0:8], start=True, stop=True), reads=[C.vrow_r, C.const_r], writes=[A1_r])
        es8 = sb("es8", [128, 8], F32); es8_r = Res("es8")
        S.op("act", lambda q: q.activation(out=es8[:], in_=A1[:, 0:8], func=AF.Exp), reads=[A1_r], writes=[es8_r])
        es = sb("es", [128, 8, 128], F32); es_r = Res("es")
        S.op("dve", lambda q: q.tensor_copy(out=es[:], in_=es8[:].unsqueeze(2).broadcast_to([128, 8, 128])), reads=[es8_r], writes=[es_r])

        S.tag = ""
        NS = 4
        xin = [sb(f"xin{i}", [128, 1024], F32) for i in range(NS)]; xin_r = [Res(f"xin{i}") for i in range(NS)]
        hb = [sb(f"hb{i}", [128, 1024], BF16) for i in range(2)]; hb_r = [Res(f"hb{i}") for i in range(2)]
        hT = [sb(f"hT{i}", [128, 8, 128], BF16) for i in range(2)]; hT_r = [Res(f"hT{i}") for i in range(2)]
        uT = [sb(f"uT{i}", [128, 4, 128], BF16) for i in range(2)]; uT_r = [Res(f"uT{i}") for i in range(2)]
        vh2 = [sb(f"vh{i}", [128, 512], F32) for i in range(2)]; vh2_r = [Res(f"vh{i}") for i in range(2)]
        nrm = sb("nrm", [128, 512], BF16); nrm_r = Res("nrm")
        bst = sb("bst", [128, 8], F32); bst_r = Res("bst")
        tsg = sb("tsg", [128, 4, 128], F32); tsg_r = Res("tsg")
        aoT = [sb(f"aoT{i}", [128, 4, 128], BF16) for i in range(NS)]; aoT_r = [Res(f"aoT{i}") for i in range(NS)]
        qk = sb("qk", [128, 10, 64], BF16); qk_r = Res("qk")
        qkf2 = [sb(f"qkf{i}", [128, 640], F32) for i in range(2)]; qkf2_r = [Res(f"qkf{i}") for i in range(2)]
        rt = sb("rt", [128, 2, 10, 16], F32); rt_r = Res("rt")
        qT = [sb(f"qT{i}", [64, 8, 128], BF16) for i in range(NS)]; qT_r = [Res(f"qT{i}") for i in range(NS)]
        kT = sb("kT", [64, 2, 4096], BF16); kT_r = [Res(f"kT{b}") for b in range(32)]
        vp = sb("vp", [128, 32, 2, 128], BF16); vp_r = [Res(f"vp{b}") for b in range(32)]
        PT = [sb(f"PT{i}", [128, 512], BF16) for i in range(2)]; PT_r = [Res(f"PT{i}") for i in range(2)]
        den = sb("den", [128, 512], F32); den_r = Res("den")
        rec = sb("rec", [64, 512], F32); rec_r = Res("rec")
        boT = [sb(f"boT{i}", [64, 8, 128], BF16) for i in range(2)]; boT_r = [Res(f"boT{i}") for i in range(2)]
        ost = [sb(f"ost{i}", [128, 1024], F32) for i in range(2)]; ost_r = [Res(f"ost{i}") for i in range(2)]
        stt = sb("stt", [128, 4], F32); stt_r = [Res(f"stt{i}") for i in range(4)]
        S.tag = "vp"
        S.op("pool", lambda q: q.memset(vp[:], 1.0), writes=vp_r)
        S.tag = ""

        pcnt = [0]
        blk0 = 0
        for L in seqs:
            nb = L // 128

            def stage_a(i, blk0=blk0):
              blk = blk0 + i
              s3 = i % NS; s2 = i % 2
              h_ = hT[s2]
              vh = vh2[s2]; vh_r = vh2_r[s2]
              qkf = qkf2[s2]; qkf_r = qkf2_r[s2]
              pieces = []
              late = []

              def piece(f):
                  pieces.append(f)
                  return f

              def latep(f):
                  late.append(f)
                  return f

              @piece
              def _p0():
                S.dma("sp", lambda q: q.dma_start(out=xin[s3][:], in_=x_src[blk * 128:(blk + 1) * 128, :]), reads=[src_res[blk]], writes=[xin_r[s3]])
                rmsnorm_block(S, C, xin[s3][:], xin_r[s3], gt, gt_r, hb[s2][:], hb_r[s2], stt, stt_r[s2], s2)
                transpose_block(S, C, hb[s2], hb_r[s2], T0, T0_r, hT[s2][:], hT_r[s2])

              @piece
              def _p1():
                pass
                S.tag = "u"
                def fu(q):
                    ins = None
                    for c in range(4):
                        for kc in range(8):
                            ins = q.matmul(A0[:, c * 128:(c + 1) * 128], lhsT=win[:, kc, c * 128:(c + 1) * 128], rhs=h_[:, kc, :], start=(kc == 0), stop=(kc == 7))
                    return ins
                S.op("pe", fu, reads=[hT_r[s2], win_r], writes=[A0_r])
                S.op("act", lambda q: q.activation(out=uT[s2][:], in_=A0[:].rearrange("p (c t) -> p c t", c=4), func=AF.Gelu), reads=[A0_r], writes=[uT_r[s2]])
              @piece
              def _p2():
                S.tag = "v"
                mm_group(S, A1[:], [(h_[:, kc, :], win[:, kc, 512:1024]) for kc in range(8)], reads=[hT_r[s2], win_r], writes=[A1_r])
                S.op("act", lambda q: q.activation(out=vh[:], in_=A1[:], func=AF.Gelu), reads=[A1_r], writes=[vh_r])
              @latep
              def _p2b():
                S.tag = "v"
                S.op("dve", lambda q: q.bn_stats(out=bst[:, 0:6], in_=vh[:]), reads=[vh_r], writes=[bst_r])
                S.op("dve", lambda q: q.bn_aggr(out=bst[:, 6:8], in_=bst[:, 0:6]), reads=[bst_r], writes=[bst_r])
                S.op("act", lambda q: q.activation(out=bst[:, 7:8], in_=bst[:, 7:8], func=AF.Ln, bias=C.eps[:, 0:1], scale=1.0), reads=[bst_r, C.const_r], writes=[bst_r])
                S.op("act", lambda q: q.activation(out=bst[:, 7:8], in_=bst[:, 7:8], func=AF.Exp, scale=-0.5), reads=[bst_r], writes=[bst_r])
                S.op("dve", lambda q: q.scalar_tensor_tensor(out=bst[:, 6:7], in0=bst[:, 6:7], scalar=-1.0, in1=bst[:, 7:8], op0=ALU.mult, op1=ALU.mult), reads=[bst_r], writes=[bst_r])
                S.op("act", lambda q: q.activation(out=nrm[:], in_=vh[:], func=AF.Identity, bias=bst[:, 6:7], scale=bst[:, 7:8]), reads=[vh_r, bst_r], writes=[nrm_r])
              @piece
              def _p3():
                S.tag = "qkv"
                mm_group(S, PQ[:, 0:512], [(h_[:, kc, :], win[:, kc, 1024:1536]) for kc in range(8)], reads=[hT_r[s2], win_r], writes=[PQ_r])
                mm_group(S, PQ[:, 512:768], [(h_[:, kc, :], win[:, kc, 1536:1792]) for kc in range(8)], reads=[hT_r[s2], win_r], writes=[PQ_r])
                S.op("act", lambda q: q.copy(out=qkf[:], in_=PQ[:, 0:640]), reads=[PQ_r], writes=[qkf_r])
                S.op("act", lambda q: q.copy(out=vp[:, i, :, 0:64], in_=PQ[:, 640:768].rearrange("p (g d) -> p g d", g=2)), reads=[PQ_r], writes=[vp_r[i]])
              @latep
              def _p3b():
                qk_ps = qkf[:].rearrange("p (h d) -> p h d", h=10)
                cc = rope[:, i, 0:16].unsqueeze(1).broadcast_to([128, 10, 16])
                sn = rope[:, i, 16:32]
                S.tag = "c1"
                S.op("act", lambda q: q.copy(out=qk[:, :, 16:64], in_=qk_ps[:, :, 16:64]), reads=[qkf_r], writes=[qk_r])
                S.tag = "r1"
                S.op("dve", lambda q: q.tensor_tensor(out=rt[:, 0, :, :], in0=qk_ps[:, :, 0:16], in1=cc, op=ALU.mult), reads=[qkf_r, rope_r], writes=[rt_r])
                S.tag = "r2"
                S.op("dve", lambda q: q.tensor_tensor(out=rt[:, 1, :, 0:8], in0=qk_ps[:, :, 8:16], in1=sn[:, 0:8].unsqueeze(1).broadcast_to([128, 10, 8]), op=ALU.mult),
                     reads=[qkf_r, rope_r], writes=[rt_r])
                S.op("dve", lambda q: q.tensor_tensor(out=rt[:, 1, :, 8:16], in0=qk_ps[:, :, 0:8], in1=sn[:, 8:16].unsqueeze(1).broadcast_to([128, 10, 8]), op=ALU.mult),
                     reads=[qkf_r, rope_r], writes=[rt_r])
                S.op("dve", lambda q: q.tensor_tensor(out=qk[:, :, 0:16], in0=rt[:, 0, :, :], in1=rt[:, 1, :, :], op=ALU.add), reads=[rt_r], writes=[qk_r])
              @latep
              def _p4():
                S.tag = "sg"
                def fs(q):
                    ins = None
                    for g in range(4):
                        ins = q.matmul(A1[:, g * 128:(g + 1) * 128], lhsT=nrm[:, g * 128:(g + 1) * 128], rhs=WT[:, g, :], start=True, stop=True)
                    return ins
                S.op("pe", fs, reads=[nrm_r, WT_r], writes=[A1_r])
                for g in range(4):
                    S.op("dve", lambda q, g=g: q.scalar_tensor_tensor(out=tsg[:, g, :], in0=A1[:, g * 128:(g + 1) * 128], scalar=lg[:, g:g + 1], in1=B2[:, g, :],
                                                                     op0=ALU.mult, op1=ALU.add), reads=[A1_r, lg_r, B2_r], writes=[tsg_r])
                S.op("dve", lambda q: q.tensor_tensor(out=aoT[s3][:], in0=tsg[:], in1=uT[s2][:], op=ALU.mult), reads=[tsg_r, uT_r[s2]], writes=[aoT_r[s3]])
              @latep
              def _p5():
                S.tag = "tq"
                def ftq(q):
                    ins = None
                    for h in range(8):
                        ins = q.transpose(out=T0[0:64, h * 128:(h + 1) * 128], in_=qk[:, h, :], identity=C.ident[:])
                    return ins
                S.op("pe", ftq, reads=[qk_r, C.const_r], writes=[T0_r])
                S.op("dve", lambda q: q.tensor_copy(out=qT[s3][:], in_=T0[0:64, :].rearrange("p (h t) -> p h t", h=8)), reads=[T0_r], writes=[qT_r[s3]])
                def ftk(q):
                    ins = None
                    for g in range(2):
                        ins = q.transpose(out=T0[0:64, g * 128:(g + 1) * 128], in_=qk[:, 8 + g, :], identity=C.ident[:])
                    return ins
                S.op("pe", ftk, reads=[qk_r, C.const_r], writes=[T0_r])
                S.op("dve", lambda q: q.tensor_copy(out=kT[:, :, i * 128:(i + 1) * 128], in_=T0[0:64, 0:256].rearrange("p (g t) -> p g t", g=2)),
                     reads=[T0_r], writes=[kT_r[i]])

              pu_, pv2_ = pieces[1], pieces[2]
              pieces[1:3] = [lambda: (pu_(), pv2_())]
              return pieces, late

            def stage_b(m, fillers, blk0=blk0, nb=nb):
                S.tag = ""
                blk = blk0 + m
                s3 = m % NS; s2 = m % 2
                kbs = [kb for kb in (m, m - 1, m + 1) if 0 <= kb < nb]
                tasks = [(g, n_, kb) for g in range(2) for n_, kb in enumerate(kbs)]

                def front(t):
                    g, n_, kb = t
                    p = pcnt[0] % 2; pcnt[0] += 1
                    S.op("pe", lambda q: q.matmul(ST[p][:], lhsT=kT[:, g, kb * 128:(kb + 1) * 128],
                                                  rhs=qT[s3][:, 4 * g:4 * g + 4, :], start=True, stop=True),
                         reads=[kT_r[kb], qT_r[s3]], writes=[ST_r[p]])
                    S.op("act", lambda q: q.activation(out=PT[p][:], in_=ST[p][:], func=AF.Exp, scale=0.125), reads=[ST_r[p]], writes=[PT_r[p]])
                    if kb != m:
                        mi = 0 if kb < m else 1
                        S.op("dve", lambda q: q.tensor_tensor(out=PT[p][:], in0=PT[p][:], in1=wmask[:, mi, :], op=ALU.mult),
                             reads=[PT_r[p], wmask_r], writes=[PT_r[p]])
                    return p

                def back(t, p):
                    g, n_, kb = t
                    S.op("pe", lambda q: q.matmul(PO[:], lhsT=vp[:, kb, g, :], rhs=PT[p][:], start=(n_ == 0), stop=(n_ == len(kbs) - 1)),
                         reads=[vp_r[kb], PT_r[p]], writes=[PO_r])
                    if n_ == len(kbs) - 1:
                        S.op("dve", lambda q: q.tensor_tensor(out=den[64:128, :].rearrange("p (h t) -> p h t", h=4), in0=PO[64:128, :].rearrange("p (h t) -> p h t", h=4),
                                                              in1=es[64:128, 4 * g:4 * g + 4, :], op=ALU.add),
                             reads=[PO_r, es_r], writes=[den_r])
                        S.op("act", lambda q: q.activation(out=rec[:], in_=den[64:128, :], func=AF.Ln), reads=[den_r], writes=[rec_r])
                        S.op("act", lambda q: q.activation(out=rec[:], in_=rec[:], func=AF.Exp, scale=-1.0), reads=[rec_r], writes=[rec_r])
                        S.op("dve", lambda q: q.tensor_tensor(out=boT[s2][:, 4 * g:4 * g + 4, :], in0=PO[0:64, :].rearrange("p (h t) -> p h t", h=4),
                                                              in1=rec[:].rearrange("p (h t) -> p h t", h=4), op=ALU.mult),
                             reads=[PO_r, rec_r], writes=[boT_r[s2]])

                ps = {}
                for k in range(len(tasks) + 1):
                    if fillers:
                        fillers.pop(0)()
                    if k < len(tasks):
                        ps[k] = front(tasks[k])
                    if k >= 1:
                        back(tasks[k - 1], ps[k - 1])
                while fillers:
                    fillers.pop(0)()
                S.tag = ""
                o = m % 2
                for dh, (PY, PY_r) in enumerate(((A0, A0_r), (A1, A1_r))):
                    pairs = [(aoT[s3][:, c, :], woA[:, c, dh * 512:(dh + 1) * 512]) for c in range(4)]
                    pairs += [(boT[s2][:, h, :], woB[:, h, dh * 512:(dh + 1) * 512]) for h in range(8)]
                    mm_group(S, PY[:], pairs, reads=[aoT_r[s3], boT_r[s2], woA_r, woB_r], writes=[PY_r])
                    S.op("dve", lambda q, dh=dh, PY=PY: q.tensor_tensor(out=ost[o][:, dh * 512:(dh + 1) * 512], in0=PY[:], in1=xin[s3][:, dh * 512:(dh + 1) * 512], op=ALU.add),
                         reads=[PY_r, xin_r[s3]], writes=[ost_r[o]])
                S.dma("sp", lambda q: q.dma_start(out=x_dst[blk * 128:(blk + 1) * 128, :], in_=ost[o][:]), reads=[ost_r[o]], writes=[dst_res[blk]], sem_res=ost_r[o])

            import os
            dbg = int(os.environ.get("DBG", "0"))
            late_prev = []
            for it in range(nb + 3):
                S.tag = ""
                if dbg == 1:
                    break
                early, late_new = stage_a(it) if it < nb else ([], [])
                pieces = []
                while early or late_prev:
                    if early:
                        pieces.append(early.pop(0))
                    if late_prev:
                        pieces.append(late_prev.pop(0))
                late_prev = late_new
                if it >= 3 and dbg != 2:
                    stage_b(it - 3, pieces)
                else:
                    while pieces:
                        pieces.pop(0)()
            blk0 += nb
        S.barrier()
        S.flush()


def mixer_c_phase(S, C, nc, W, x_src, src_res, x_dst, dst_res, seqs):
    with ExitStack() as ph:
        sb = lambda n, sh, dt: ph.enter_context(nc.sbuf_tensor(U(n), sh, dt))
        pst = lambda n, sh, dt: ph.enter_context(nc.psum_tensor(U(n), sh, dt))
        C.pbc = pst("pbc", [128, 1024], F32); C.pbc_r = Res("pbc")
        gt, gt_r = load_gain(S, C, ph, nc, W["l1_mix_norm"], "gt")
        P0 = C.pbc[:, 0:512]; P1 = C.pbc[:, 512:1024]
        P_r = [Res("P0"), Res("P1")]
        T0 = C.pbc.bitcast(BF16); T0_r = P_r[0]
        NST = 4
        ST = [pst(f"ST{i}", [128, 512], F32) for i in range(NST)]; ST_r = [Res(f"ST{i}") for i in range(NST)]
        PO = [pst(f"PO{i}", [128, 512], F32) for i in range(2)]; PO_r = [Res(f"PO{i}") for i in range(2)]

        wq = sb("wq", [128, 8, 3072], BF16); wq_r = Res("wq")
        wq_v = W["l1_w_qkv"].rearrange("(kc p) f -> p kc f", p=128)
        for h in range(3):
            S.dma("pool", lambda q, h=h: q.dma_start(out=wq[:, :, h * 1024:(h + 1) * 1024], in_=wq_v[:, :, h * 1024:(h + 1) * 1024]),
                  writes=[wq_r], sem_res=Res("wql"))
        woC = sb("woC", [128, 8, 1024], BF16); woC_r = Res("woC")
        S.dma("pool", lambda q: q.dma_start(out=woC[:], in_=W["l1_w_out"].rearrange("(c p) d -> p c d", p=128)), writes=[woC_r])
        E = sb("E", [128, 7, 16, 128], BF16); E_r = Res("E")
        if True:
            sb2 = sb
            rp = sb2("rp", [16, 15, 31], F32); rp_r = Res("rp")
            rr = sb2("rr", [16, 15, 31], F32); rr_r = Res("rr")
            zt = sb2("zt", [1, 64], F32); zt_r = Res("zt")
            cm = sb2("cm", [128, 128], F32); cm_r = Res("cm")
            est = sb2("est", [128, 16, 128], F32); est_r = Res("est")
            rsc_r = Res("rsc"); gsc_r = Res("gsc")
            S.dma("sp", lambda q: q.dma_start(out=rp[:], in_=W["l1_rpb"]), writes=[rp_r])
            S.dma("sp", lambda q: q.dma_start(out=cm[:], in_=C.cmask_in), writes=[cm_r])
            S.op("dve", lambda q: q.memset(zt[:], 0.0), writes=[zt_r])
            for t in range(31):
                S.op("dve" if t % 2 else "act",
                     (lambda t: (lambda q: q.tensor_copy(out=rr[:, :, 30 - t:31 - t], in_=rp[:, :, t:t + 1])) if t % 2 else
                      (lambda q: q.copy(out=rr[:, :, 30 - t:31 - t], in_=rp[:, :, t:t + 1])))(t),
                     reads=[rp_r], writes=[rr_r])
            rsc = C.rsc
            S.dma("sp", lambda q: q.dma_start(out=rsc[0:64].unsqueeze(0), in_=zt[:]), reads=[zt_r], writes=[rsc_r], sem_res=rsc_r)
            S.dma("sp", lambda q: q.dma_start(out=rsc[64 + 7440:128 + 7440].unsqueeze(0), in_=zt[:]), reads=[zt_r], writes=[rsc_r], sem_res=rsc_r)
            S.dma("sp", lambda q: q.dma_start(out=rsc[64:64 + 7440].rearrange("(h x) -> h x", h=16), in_=rr[:].rearrange("p a b -> p (a b)")),
                  reads=[rr_r], writes=[rsc_r], sem_res=rsc_r)
            gsc = C.gsc
            for kc in range(64):
                src = bass.AP(C.rsc_t, 64 + 15 - kc, [[31, 240], [1, 64]])
                S.dma("sp", lambda q, kc=kc, src=src: q.dma_start(out=gsc[:, kc, :], in_=src), reads=[rsc_r], writes=[gsc_r], sem_res=gsc_r, nowaw=(kc % 8 != 0))
            g4 = gsc.rearrange("(h r) k c -> h r k c", h=16)
            for di, dl in enumerate(range(-3, 4)):
                S.op("dve", lambda q: q.memset(est[:], 0.0), writes=[est_r])
                for kr in range(2):
                    for qr in range(2):
                        dr = 2 * dl + kr - qr + 7
                        if 0 <= dr <= 14:
                            S.dma("sp", lambda q, kr=kr, qr=qr, dr=dr: q.dma_start(
                                out=est[kr * 64:(kr + 1) * 64, :, qr * 64:(qr + 1) * 64], in_=g4[:, dr, :, :].rearrange("h k c -> k h c")),
                                reads=[gsc_r], writes=[est_r], nowaw=(kr + qr > 0))
                S.op("act", lambda q: q.activation(out=est[:], in_=est[:], func=AF.Exp), reads=[est_r], writes=[est_r])
                S.op("dve", lambda q, di=di: q.tensor_tensor(out=E[:, di, :, :], in0=est[:], in1=cm[:].unsqueeze(1).broadcast_to([128, 16, 128]), op=ALU.mult),
                     reads=[est_r, cm_r], writes=[E_r])

        NQ = 4
        NR = 7
        xin = [sb(f"xin{i}", [128, 1024], F32) for i in range(NQ)]; xin_r = [Res(f"xin{i}") for i in range(NQ)]
        hb = [sb("hb0", [128, 1024], BF16)] * 2; hb_r = [Res("hb0")] * 2
        hT = [sb("hT0", [128, 8, 128], BF16)] * 2; hT_r = [Res("hT0")] * 2
        qT = [sb(f"qT{i}", [128, 8, 128], BF16) for i in range(NQ)]; qT_r = [Res(f"qT{i}") for i in range(NQ)]
        kT = sb("kT", [128, 8, NR * 128], BF16); kT_r = [Res(f"kT{i}") for i in range(NR)]
        vp = sb("vp", [128, NR, 16, 128], BF16); vp_r = [Res(f"vp{i}") for i in range(NR)]
        PT = [sb(f"PT{i}", [128, 512], BF16) for i in range(NST)]; PT_r = [Res(f"PT{i}") for i in range(NST)]
        rec = sb("rec", [128, 512], F32); rec_r = Res("rec")
        ocT = [sb("ocT0", [128, 8, 128], BF16)] * 2; ocT_r = [Res("ocT0")] * 2
        ost = [sb(f"ost{i}", [128, 1024], F32) for i in range(2)]; ost_r = [Res(f"ost{i}") for i in range(2)]
        stt = sb("stt", [128, 4], F32); stt_r = [Res(f"stt{i}") for i in range(4)]
        S.op("pool", lambda q: q.memset(vp[:], 1.0), writes=vp_r)
        zb = sb("zb", [128, 128], BF16); zb_r = Res("zb")
        S.op("dve", lambda q: q.memset(zb[:], 0.0), writes=[zb_r])

        pc = [0]
        sc = [0]
        oc = [0]
        blk0 = 0
        for L in seqs:
            nb = L // 128
            R = L // 64

            def stage_a(i, blk0=blk0):
                blk = blk0 + i
                sq = i % NQ; s2 = i % 2; sr = i % NR
                h_ = hT[s2]
                pieces = []

                def p0():
                    S.dma("sp", lambda q: q.dma_start(out=xin[sq][:], in_=x_src[blk * 128:(blk + 1) * 128, :]), reads=[src_res[blk]], writes=[xin_r[sq]])
                    rmsnorm_block(S, C, xin[sq][:], xin_r[sq], gt, gt_r, hb[s2][:], hb_r[s2], stt, stt_r[s2], s2)
                    transpose_block(S, C, hb[s2], hb_r[s2], T0, T0_r, hT[s2][:], hT_r[s2])
                pieces.append(p0)

                def pq(bq):
                    p = pc[0] % 2; pc[0] += 1
                    Pb = P0 if p == 0 else P1
                    def fq(q):
                        ins = None
                        for c4 in range(4):
                            ch = bq * 4 + c4
                            for kc in range(8):
                                ins = q.matmul(Pb[:, c4 * 128:(c4 + 1) * 128], lhsT=wq[:, kc, ch * 128:(ch + 1) * 128], rhs=h_[:, kc, :], start=(kc == 0), stop=(kc == 7))
                        return ins
                    S.op("pe", fq, reads=[hT_r[s2], wq_r], writes=[P_r[p]])
                    src = Pb.rearrange("p (c t) -> p c t", c=4)
                    if bq < 2:
                        S.op("act", lambda q: q.copy(out=qT[sq][:, bq * 4:(bq + 1) * 4, :], in_=src), reads=[P_r[p]], writes=[qT_r[sq]])
                    else:
                        S.op("act", lambda q: q.copy(out=kT[:, (bq - 2) * 4:(bq - 1) * 4, sr * 128:(sr + 1) * 128], in_=src), reads=[P_r[p]], writes=[kT_r[sr]])
                for bq in range(4):
                    pieces.append(lambda bq=bq: pq(bq))

                def pv_(dh):
                    p = pc[0] % 2; pc[0] += 1
                    Pb = P0 if p == 0 else P1
                    mm_group(S, Pb, [(h_[:, kc, :], wq[:, kc, 2048 + dh * 512:2048 + (dh + 1) * 512]) for kc in range(8)], reads=[hT_r[s2], wq_r], writes=[P_r[p]])
                    pv3 = Pb.rearrange("p (h d) -> p h d", h=8)
                    S.op("act", lambda q: q.copy(out=vp[:, sr, dh * 8:(dh + 1) * 8:2, 0:64], in_=pv3[:, 0:8:2, :]), reads=[P_r[p]], writes=[vp_r[sr]])
                    S.op("act", lambda q: q.copy(out=vp[:, sr, dh * 8 + 1:(dh + 1) * 8:2, 64:128], in_=pv3[:, 1:8:2, :]), reads=[P_r[p]], writes=[vp_r[sr]])
                for dh in range(2):
                    pieces.append(lambda dh=dh: pv_(dh))
                return pieces

            def make_plan(m, R=R):
                rs = [min(max(2 * m + qr - 4, 0), R - 8) for qr in range(2)]
                js = sorted({(rs[qr] + a) // 2 for qr in range(2) for a in range(8)})
                js = [m] + [j for j in js if j != m]
                plan = []
                for j in js:
                    v = [[rs[qr] <= 2 * j + kr < rs[qr] + 8 for qr in range(2)] for kr in range(2)]
                    rects = []
                    if all(v[kr][qr] for kr in range(2) for qr in range(2)):
                        rects.append((0, 128, 0, 128))
                    else:
                        for qr in range(2):
                            ks = [kr for kr in range(2) if v[kr][qr]]
                            if len(ks) == 2:
                                rects.append((0, 128, qr * 64, 64))
                            elif len(ks) == 1:
                                rects.append((ks[0] * 64, 64, qr * 64, 64))
                    plan.append((j, rects))
                assert plan[0][1] == [(0, 128, 0, 128)]
                return plan

            def stage_b(m, fillers, blk0=blk0, nb=nb, R=R):
                blk = blk0 + m
                sq = m % NQ
                o2 = oc[0] % 2; oc[0] += 1
                plan = make_plan(m)
                nj = len(plan)
                tasks = [(hq, jn) for hq in range(4) for jn in range(nj)]

                def front(t):
                    hq, jn = t
                    j, rects = plan[jn]
                    sr = j % NR
                    di = j - m + 3
                    p = sc[0] % NST; sc[0] += 1
                    def fsc(q):
                        ins = None
                        for hh in range(4):
                            h = 2 * (4 * (hq // 2) + hh) + (hq % 2)
                            pb = (h % 2) * 64
                            ins = q.matmul(ST[p][:, hh * 128:(hh + 1) * 128], lhsT=kT[pb:pb + 64, h // 2, sr * 128:(sr + 1) * 128],
                                           rhs=qT[sq][pb:pb + 64, h // 2, :], start=True, stop=True)
                        return ins
                    S.op("pe", fsc, reads=[kT_r[sr], qT_r[sq]], writes=[ST_r[p]])
                    S.op("act", lambda q: q.activation(out=PT[p][:], in_=ST[p][:], func=AF.Exp, scale=0.125), reads=[ST_r[p]], writes=[PT_r[p]])
                    S.op("dve", lambda q: q.tensor_tensor(out=PT[p][:].rearrange("p (h t) -> p h t", h=4), in0=PT[p][:].rearrange("p (h t) -> p h t", h=4),
                                                          in1=E[:, di, 8 * (hq // 2) + (hq % 2):8 * (hq // 2) + 8:2, :], op=ALU.mult),
                         reads=[PT_r[p], E_r], writes=[PT_r[p]])
                    return p

                def back(t, p):
                    hq, jn = t
                    j, rects = plan[jn]
                    sr = j % NR
                    po = hq % 2
                    if jn == 0:
                        S.op("pe", lambda q: q.matmul(PO[po][:], lhsT=zb[:], rhs=E[:, 3, 0:4, :], start=True, stop=False),
                             reads=[zb_r, E_r], writes=[PO_r[po]])
                    def fpv(q):
                        ins = None
                        for hh in range(4):
                            h = 2 * (4 * (hq // 2) + hh) + (hq % 2)
                            for ri, (k0, kn, c0, cn) in enumerate(rects):
                                ins = q.matmul(PO[po][:, hh * 128 + c0:hh * 128 + c0 + cn], lhsT=vp[k0:k0 + kn, sr, h, :],
                                               rhs=PT[p][k0:k0 + kn, hh * 128 + c0:hh * 128 + c0 + cn],
                                               start=False, stop=(jn == nj - 1 and ri == len(rects) - 1))
                        return ins
                    S.op("pe", fpv, reads=[vp_r[sr], PT_r[p]], writes=[PO_r[po]])
                    if jn == nj - 1:
                        if hq % 2 == 0:
                            o_lo, d_lo = 0, 64
                        else:
                            o_lo, d_lo = 64, 0
                        S.op("act", lambda q: q.activation(out=rec[o_lo:o_lo + 64, :], in_=PO[po][d_lo:d_lo + 64, :], func=AF.Ln), reads=[PO_r[po]], writes=[rec_r])
                        S.op("act", lambda q: q.activation(out=rec[o_lo:o_lo + 64, :], in_=rec[o_lo:o_lo + 64, :], func=AF.Exp, scale=-1.0), reads=[rec_r], writes=[rec_r])
                        S.op("dve", lambda q: q.tensor_tensor(out=ocT[o2][o_lo:o_lo + 64, 4 * (hq // 2):4 * (hq // 2) + 4, :],
                                                              in0=PO[po][o_lo:o_lo + 64, :].rearrange("p (h t) -> p h t", h=4),
                                                              in1=rec[o_lo:o_lo + 64, :].rearrange("p (h t) -> p h t", h=4), op=ALU.mult),
                             reads=[PO_r[po], rec_r], writes=[ocT_r[o2]])

                DEP = NST - 1
                ps = {}
                for k in range(len(tasks) + DEP):
                    if fillers and k % 3 == 1:
                        fillers.pop(0)()
                    if k < len(tasks):
                        ps[k] = front(tasks[k])
                    if k >= DEP:
                        back(tasks[k - DEP], ps[k - DEP])
                while fillers:
                    fillers.pop(0)()
                S.tag = ""
                for dh in range(2):
                    p = pc[0] % 2; pc[0] += 1
                    Pb = P0 if p == 0 else P1
                    mm_group(S, Pb, [(ocT[o2][:, h, :], woC[:, h, dh * 512:(dh + 1) * 512]) for h in range(8)], reads=[ocT_r[o2], woC_r], writes=[P_r[p]])
                    S.op("dve", lambda q, dh=dh, Pb=Pb: q.tensor_tensor(out=ost[o2][:, dh * 512:(dh + 1) * 512], in0=Pb, in1=xin[sq][:, dh * 512:(dh + 1) * 512], op=ALU.add),
                         reads=[P_r[p], xin_r[sq]], writes=[ost_r[o2]])
                S.dma("sp", lambda q: q.dma_start(out=x_dst[blk * 128:(blk + 1) * 128, :], in_=ost[o2][:]), reads=[ost_r[o2]], writes=[dst_res[blk]], sem_res=ost_r[o2])

            import os
            dbg = int(os.environ.get("DBG", "0"))
            for it in range(nb + 3):
                if dbg == 1:
                    break
                pieces = stage_a(it) if it < nb else []
                if it >= 3 and dbg != 2:
                    m = it - 3
                    if max(j for j, _ in make_plan(m)) >= it:
                        while pieces:
                            pieces.pop(0)()
                    stage_b(m, pieces)
                else:
                    while pieces:
                        pieces.pop(0)()
            blk0 += nb
        S.barrier()
        S.flush()


def build_nc(seqs=SEQS, nphase=6, only=None):
    nc = bass.Bass("TRN2", target_bir_lowering=False)
    print("sbuf bytes remaining at start", nc.sbuf_bytes_remaining)
    ntok = sum(seqs)
    nblk = ntok // 128
    x_in = nc.dram_tensor("x", [ntok, D], F32, kind="ExternalInput").ap()
    W = {n: nc.dram_tensor(n, list(s), F32, kind="ExternalInput").ap() for n, s in WSPECS}
    C = Ctx()
    ident_in = nc.dram_tensor("c_ident", [128, 128], F32, kind="ExternalInput").ap()
    C.rope_in = nc.dram_tensor("c_rope", [4096, 32], F32, kind="ExternalInput").ap()
    C.wmask_in = nc.dram_tensor("c_wmask", [128, 2, 512], F32, kind="ExternalInput").ap()
    C.cmask_in = nc.dram_tensor("c_cmask", [128, 128], F32, kind="ExternalInput").ap()
    y_out = nc.dram_tensor("y", [ntok, D], F32, kind="ExternalOutput").ap()
    xa = nc.dram_tensor("xa", [ntok, D], F32).ap()
    xb = nc.dram_tensor("xb", [ntok, D], F32).ap()
    C.rsc_t = nc.dram_tensor("rsc", [7440 + 128], F32)
    C.rsc = C.rsc_t.ap()
    C.gsc = nc.dram_tensor("gsc", [240, 64, 64], F32).ap()
    in_r = [Res(f"in{b}") for b in range(nblk)]
    xa_r = [Res(f"xa{b}") for b in range(nblk)]
    xb_r = [Res(f"xb{b}") for b in range(nblk)]
    y_r = [Res(f"y{b}") for b in range(nblk)]
    with ExitStack() as st:
        S = Sched(nc, st)
        sb = lambda n, sh, dt: st.enter_context(nc.sbuf_tensor(U(n), sh, dt))
        C.ident = sb("ident", [128, 128], BF16)
        C.identf = sb("identf", [128, 128], F32)
        C.ones_row = sb("ones_row", [1, 128], F32)
        C.ones_col = sb("ones_col", [128, 1], F32)
        C.eps = sb("eps", [128, 1], F32)
        C.vrow = sb("vrow", [1, 1024], F32); C.vrow_r = Res("vrow")
        C.junk = sb("junk", [128, 1024], BF16)
        C.const_r = Res("const")
        cl = Res("constload")
        S.dma("pool", lambda q: q.dma_start(out=C.ident[:], in_=ident_in), writes=[C.const_r], sem_res=cl)
        S.dma("sp", lambda q: q.dma_start(out=C.identf[:], in_=ident_in), writes=[C.const_r], sem_res=cl)
        S.op("dve", lambda q: q.memset(C.ones_row[:], 1.0), writes=[C.const_r])
        S.op("dve", lambda q: q.memset(C.ones_col[:], 1.0), writes=[C.const_r])
        S.op("dve", lambda q: q.memset(C.eps[:], EPS), writes=[C.const_r])
        S.barrier()

        def ffn(src, src_r, dst, dst_r, pre, final=False):
            ffn_phase(S, C, nc, src, src_r, dst, dst_r, W[pre + "_norm"], W[pre + "_w_gate"], W[pre + "_w_up"], W[pre + "_w_down"], ntok,
                      final_g=(W["final_norm"] if final else None), is_output=(dst is y_out))

        phases = [
            lambda s, sr, d, dr: ffn(s, sr, d, dr, "l0_ffn1"),
            lambda s, sr, d, dr: mixer_ab_phase(S, C, nc, W, s, sr, d, dr, seqs),
            lambda s, sr, d, dr: ffn(s, sr, d, dr, "l0_ffn2"),
            lambda s, sr, d, dr: ffn(s, sr, d, dr, "l1_ffn1"),
            lambda s, sr, d, dr: mixer_c_phase(S, C, nc, W, s, sr, d, dr, seqs),
            lambda s, sr, d, dr: ffn(s, sr, d, dr, "l1_ffn2", final=(nphase == 6)),
        ]
        cur, cur_r = x_in, in_r
        scr = [(xa, xa_r), (xb, xb_r)]
        for pi in range(nphase):
            if only is not None and pi != only:
                continue
            if pi == nphase - 1:
                dst, dst_r = y_out, y_r
            else:
                dst, dst_r = scr[pi % 2]
            phases[pi](cur, cur_r, dst, dst_r)
            cur, cur_r = dst, dst_r
        S.flush(final=True)
    return nc


def host_consts():
    ident = np.eye(128, dtype=np.float32)
    pos = np.arange(4096, dtype=np.float64)[:, None]
    inv = np.power(500000.0, -np.arange(0, 16, 2, dtype=np.float64) / 16.0)[None, :]
    ang = (pos.astype(np.float32) * inv.astype(np.float32)).astype(np.float32)
    cos = np.cos(ang).astype(np.float32); sin = np.sin(ang).astype(np.float32)
    rope = np.concatenate([cos, cos, -sin, sin], axis=1).astype(np.float32)
    j = np.arange(128)[:, None]; i = np.arange(128)[None, :]
    m0 = (j >= i).astype(np.float32); m1 = (j <= i).astype(np.float32)
    wmask = np.stack([np.tile(m0, (1, 4)), np.tile(m1, (1, 4))], axis=1).astype(np.float32)
    qc = np.arange(64)
    cs = np.clip(qc - 8, 0, 48)
    kc = np.arange(64)[:, None]
    cm64 = ((kc >= cs[None, :]) & (kc < cs[None, :] + 16)).astype(np.float32)
    cmask = np.tile(cm64, (2, 2)).astype(np.float32)
    return {"c_ident": ident, "c_rope": rope, "c_wmask": wmask, "c_cmask": cmask}


_NC_CACHE = {}


def kernel(**inputs):
    n = 8
    xp = np.asarray(inputs["x_prompt"], dtype=np.float32)
    xs = np.asarray(inputs["x_sample"], dtype=np.float32)
    if "nc" not in _NC_CACHE:
        _NC_CACHE["nc"] = build_nc(SEQS, 6)
    nc = _NC_CACHE["nc"]
    consts = host_consts()
    wmaps = {nm: np.ascontiguousarray(np.asarray(inputs[nm], dtype=np.float32)) for nm, _ in WSPECS}
    in_maps = []
    for c in range(n):
        xc = np.concatenate([xp[c], xs[2 * c], xs[2 * c + 1]], axis=0)
        m = {"x": np.ascontiguousarray(xc)}
        m.update(wmaps)
        m.update(consts)
        in_maps.append(m)
    res = run_bass_kernel_spmd(nc, in_maps, core_ids=list(range(n)))
    yp = np.empty_like(xp)
    ys = np.empty_like(xs)
    for c in range(n):
        y = np.asarray(res.results[c]["y"], dtype=np.float32)
        yp[c] = y[0:4096]
        ys[2 * c] = y[4096:6144]
        ys[2 * c + 1] = y[6144:8192]
    return (yp, ys)
```

```python
import numpy as np
from contextlib import ExitStack
import concourse.bass as bass
import concourse.mybir as mybir
from concourse.bass_utils import run_bass_kernel_spmd

F32 = mybir.dt.float32
BF16 = mybir.dt.bfloat16
AF = mybir.ActivationFunctionType
ALU = mybir.AluOpType

D = 1024
DFF = 2816
NFC = DFF // 128
EPS = 1e-6
SEQS = (4096, 2048, 2048)

WSPECS = [
    ("l0_ffn1_norm", (1024,)), ("l0_ffn1_w_gate", (1024, 2816)), ("l0_ffn1_w_up", (1024, 2816)),
    ("l0_ffn1_w_down", (2816, 1024)), ("l0_mix_norm", (1024,)), ("l0_w_in", (1024, 1792)),
    ("l0_sgu_ln_g", (512,)), ("l0_sgu_ln_b", (512,)), ("l0_sgu_w", (4, 128, 128)), ("l0_sgu_b", (4, 128)),
    ("l0_sink", (8,)), ("l0_w_out", (1024, 1024)), ("l0_ffn2_norm", (1024,)),
    ("l0_ffn2_w_gate", (1024, 2816)), ("l0_ffn2_w_up", (1024, 2816)), ("l0_ffn2_w_down", (2816, 1024)),
    ("l1_ffn1_norm", (1024,)), ("l1_ffn1_w_gate", (1024, 2816)), ("l1_ffn1_w_up", (1024, 2816)),
    ("l1_ffn1_w_down", (2816, 1024)), ("l1_mix_norm", (1024,)), ("l1_w_qkv", (1024, 3072)),
    ("l1_rpb", (16, 15, 31)), ("l1_w_out", (1024, 1024)), ("l1_ffn2_norm", (1024,)),
    ("l1_ffn2_w_gate", (1024, 2816)), ("l1_ffn2_w_up", (1024, 2816)), ("l1_ffn2_w_down", (2816, 1024)),
    ("final_norm", (1024,)),
]


class Res:
    __slots__ = ("name", "w", "r", "dsem", "dcnt")

    def __init__(self, name):
        self.name = name
        self.w = None
        self.r = {}
        self.dsem = None
        self.dcnt = 0


class Sched:
    ENGS = ("pe", "act", "dve", "pool", "sp")

    def __init__(self, nc, stack):
        self.nc = nc
        self.stack = stack
        self.q = {e: [] for e in self.ENGS}
        self.cnt = {e: 0 for e in self.ENGS}
        self.sem = {}
        self.seen = {e: {} for e in self.ENGS}
        self.nsem = 0
        for e in ("pe", "act", "dve", "pool"):
            self.sem[e] = self.new_sem("c_" + e)
        self.out_tokens = []
        self.free_d = []
        self.live_d = []
        self.dma_hi = {}

    def new_sem(self, name):
        self.nsem += 1
        return self.stack.enter_context(self.nc.semaphore(f"{name}_{self.nsem}"))

    def _waits(self, eng, reads, writes, nowaw=False):
        waits = {}

        def need(tok):
            if tok is None:
                return
            sem, val = tok
            if eng == "pe" and sem is self.sem["pe"]:
                return
            k = id(sem)
            if self.seen[eng].get(k, 0) >= val:
                return
            if k not in waits or waits[k][1] < val:
                waits[k] = (sem, val)

        for r in reads:
            need(r.w)
        for w in writes:
            if not nowaw:
                need(w.w)
            for tok in w.r.values():
                need(tok)
        for k, (sem, val) in waits.items():
            self.seen[eng][k] = val
        return list(waits.values())

    def _commit(self, tok, reads, writes):
        k = id(tok[0])
        for r in reads:
            old = r.r.get(k)
            if old is None or old[1] < tok[1]:
                r.r[k] = tok
        for w in writes:
            w.w = tok
            w.r = {}

    tag = ""

    def op(self, eng, fn, reads=(), writes=()):
        waits = self._waits(eng, reads, writes)
        self.cnt[eng] += 1
        tok = (self.sem[eng], self.cnt[eng])
        self._commit(tok, reads, writes)
        self.q[eng].append((waits, fn, tok, 1))
        return tok

    def dma(self, eng, fn, reads=(), writes=(), sem_res=None, is_output=False, nowaw=False):
        waits = self._waits(eng, reads, writes, nowaw)
        if sem_res is None:
            sem_res = writes[0]
        if sem_res.dsem is None:
            if self.free_d:
                sem_res.dsem, sem_res.dcnt = self.free_d.pop()
            else:
                sem_res.dsem, sem_res.dcnt = self.new_sem("d"), 0
            self.live_d.append(sem_res)
        sem_res.dcnt += 16
        tok = (sem_res.dsem, sem_res.dcnt)
        self.dma_hi[id(tok[0])] = tok
        self._commit(tok, reads, writes)
        self.q[eng].append((waits, fn, tok, 16))
        if is_output:
            self.out_tokens.append(tok)
        return tok

    def barrier(self):
        toks = [(self.sem[e], self.cnt[e]) for e in ("pe", "act", "dve", "pool") if self.cnt[e] > 0]
        toks += list(self.dma_hi.values())
        for e in self.ENGS:
            ws = []
            for sem, val in toks:
                if self.seen[e].get(id(sem), 0) < val:
                    ws.append((sem, val))
                    self.seen[e][id(sem)] = val
            if ws:
                self.q[e].append((ws, None, None, 0))
        for r in self.live_d:
            self.free_d.append((r.dsem, r.dcnt))
            r.dsem = None
        self.live_d = []

    def flush(self, final=False):
        nc = self.nc
        q = self.q
        fin = []
        if final:
            d = {}
            for sem, val in self.out_tokens:
                if id(sem) not in d or d[id(sem)][1] < val:
                    d[id(sem)] = (sem, val)
            fin = list(d.values())
        csem = {id(self.sem[e]): e for e in ("pe", "act", "dve", "pool")}
        needed = {e: set() for e in csem.values()}
        for e in self.ENGS:
            for waits, fn, tok, inc in q[e]:
                for sem, val in waits:
                    if id(sem) in csem:
                        needed[csem[id(sem)]].add(val)
        if not hasattr(self, "base"):
            self.base = {e: 0 for e in csem.values()}
        remap = {}
        for e, vals in needed.items():
            for rank, v in enumerate(sorted(vals)):
                remap[(e, v)] = self.base[e] + rank + 1
            self.base[e] += len(vals)

        def tr(sem, val):
            if id(sem) in csem:
                return remap[(csem[id(sem)], val)]
            return val

        def run(e, items, extra=()):
            for waits, fn, tok, inc in items:
                for sem, val in waits:
                    e.wait_ge(sem, tr(sem, val))
                if fn is not None:
                    ins = fn(e)
                    if inc == 16:
                        ins.then_inc(tok[0], 16)
                    elif (csem[id(tok[0])], tok[1]) in remap:
                        ins.then_inc(tok[0], 1)
            for sem, val in extra:
                e.wait_ge(sem, val)

        with nc.Block() as block:
            @block.sync
            def _(e):
                run(e, q["sp"], fin)

            @block.gpsimd
            def _(e):
                run(e, q["pool"])

            @block.tensor
            def _(e):
                run(e, q["pe"])

            @block.scalar
            def _(e):
                run(e, q["act"])

            @block.vector
            def _(e):
                run(e, q["dve"])
        self.q = {e: [] for e in self.ENGS}
        self.n_inc = getattr(self, "n_inc", 0) + sum(len(v) for v in needed.values())


def mm_group(S, out_ap, pairs, reads, writes):
    def f(q, pairs=pairs, out_ap=out_ap):
        n = len(pairs)
        ins = None
        for i, (l, r) in enumerate(pairs):
            ins = q.matmul(out_ap, lhsT=l, rhs=r, start=(i == 0), stop=(i == n - 1))
        return ins
    S.op("pe", f, reads, writes)


class Ctx:
    pass


_UC = [0]


def U(n):
    _UC[0] += 1
    return f"{n}_{_UC[0]}"


def rmsnorm_block(S, C, x_ap, x_res, gt, gt_res, hb, hb_res, st, st_res, col):
    S.op("act", lambda q: q.activation(out=C.junk[:], in_=x_ap, func=AF.Square, scale=1.0 / 32.0,
                                       accum_out=st[:, col:col + 1]), reads=[x_res], writes=[st_res])
    S.op("act", lambda q: q.activation(out=st[:, col:col + 1], in_=st[:, col:col + 1], func=AF.Ln,
                                       bias=C.eps[:, 0:1], scale=1.0), reads=[st_res, C.const_r], writes=[st_res])
    S.op("act", lambda q: q.activation(out=st[:, col:col + 1], in_=st[:, col:col + 1], func=AF.Exp, scale=-0.5), reads=[st_res], writes=[st_res])
    S.op("dve", lambda q: q.scalar_tensor_tensor(out=hb, in0=x_ap, scalar=st[:, col:col + 1], in1=gt[:],
                                                 op0=ALU.mult, op1=ALU.mult),
         reads=[x_res, st_res, gt_res], writes=[hb_res])


def transpose_block(S, C, hb, hb_res, ptr, ptr_res, dst3, dst_res, eng="act"):
    def f(q):
        ins = None
        for kc in range(8):
            ins = q.transpose(out=ptr[:, kc * 128:(kc + 1) * 128], in_=hb[:, kc * 128:(kc + 1) * 128], identity=C.ident[:])
        return ins
    S.op("pe", f, reads=[hb_res, C.const_r], writes=[ptr_res])
    src = ptr[:, 0:1024].rearrange("p (k t) -> p k t", k=8)
    if eng == "act":
        S.op("act", lambda q: q.copy(out=dst3, in_=src), reads=[ptr_res], writes=[dst_res])
    else:
        S.op("dve", lambda q: q.tensor_copy(out=dst3, in_=src), reads=[ptr_res], writes=[dst_res])


def load_gain(S, C, ph, nc, g_ap, name):
    gt = ph.enter_context(nc.sbuf_tensor(U(name), [128, 1024], F32))
    gt_r = Res(name)
    S.dma("sp", lambda q: q.dma_start(out=C.vrow[:], in_=g_ap.unsqueeze(0)), writes=[C.vrow_r])
    def f(q):
        q.matmul(C.pbc[:, 0:512], lhsT=C.ones_row[:], rhs=C.vrow[:, 0:512], start=True, stop=True)
        return q.matmul(C.pbc[:, 512:1024], lhsT=C.ones_row[:], rhs=C.vrow[:, 512:1024], start=True, stop=True)
    S.op("pe", f, reads=[C.vrow_r, C.const_r], writes=[C.pbc_r])
    S.op("act", lambda q: q.copy(out=gt[:], in_=C.pbc[:]), reads=[C.pbc_r], writes=[gt_r])
    return gt, gt_r


def ffn_phase(S, C, nc, x_src, src_res, x_dst, dst_res, g_ap, wg, wu, wd, ntok, final_g=None, is_output=False):
    TM = 1024
    NB = TM // 128
    NX = 4
    NO = 3
    groups = [(0, 4), (4, 4), (8, 4), (12, 4), (16, 4), (20, 2)]
    with ExitStack() as ph:
        sb = lambda n, sh, dt: ph.enter_context(nc.sbuf_tensor(U(n), sh, dt))
        pst = lambda n, sh, dt: ph.enter_context(nc.psum_tensor(U(n), sh, dt))
        C.pbc = pst("pbc", [128, 1024], F32); C.pbc_r = Res("pbc")
        gt, gt_r = load_gain(S, C, ph, nc, g_ap, "gt")
        if final_g is not None:
            fgt, fgt_r = load_gain(S, C, ph, nc, final_g, "fgt")
        xt = sb("xt", [128, NX, 1024], F32); xt_r = [Res(f"xt{b}") for b in range(NX)]
        hb = [sb(f"hb{i}", [128, 1024], BF16) for i in range(2)]; hb_r = [Res(f"hb{i}") for i in range(2)]
        hT = [sb(f"hT{i}", [128, 8, TM], BF16) for i in range(2)]
        hT_r = [[Res(f"hT{i}_{b}") for b in range(NB)] for i in range(2)]
        aT = sb("aT", [128, NFC, TM], BF16); aT_r = [[Res(f"aT{j}_{s}") for s in range(2)] for j in range(NFC)]
        wdt = sb("wdt", [128, NFC, 1024], BF16); wd_r = [Res(f"wd{g}") for g in range(len(groups))]
        wgt = [sb(f"wgt{i}", [128, 8, 512], BF16) for i in range(2)]; wg_r = [Res(f"wg{i}") for i in range(2)]
        wut = [sb(f"wut{i}", [128, 8, 512], BF16) for i in range(2)]; wu_r = [Res(f"wu{i}") for i in range(2)]
        sg = [sb(f"sg{i}", [128, 512], F32) for i in range(2)]; sg_r = [Res(f"sg{i}") for i in range(2)]
        ost = [sb(f"ost{i}", [128, 1024], F32) for i in range(NO)]; ost_r = [Res(f"ost{i}") for i in range(NO)]
        stt = sb("stt", [128, 16], F32); stt_r = [Res(f"stt{b}") for b in range(16)]
        pg = [pst(f"pg{i}", [128, 512], F32) for i in range(2)]; pg_r = [Res(f"pg{i}") for i in range(2)]
        pu = [pst(f"pu{i}", [128, 512], F32) for i in range(2)]; pu_r = [Res(f"pu{i}") for i in range(2)]
        ptr = C.pbc.bitcast(BF16); ptr_r = C.pbc_r
        py = [pst(f"py{i}", [128, 512], F32) for i in range(2)]; py_r = [Res(f"py{i}") for i in range(2)]

        wg_v = wg.rearrange("(kc p) f -> p kc f", p=128)
        wu_v = wu.rearrange("(kc p) f -> p kc f", p=128)
        wd_v = wd.rearrange("(fc p) d -> p fc d", p=128)

        def issue_gu(gi):
            f0, nf = groups[gi]
            sl = gi % 2
            S.dma("pool", lambda q: q.dma_start(out=wgt[sl][:, :, 0:nf * 128], in_=wg_v[:, :, f0 * 128:(f0 + nf) * 128]), writes=[wg_r[sl]])
            S.dma("pool", lambda q: q.dma_start(out=wut[sl][:, :, 0:nf * 128], in_=wu_v[:, :, f0 * 128:(f0 + nf) * 128]), writes=[wu_r[sl]])

        def issue_wd(gi):
            f0, nf = groups[gi]
            S.dma("pool", lambda q: q.dma_start(out=wdt[:, f0:f0 + nf, :], in_=wd_v[:, f0:f0 + nf, :]), writes=[wd_r[gi]])

        nmt = ntok // TM
        pcnt = 0
        ycnt = 0
        xcnt = [0]

        def stage1_front(mt, b):
            blk = (mt * TM // 128) + b
            xs = xcnt[0] % NX; xcnt[0] += 1
            sl = b % 2
            S.dma("sp", lambda q: q.dma_start(out=xt[:, xs, :], in_=x_src[blk * 128:(blk + 1) * 128, :]), reads=[src_res[blk]], writes=[xt_r[xs]])
            rmsnorm_block(S, C, xt[:, xs, :], xt_r[xs], gt, gt_r, hb[sl][:], hb_r[sl], stt, stt_r[b], b)

        def stage1_back(mt, b):
            sl = b % 2
            hs = mt % 2
            transpose_block(S, C, hb[sl], hb_r[sl], ptr, ptr_r, hT[hs][:, :, b * 128:(b + 1) * 128], hT_r[hs][b])

        def load_res(mt, tb):
            blk = (mt * TM // 128) + tb
            o = blk % NO
            S.dma("sp", lambda q: q.dma_start(out=ost[o][:], in_=x_src[blk * 128:(blk + 1) * 128, :]), reads=[src_res[blk]], writes=[ost_r[o]])

        issue_gu(0); issue_gu(1)
        for b in range(NB):
            stage1_front(0, b)
            stage1_back(0, b)
        for mt in range(nmt):
            t0 = mt * TM
            hs = mt % 2
            if mt > 0:
                pass
            for gi, (f0, nf) in enumerate(groups):
                sl = gi % 2
                for jl in range(nf):
                    j = f0 + jl
                    for s in range(2):
                        p = pcnt % 2; pcnt += 1
                        hrd = hT_r[hs][s * 4:(s + 1) * 4]
                        mm_group(S, pg[p][:], [(wgt[sl][:, kc, jl * 128:(jl + 1) * 128], hT[hs][:, kc, s * 512:(s + 1) * 512]) for kc in range(8)],
                                 reads=hrd + [wg_r[sl]], writes=[pg_r[p]])
                        mm_group(S, pu[p][:], [(wut[sl][:, kc, jl * 128:(jl + 1) * 128], hT[hs][:, kc, s * 512:(s + 1) * 512]) for kc in range(8)],
                                 reads=hrd + [wu_r[sl]], writes=[pu_r[p]])
                        S.op("act", lambda q, p=p: q.activation(out=sg[p][:], in_=pg[p][:], func=AF.Silu), reads=[pg_r[p]], writes=[sg_r[p]])
                        S.op("dve", lambda q, p=p, j=j, s=s: q.tensor_tensor(out=aT[:, j, s * 512:(s + 1) * 512], in0=sg[p][:], in1=pu[p][:], op=ALU.mult),
                             reads=[sg_r[p], pu_r[p]], writes=[aT_r[j][s]])
                issue_wd(gi)
                if gi + 2 < len(groups):
                    issue_gu(gi + 2)
            if mt + 1 < nmt:
                issue_gu(0); issue_gu(1)
            load_res(mt, 0)
            for tb in range(NB):
                blk = (t0 // 128) + tb
                o = blk % NO
                if tb + 1 < NB:
                    load_res(mt, tb + 1)
                if mt + 1 < nmt:
                    stage1_front(mt + 1, tb)
                    if tb >= 1:
                        stage1_back(mt + 1, tb - 1)
                for dh in range(2):
                    y = ycnt % 2; ycnt += 1
                    mm_group(S, py[y][:], [(aT[:, j, tb * 128:(tb + 1) * 128], wdt[:, j, dh * 512:(dh + 1) * 512]) for j in range(NFC)],
                             reads=[aT_r[j][tb // 4] for j in range(NFC)] + wd_r, writes=[py_r[y]])
                    S.op("dve", lambda q, y=y, o=o, dh=dh: q.scalar_tensor_tensor(
                        out=ost[o][:, dh * 512:(dh + 1) * 512], in0=py[y][:], scalar=0.5, in1=ost[o][:, dh * 512:(dh + 1) * 512],
                        op0=ALU.mult, op1=ALU.add), reads=[py_r[y], ost_r[o]], writes=[ost_r[o]])
                if final_g is not None:
                    rmsnorm_block(S, C, ost[o][:], ost_r[o], fgt, fgt_r, ost[o][:], ost_r[o], stt, stt_r[8 + tb], 8 + tb)
                S.dma("sp", lambda q, o=o, blk=blk: q.dma_start(out=x_dst[blk * 128:(blk + 1) * 128, :], in_=ost[o][:]),
                      reads=[ost_r[o]], writes=[dst_res[blk]], sem_res=ost_r[o], is_output=is_output)
            if mt + 1 < nmt:
                stage1_back(mt + 1, NB - 1)
        S.barrier()
        S.flush()


def mixer_ab_phase(S, C, nc, W, x_src, src_res, x_dst, dst_res, seqs):
    with ExitStack() as ph:
        sb = lambda n, sh, dt: ph.enter_context(nc.sbuf_tensor(U(n), sh, dt))
        pst = lambda n, sh, dt: ph.enter_context(nc.psum_tensor(U(n), sh, dt))
        C.pbc = pst("pbc", [128, 1024], F32); C.pbc_r = Res("pbc")
        gt, gt_r = load_gain(S, C, ph, nc, W["l0_mix_norm"], "gt")
        A0 = pst("A0", [128, 512], F32); A0_r = Res("A0")
        A1 = pst("A1", [128, 512], F32); A1_r = Res("A1")
        T0 = A1.bitcast(BF16); T0_r = A1_r
        NST = 3
        ST = [pst(f"ST{i}", [128, 512], F32) for i in range(NST)]; ST_r = [Res(f"ST{i}") for i in range(NST)]
        PO = pst("PO", [128, 512], F32); PO_r = Res("PO")
        PQ = C.pbc; PQ_r = C.pbc_r

        win = sb("win", [128, 8, 1792], BF16); win_r = Res("win")
        win_v = W["l0_w_in"].rearrange("(kc p) f -> p kc f", p=128)
        for h in range(2):
            S.dma("pool", lambda q, h=h: q.dma_start(out=win[:, :, h * 896:(h + 1) * 896], in_=win_v[:, :, h * 896:(h + 1) * 896]),
                  writes=[win_r], sem_res=Res("winl"))
        woA = sb("woA", [128, 4, 1024], BF16); woA_r = Res("woA")
        woB = sb("woB", [64, 8, 1024], BF16); woB_r = Res("woB")
        S.dma("pool", lambda q: q.dma_start(out=woA[:], in_=W["l0_w_out"][0:512, :].rearrange("(c p) d -> p c d", p=128)), writes=[woA_r])
        S.dma("pool", lambda q: q.dma_start(out=woB[:], in_=W["l0_w_out"][512:1024, :].rearrange("(h p) d -> p h d", p=64)), writes=[woB_r])
        S.tag = "wmask"
        wmask = sb("wmask", [128, 2, 512], BF16); wmask_r = Res("wmask")
        S.dma("pool", lambda q: q.dma_start(out=wmask[:], in_=C.wmask_in), writes=[wmask_r])
        S.tag = "rope"
        rope = sb("rope", [128, 32, 32], F32); rope_r = Res("rope")
        S.dma("sp", lambda q: q.dma_start(out=rope[:], in_=C.rope_in.rearrange("(b p) c -> p b c", p=128)), writes=[rope_r])
        S.tag = "tr"
        sw = sb("sw", [128, 4, 128], F32); sw_r = Res("sw")
        S.dma("sp", lambda q: q.dma_start(out=sw[:], in_=W["l0_sgu_w"].rearrange("g i j -> i g j")), writes=[sw_r])
        WT = sb("WT", [128, 4, 128], BF16); WT_r = Res("WT")
        WTf = sb("WTf", [128, 4, 128], F32); WTf_r = Res("WTf")
        def ftr(q):
            ins = None
            for g in range(4):
                ins = q.transpose(out=A0[:, g * 128:(g + 1) * 128], in_=sw[:, g, :], identity=C.identf[:])
            return ins
        S.op("pe", ftr, reads=[sw_r, C.const_r], writes=[A0_r])
        S.op("act", lambda q: q.copy(out=WTf[:], in_=A0[:].rearrange("p (g i) -> p g i", g=4)), reads=[A0_r], writes=[WTf_r])
        S.op("dve", lambda q: q.tensor_copy(out=WT[:], in_=WTf[:]), reads=[WTf_r], writes=[WT_r])
        S.tag = "k2"
        k2l = sb("k2l", [2, 4, 128], F32); k2l_r = Res("k2l")
        k2r = sb("k2r", [2, 4, 128], F32); k2r_r = Res("k2r")
        S.op("dve", lambda q: q.memset(k2l[:], 1.0), writes=[k2l_r])
        S.dma("sp", lambda q: q.dma_start(out=k2l[0:1, :, :], in_=W["l0_sgu_ln_b"].rearrange("(o g c) -> o g c", o=1, g=4)), writes=[k2l_r])
        S.dma("sp", lambda q: q.dma_start(out=k2r[1:2, :, :], in_=W["l0_sgu_b"].unsqueeze(0)), writes=[k2r_r])
        S.op("pe", lambda q: q.matmul(A1[0:1, :], lhsT=C.ones_col[:, 0:1], rhs=WTf[:].rearrange("p g i -> p (g i)"), start=True, stop=True),
             reads=[WTf_r, C.const_r], writes=[A1_r])
        S.op("act", lambda q: q.copy(out=k2r[0:1, :, :], in_=A1[0:1, :].rearrange("p (g i) -> p g i", g=4)), reads=[A1_r], writes=[k2r_r])
        B2 = sb("B2", [128, 4, 128], F32); B2_r = Res("B2")
        def fb2(q):
            ins = None
            for g in range(4):
                ins = q.matmul(A0[:, g * 128:(g + 1) * 128], lhsT=k2l[:, g, :], rhs=k2r[:, g, :], start=True, stop=True)
            return ins
        S.op("pe", fb2, reads=[k2l_r, k2r_r], writes=[A0_r])
        S.op("act", lambda q: q.copy(out=B2[:], in_=A0[:].rearrange("p (g i) -> p g i", g=4)), reads=[A0_r], writes=[B2_r])
        S.tag = "lg"
        lg = sb("lg", [128, 4], F32); lg_r = Res("lg")
        S.dma("sp", lambda q: q.dma_start(out=lg[:], in_=W["l0_sgu_ln_g"].rearrange("(g c) -> c g", g=4), allow_slow_non_contiguous=True), writes=[lg_r])
        S.tag = "es"
        S.dma("sp", lambda q: q.dma_start(out=C.vrow[:, 0:8], in_=W["l0_sink"].unsqueeze(0)), writes=[C.vrow_r])
        S.op("pe", lambda q: q.matmul(A1[:, 0:8], lhsT=C.ones_row[:], rhs=C.vrow[:, 0:8], start=True, stop=True), reads=[C.vrow_r, C.const_r], writes=[A1_r])
        es8 = sb("es8", [128, 8], F32); es8_r = Res("es8")
        S.op("act", lambda q: q.activation(out=es8[:], in_=A1[:, 0:8], func=AF.Exp), reads=[A1_r], writes=[es8_r])
        es = sb("es", [128, 8, 128], F32); es_r = Res("es")
        S.op("dve", lambda q: q.tensor_copy(out=es[:], in_=es8[:].unsqueeze(2).broadcast_to([128, 8, 128])), reads=[es8_r], writes=[es_r])

        S.tag = ""
        NS = 4
        xin = [sb(f"xin{i}", [128, 1024], F32) for i in range(NS)]; xin_r = [Res(f"xin{i}") for i in range(NS)]
        hb = [sb(f"hb{i}", [128, 1024], BF16) for i in range(2)]; hb_r = [Res(f"hb{i}") for i in range(2)]
        hT = [sb(f"hT{i}", [128, 8, 128], BF16) for i in range(2)]; hT_r = [Res(f"hT{i}") for i in range(2)]
        uT = [sb(f"uT{i}", [128, 4, 128], BF16) for i in range(2)]; uT_r = [Res(f"uT{i}") for i in range(2)]
        vh2 = [sb(f"vh{i}", [128, 512], F32) for i in range(2)]; vh2_r = [Res(f"vh{i}") for i in range(2)]
        nrm = sb("nrm", [128, 512], BF16); nrm_r = Res("nrm")
        bst = sb("bst", [128, 8], F32); bst_r = Res("bst")
        tsg = sb("tsg", [128, 4, 128], F32); tsg_r = Res("tsg")
        aoT = [sb(f"aoT{i}", [128, 4, 128], BF16) for i in range(NS)]; aoT_r = [Res(f"aoT{i}") for i in range(NS)]
        qk = sb("qk", [128, 10, 64], BF16); qk_r = Res("qk")
        qkf2 = [sb(f"qkf{i}", [128, 640], F32) for i in range(2)]; qkf2_r = [Res(f"qkf{i}") for i in range(2)]
        rt = sb("rt", [128, 2, 10, 16], F32); rt_r = Res("rt")
        qT = [sb(f"qT{i}", [64, 8, 128], BF16) for i in range(NS)]; qT_r = [Res(f"qT{i}") for i in range(NS)]
        kT = sb("kT", [64, 2, 4096], BF16); kT_r = [Res(f"kT{b}") for b in range(32)]
        vp = sb("vp", [128, 32, 2, 128], BF16); vp_r = [Res(f"vp{b}") for b in range(32)]
        PT = [sb(f"PT{i}", [128, 512], BF16) for i in range(NST)]; PT_r = [Res(f"PT{i}") for i in range(NST)]
        den = sb("den", [128, 512], F32); den_r = Res("den")
        rec = sb("rec", [64, 512], F32); rec_r = Res("rec")
        boT = [sb(f"boT{i}", [64, 8, 128], BF16) for i in range(2)]; boT_r = [Res(f"boT{i}") for i in range(2)]
        ost = [sb(f"ost{i}", [128, 1024], F32) for i in range(2)]; ost_r = [Res(f"ost{i}") for i in range(2)]
        stt = sb("stt", [128, 4], F32); stt_r = [Res(f"stt{i}") for i in range(4)]
        S.tag = "vp"
        S.op("pool", lambda q: q.memset(vp[:], 1.0), writes=vp_r)
        S.tag = ""

        pcnt = [0]
        blk0 = 0
        for L in seqs:
            nb = L // 128

            def stage_a(i, blk0=blk0):
              blk = blk0 + i
              s3 = i % NS; s2 = i % 2
              h_ = hT[s2]
              vh = vh2[s2]; vh_r = vh2_r[s2]
              qkf = qkf2[s2]; qkf_r = qkf2_r[s2]
              pieces = []
              late = []

              def piece(f):
                  pieces.append(f)
                  return f

              def latep(f):
                  late.append(f)
                  return f

              @piece
              def _p0():
                S.dma("sp", lambda q: q.dma_start(out=xin[s3][:], in_=x_src[blk * 128:(blk + 1) * 128, :]), reads=[src_res[blk]], writes=[xin_r[s3]])
                rmsnorm_block(S, C, xin[s3][:], xin_r[s3], gt, gt_r, hb[s2][:], hb_r[s2], stt, stt_r[s2], s2)
                transpose_block(S, C, hb[s2], hb_r[s2], T0, T0_r, hT[s2][:], hT_r[s2])

              @piece
              def _p1():
                pass
                S.tag = "u"
                def fu(q):
                    ins = None
                    for c in range(4):
                        for kc in range(8):
                            ins = q.matmul(A0[:, c * 128:(c + 1) * 128], lhsT=win[:, kc, c * 128:(c + 1) * 128], rhs=h_[:, kc, :], start=(kc == 0), stop=(kc == 7))
                    return ins
                S.op("pe", fu, reads=[hT_r[s2], win_r], writes=[A0_r])
                S.op("act", lambda q: q.activation(out=uT[s2][:], in_=A0[:].rearrange("p (c t) -> p c t", c=4), func=AF.Gelu), reads=[A0_r], writes=[uT_r[s2]])
              @piece
              def _p2():
                S.tag = "v"
                mm_group(S, A1[:], [(h_[:, kc, :], win[:, kc, 512:1024]) for kc in range(8)], reads=[hT_r[s2], win_r], writes=[A1_r])
                S.op("act", lambda q: q.activation(out=vh[:], in_=A1[:], func=AF.Gelu), reads=[A1_r], writes=[vh_r])
              @latep
              def _p2b():
                S.tag = "v"
                S.op("dve", lambda q: q.bn_stats(out=bst[:, 0:6], in_=vh[:]), reads=[vh_r], writes=[bst_r])
                S.op("dve", lambda q: q.bn_aggr(out=bst[:, 6:8], in_=bst[:, 0:6]), reads=[bst_r], writes=[bst_r])
                S.op("act", lambda q: q.activation(out=bst[:, 7:8], in_=bst[:, 7:8], func=AF.Ln, bias=C.eps[:, 0:1], scale=1.0), reads=[bst_r, C.const_r], writes=[bst_r])
                S.op("act", lambda q: q.activation(out=bst[:, 7:8], in_=bst[:, 7:8], func=AF.Exp, scale=-0.5), reads=[bst_r], writes=[bst_r])
                S.op("dve", lambda q: q.scalar_tensor_tensor(out=bst[:, 6:7], in0=bst[:, 6:7], scalar=-1.0, in1=bst[:, 7:8], op0=ALU.mult, op1=ALU.mult), reads=[bst_r], writes=[bst_r])
                S.op("act", lambda q: q.activation(out=nrm[:], in_=vh[:], func=AF.Identity, bias=bst[:, 6:7], scale=bst[:, 7:8]), reads=[vh_r, bst_r], writes=[nrm_r])
              @piece
              def _p3():
                S.tag = "qkv"
                mm_group(S, PQ[:, 0:512], [(h_[:, kc, :], win[:, kc, 1024:1536]) for kc in range(8)], reads=[hT_r[s2], win_r], writes=[PQ_r])
                mm_group(S, PQ[:, 512:768], [(h_[:, kc, :], win[:, kc, 1536:1792]) for kc in range(8)], reads=[hT_r[s2], win_r], writes=[PQ_r])
                S.op("act", lambda q: q.copy(out=qkf[:], in_=PQ[:, 0:640]), reads=[PQ_r], writes=[qkf_r])
                S.op("act", lambda q: q.copy(out=vp[:, i, :, 0:64], in_=PQ[:, 640:768].rearrange("p (g d) -> p g d", g=2)), reads=[PQ_r], writes=[vp_r[i]])
              @latep
              def _p3b():
                qk_ps = qkf[:].rearrange("p (h d) -> p h d", h=10)
                cc = rope[:, i, 0:16].unsqueeze(1).broadcast_to([128, 10, 16])
                sn = rope[:, i, 16:32]
                S.tag = "c1"
                S.op("act", lambda q: q.copy(out=qk[:, :, 16:64], in_=qk_ps[:, :, 16:64]), reads=[qkf_r], writes=[qk_r])
                S.tag = "r1"
                S.op("dve", lambda q: q.tensor_tensor(out=rt[:, 0, :, :], in0=qk_ps[:, :, 0:16], in1=cc, op=ALU.mult), reads=[qkf_r, rope_r], writes=[rt_r])
                S.tag = "r2"
                S.op("dve", lambda q: q.tensor_tensor(out=rt[:, 1, :, 0:8], in0=qk_ps[:, :, 8:16], in1=sn[:, 0:8].unsqueeze(1).broadcast_to([128, 10, 8]), op=ALU.mult),
                     reads=[qkf_r, rope_r], writes=[rt_r])
                S.op("dve", lambda q: q.tensor_tensor(out=rt[:, 1, :, 8:16], in0=qk_ps[:, :, 0:8], in1=sn[:, 8:16].unsqueeze(1).broadcast_to([128, 10, 8]), op=ALU.mult),
                     reads=[qkf_r, rope_r], writes=[rt_r])
                S.op("dve", lambda q: q.tensor_tensor(out=qk[:, :, 0:16], in0=rt[:, 0, :, :], in1=rt[:, 1, :, :], op=ALU.add), reads=[rt_r], writes=[qk_r])
              @latep
              def _p4():
                S.tag = "sg"
                def fs(q):
                    ins = None
                    for g in range(4):
                        ins = q.matmul(A1[:, g * 128:(g + 1) * 128], lhsT=nrm[:, g * 128:(g + 1) * 128], rhs=WT[:, g, :], start=True, stop=True)
                    return ins
                S.op("pe", fs, reads=[nrm_r, WT_r], writes=[A1_r])
                for g in range(4):
                    S.op("dve", lambda q, g=g: q.scalar_tensor_tensor(out=tsg[:, g, :], in0=A1[:, g * 128:(g + 1) * 128], scalar=lg[:, g:g + 1], in1=B2[:, g, :],
                                                                     op0=ALU.mult, op1=ALU.add), reads=[A1_r, lg_r, B2_r], writes=[tsg_r])
                S.op("dve", lambda q: q.tensor_tensor(out=aoT[s3][:], in0=tsg[:], in1=uT[s2][:], op=ALU.mult), reads=[tsg_r, uT_r[s2]], writes=[aoT_r[s3]])
              @latep
              def _p5():
                S.tag = "tq"
                def ftq(q):
                    ins = None
                    for h in range(8):
                        ins = q.transpose(out=T0[0:64, h * 128:(h + 1) * 128], in_=qk[:, h, :], identity=C.ident[:])
                    return ins
                S.op("pe", ftq, reads=[qk_r, C.const_r], writes=[T0_r])
                S.op("dve", lambda q: q.tensor_copy(out=qT[s3][:], in_=T0[0:64, :].rearrange("p (h t) -> p h t", h=8)), reads=[T0_r], writes=[qT_r[s3]])
                def ftk(q):
                    ins = None
                    for g in range(2):
                        ins = q.transpose(out=T0[0:64, g * 128:(g + 1) * 128], in_=qk[:, 8 + g, :], identity=C.ident[:])
                    return ins
                S.op("pe", ftk, reads=[qk_r, C.const_r], writes=[T0_r])
                S.op("dve", lambda q: q.tensor_copy(out=kT[:, :, i * 128:(i + 1) * 128], in_=T0[0:64, 0:256].rearrange("p (g t) -> p g t", g=2)),
                     reads=[T0_r], writes=[kT_r[i]])

              pu_, pv2_ = pieces[1], pieces[2]
              pieces[1:3] = [lambda: (pu_(), pv2_())]
              return pieces, late

            def stage_b(m, fillers, blk0=blk0, nb=nb):
                S.tag = ""
                blk = blk0 + m
                s3 = m % NS; s2 = m % 2
                kbs = [kb for kb in (m, m - 1, m + 1) if 0 <= kb < nb]
                tasks = [(g, n_, kb) for g in range(2) for n_, kb in enumerate(kbs)]

                def front(t):
                    g, n_, kb = t
                    p = pcnt[0] % NST; pcnt[0] += 1
                    S.op("pe", lambda q: q.matmul(ST[p][:], lhsT=kT[:, g, kb * 128:(kb + 1) * 128],
                                                  rhs=qT[s3][:, 4 * g:4 * g + 4, :], start=True, stop=True),
                         reads=[kT_r[kb], qT_r[s3]], writes=[ST_r[p]])
                    S.op("act", lambda q: q.activation(out=PT[p][:], in_=ST[p][:], func=AF.Exp, scale=0.125), reads=[ST_r[p]], writes=[PT_r[p]])
                    if kb != m:
                        mi = 0 if kb < m else 1
                        S.op("dve", lambda q: q.tensor_tensor(out=PT[p][:], in0=PT[p][:], in1=wmask[:, mi, :], op=ALU.mult),
                             reads=[PT_r[p], wmask_r], writes=[PT_r[p]])
                    return p

                def back(t, p):
                    g, n_, kb = t
                    S.op("pe", lambda q: q.matmul(PO[:], lhsT=vp[:, kb, g, :], rhs=PT[p][:], start=(n_ == 0), stop=(n_ == len(kbs) - 1)),
                         reads=[vp_r[kb], PT_r[p]], writes=[PO_r])
                    if n_ == len(kbs) - 1:
                        S.op("dve", lambda q: q.tensor_tensor(out=den[64:128, :].rearrange("p (h t) -> p h t", h=4), in0=PO[64:128, :].rearrange("p (h t) -> p h t", h=4),
                                                              in1=es[64:128, 4 * g:4 * g + 4, :], op=ALU.add),
                             reads=[PO_r, es_r], writes=[den_r])
                        S.op("act", lambda q: q.activation(out=rec[:], in_=den[64:128, :], func=AF.Ln), reads=[den_r], writes=[rec_r])
                        S.op("act", lambda q: q.activation(out=rec[:], in_=rec[:], func=AF.Exp, scale=-1.0), reads=[rec_r], writes=[rec_r])
                        S.op("dve", lambda q: q.tensor_tensor(out=boT[s2][:, 4 * g:4 * g + 4, :], in0=PO[0:64, :].rearrange("p (h t) -> p h t", h=4),
                                                              in1=rec[:].rearrange("p (h t) -> p h t", h=4), op=ALU.mult),
                             reads=[PO_r, rec_r], writes=[boT_r[s2]])

                ps = {}
                DEP = NST - 1
                for k in range(len(tasks) + DEP):
                    if fillers:
                        fillers.pop(0)()
                    if k < len(tasks):
                        ps[k] = front(tasks[k])
                    if k >= DEP:
                        back(tasks[k - DEP], ps[k - DEP])
                while fillers:
                    fillers.pop(0)()
                S.tag = ""
                o = m % 2
                for dh, (PY, PY_r) in enumerate(((A0, A0_r), (A1, A1_r))):
                    pairs = [(aoT[s3][:, c, :], woA[:, c, dh * 512:(dh + 1) * 512]) for c in range(4)]
                    pairs += [(boT[s2][:, h, :], woB[:, h, dh * 512:(dh + 1) * 512]) for h in range(8)]
                    mm_group(S, PY[:], pairs, reads=[aoT_r[s3], boT_r[s2], woA_r, woB_r], writes=[PY_r])
                    S.op("dve", lambda q, dh=dh, PY=PY: q.tensor_tensor(out=ost[o][:, dh * 512:(dh + 1) * 512], in0=PY[:], in1=xin[s3][:, dh * 512:(dh + 1) * 512], op=ALU.add),
                         reads=[PY_r, xin_r[s3]], writes=[ost_r[o]])
                S.dma("sp", lambda q: q.dma_start(out=x_dst[blk * 128:(blk + 1) * 128, :], in_=ost[o][:]), reads=[ost_r[o]], writes=[dst_res[blk]], sem_res=ost_r[o])

            dbg = 0
            late_prev = []
            for it in range(nb + 3):
                S.tag = ""
                if dbg == 1:
                    break
                early, late_new = stage_a(it) if it < nb else ([], [])
                pieces = []
                while early or late_prev:
                    if early:
                        pieces.append(early.pop(0))
                    if late_prev:
                        pieces.append(late_prev.pop(0))
                late_prev = late_new
                if it >= 3 and dbg != 2:
                    stage_b(it - 3, pieces)
                else:
                    while pieces:
                        pieces.pop(0)()
            blk0 += nb
        S.barrier()
        S.flush()


def mixer_c_phase(S, C, nc, W, x_src, src_res, x_dst, dst_res, seqs):
    with ExitStack() as ph:
        sb = lambda n, sh, dt: ph.enter_context(nc.sbuf_tensor(U(n), sh, dt))
        pst = lambda n, sh, dt: ph.enter_context(nc.psum_tensor(U(n), sh, dt))
        C.pbc = pst("pbc", [128, 1024], F32); C.pbc_r = Res("pbc")
        gt, gt_r = load_gain(S, C, ph, nc, W["l1_mix_norm"], "gt")
        P0 = C.pbc[:, 0:512]; P1 = C.pbc[:, 512:1024]
        P_r = [Res("P0"), Res("P1")]
        T0 = C.pbc.bitcast(BF16); T0_r = P_r[0]
        NST = 4
        ST = [pst(f"ST{i}", [128, 512], F32) for i in range(NST)]; ST_r = [Res(f"ST{i}") for i in range(NST)]
        PO = [pst(f"PO{i}", [128, 512], F32) for i in range(2)]; PO_r = [Res(f"PO{i}") for i in range(2)]

        wq = sb("wq", [128, 8, 3072], BF16); wq_r = Res("wq")
        wq_v = W["l1_w_qkv"].rearrange("(kc p) f -> p kc f", p=128)
        for h in range(3):
            S.dma("pool", lambda q, h=h: q.dma_start(out=wq[:, :, h * 1024:(h + 1) * 1024], in_=wq_v[:, :, h * 1024:(h + 1) * 1024]),
                  writes=[wq_r], sem_res=Res("wql"))
        woC = sb("woC", [128, 8, 1024], BF16); woC_r = Res("woC")
        S.dma("pool", lambda q: q.dma_start(out=woC[:], in_=W["l1_w_out"].rearrange("(c p) d -> p c d", p=128)), writes=[woC_r])
        E = sb("E", [128, 7, 16, 128], BF16); E_r = Res("E")
        if True:
            sb2 = sb
            rp = sb2("rp", [16, 15, 31], F32); rp_r = Res("rp")
            rr = sb2("rr", [16, 15, 31], F32); rr_r = Res("rr")
            zt = sb2("zt", [1, 64], F32); zt_r = Res("zt")
            cm = sb2("cm", [128, 128], F32); cm_r = Res("cm")
            est = sb2("est", [128, 16, 128], F32); est_r = Res("est")
            rsc_r = Res("rsc"); gsc_r = Res("gsc")
            S.dma("sp", lambda q: q.dma_start(out=rp[:], in_=W["l1_rpb"]), writes=[rp_r])
            S.dma("sp", lambda q: q.dma_start(out=cm[:], in_=C.cmask_in), writes=[cm_r])
            S.op("dve", lambda q: q.memset(zt[:], 0.0), writes=[zt_r])
            for t in range(31):
                S.op("dve" if t % 2 else "act",
                     (lambda t: (lambda q: q.tensor_copy(out=rr[:, :, 30 - t:31 - t], in_=rp[:, :, t:t + 1])) if t % 2 else
                      (lambda q: q.copy(out=rr[:, :, 30 - t:31 - t], in_=rp[:, :, t:t + 1])))(t),
                     reads=[rp_r], writes=[rr_r])
            rsc = C.rsc
            S.dma("sp", lambda q: q.dma_start(out=rsc[0:64].unsqueeze(0), in_=zt[:]), reads=[zt_r], writes=[rsc_r], sem_res=rsc_r)
            S.dma("sp", lambda q: q.dma_start(out=rsc[64 + 7440:128 + 7440].unsqueeze(0), in_=zt[:]), reads=[zt_r], writes=[rsc_r], sem_res=rsc_r)
            S.dma("sp", lambda q: q.dma_start(out=rsc[64:64 + 7440].rearrange("(h x) -> h x", h=16), in_=rr[:].rearrange("p a b -> p (a b)")),
                  reads=[rr_r], writes=[rsc_r], sem_res=rsc_r)
            gsc = C.gsc
            for kc in range(64):
                src = bass.AP(C.rsc_t, 64 + 15 - kc, [[31, 240], [1, 64]])
                S.dma("sp", lambda q, kc=kc, src=src: q.dma_start(out=gsc[:, kc, :], in_=src), reads=[rsc_r], writes=[gsc_r], sem_res=gsc_r, nowaw=(kc % 8 != 0))
            g4 = gsc.rearrange("(h r) k c -> h r k c", h=16)
            for di, dl in enumerate(range(-3, 4)):
                S.op("dve", lambda q: q.memset(est[:], 0.0), writes=[est_r])
                for kr in range(2):
                    for qr in range(2):
                        dr = 2 * dl + kr - qr + 7
                        if 0 <= dr <= 14:
                            S.dma("sp", lambda q, kr=kr, qr=qr, dr=dr: q.dma_start(
                                out=est[kr * 64:(kr + 1) * 64, :, qr * 64:(qr + 1) * 64], in_=g4[:, dr, :, :].rearrange("h k c -> k h c")),
                                reads=[gsc_r], writes=[est_r], nowaw=(kr + qr > 0))
                S.op("act", lambda q: q.activation(out=est[:], in_=est[:], func=AF.Exp), reads=[est_r], writes=[est_r])
                S.op("dve", lambda q, di=di: q.tensor_tensor(out=E[:, di, :, :], in0=est[:], in1=cm[:].unsqueeze(1).broadcast_to([128, 16, 128]), op=ALU.mult),
                     reads=[est_r, cm_r], writes=[E_r])

        NQ = 4
        NR = 7
        xin = [sb(f"xin{i}", [128, 1024], F32) for i in range(NQ)]; xin_r = [Res(f"xin{i}") for i in range(NQ)]
        hb = [sb("hb0", [128, 1024], BF16)] * 2; hb_r = [Res("hb0")] * 2
        hT = [sb("hT0", [128, 8, 128], BF16)] * 2; hT_r = [Res("hT0")] * 2
        qT = [sb(f"qT{i}", [128, 8, 128], BF16) for i in range(NQ)]; qT_r = [Res(f"qT{i}") for i in range(NQ)]
        kT = sb("kT", [128, 8, NR * 128], BF16); kT_r = [Res(f"kT{i}") for i in range(NR)]
        vp = sb("vp", [128, NR, 16, 128], BF16); vp_r = [Res(f"vp{i}") for i in range(NR)]
        PT = [sb(f"PT{i}", [128, 512], BF16) for i in range(NST)]; PT_r = [Res(f"PT{i}") for i in range(NST)]
        rec = sb("rec", [128, 512], F32); rec_r = Res("rec")
        ocT = [sb("ocT0", [128, 8, 128], BF16)] * 2; ocT_r = [Res("ocT0")] * 2
        ost = [sb(f"ost{i}", [128, 1024], F32) for i in range(2)]; ost_r = [Res(f"ost{i}") for i in range(2)]
        stt = sb("stt", [128, 4], F32); stt_r = [Res(f"stt{i}") for i in range(4)]
        S.op("pool", lambda q: q.memset(vp[:], 1.0), writes=vp_r)
        zb = sb("zb", [128, 128], BF16); zb_r = Res("zb")
        S.op("dve", lambda q: q.memset(zb[:], 0.0), writes=[zb_r])

        pc = [0]
        sc = [0]
        oc = [0]
        blk0 = 0
        for L in seqs:
            nb = L // 128
            R = L // 64

            def stage_a(i, blk0=blk0):
                blk = blk0 + i
                sq = i % NQ; s2 = i % 2; sr = i % NR
                h_ = hT[s2]
                pieces = []

                def p0():
                    S.dma("sp", lambda q: q.dma_start(out=xin[sq][:], in_=x_src[blk * 128:(blk + 1) * 128, :]), reads=[src_res[blk]], writes=[xin_r[sq]])
                    rmsnorm_block(S, C, xin[sq][:], xin_r[sq], gt, gt_r, hb[s2][:], hb_r[s2], stt, stt_r[s2], s2)
                    transpose_block(S, C, hb[s2], hb_r[s2], T0, T0_r, hT[s2][:], hT_r[s2])
                pieces.append(p0)

                def pq(bq):
                    p = pc[0] % 2; pc[0] += 1
                    Pb = P0 if p == 0 else P1
                    def fq(q):
                        ins = None
                        for c4 in range(4):
                            ch = bq * 4 + c4
                            for kc in range(8):
                                ins = q.matmul(Pb[:, c4 * 128:(c4 + 1) * 128], lhsT=wq[:, kc, ch * 128:(ch + 1) * 128], rhs=h_[:, kc, :], start=(kc == 0), stop=(kc == 7))
                        return ins
                    S.op("pe", fq, reads=[hT_r[s2], wq_r], writes=[P_r[p]])
                    src = Pb.rearrange("p (c t) -> p c t", c=4)
                    if bq < 2:
                        S.op("act", lambda q: q.copy(out=qT[sq][:, bq * 4:(bq + 1) * 4, :], in_=src), reads=[P_r[p]], writes=[qT_r[sq]])
                    else:
                        S.op("act", lambda q: q.copy(out=kT[:, (bq - 2) * 4:(bq - 1) * 4, sr * 128:(sr + 1) * 128], in_=src), reads=[P_r[p]], writes=[kT_r[sr]])
                for bq in range(4):
                    pieces.append(lambda bq=bq: pq(bq))

                def pv_(dh):
                    p = pc[0] % 2; pc[0] += 1
                    Pb = P0 if p == 0 else P1
                    mm_group(S, Pb, [(h_[:, kc, :], wq[:, kc, 2048 + dh * 512:2048 + (dh + 1) * 512]) for kc in range(8)], reads=[hT_r[s2], wq_r], writes=[P_r[p]])
                    pv3 = Pb.rearrange("p (h d) -> p h d", h=8)
                    S.op("act", lambda q: q.copy(out=vp[:, sr, dh * 8:(dh + 1) * 8:2, 0:64], in_=pv3[:, 0:8:2, :]), reads=[P_r[p]], writes=[vp_r[sr]])
                    S.op("act", lambda q: q.copy(out=vp[:, sr, dh * 8 + 1:(dh + 1) * 8:2, 64:128], in_=pv3[:, 1:8:2, :]), reads=[P_r[p]], writes=[vp_r[sr]])
                for dh in range(2):
                    pieces.append(lambda dh=dh: pv_(dh))
                return pieces

            def make_plan(m, R=R):
                rs = [min(max(2 * m + qr - 4, 0), R - 8) for qr in range(2)]
                js = sorted({(rs[qr] + a) // 2 for qr in range(2) for a in range(8)})
                js = [m] + [j for j in js if j != m]
                plan = []
                for j in js:
                    v = [[rs[qr] <= 2 * j + kr < rs[qr] + 8 for qr in range(2)] for kr in range(2)]
                    rects = []
                    if all(v[kr][qr] for kr in range(2) for qr in range(2)):
                        rects.append((0, 128, 0, 128))
                    else:
                        for qr in range(2):
                            ks = [kr for kr in range(2) if v[kr][qr]]
                            if len(ks) == 2:
                                rects.append((0, 128, qr * 64, 64))
                            elif len(ks) == 1:
                                rects.append((ks[0] * 64, 64, qr * 64, 64))
                    plan.append((j, rects))
                assert plan[0][1] == [(0, 128, 0, 128)]
                return plan

            def stage_b(m, fillers, blk0=blk0, nb=nb, R=R):
                blk = blk0 + m
                sq = m % NQ
                o2 = oc[0] % 2; oc[0] += 1
                plan = make_plan(m)
                nj = len(plan)
                tasks = [(hq, jn) for hq in range(4) for jn in range(nj)]

                def front(t):
                    hq, jn = t
                    j, rects = plan[jn]
                    sr = j % NR
                    di = j - m + 3
                    p = sc[0] % NST; sc[0] += 1
                    def fsc(q):
                        ins = None
                        for hh in range(4):
                            h = 2 * (4 * (hq // 2) + hh) + (hq % 2)
                            pb = (h % 2) * 64
                            ins = q.matmul(ST[p][:, hh * 128:(hh + 1) * 128], lhsT=kT[pb:pb + 64, h // 2, sr * 128:(sr + 1) * 128],
                                           rhs=qT[sq][pb:pb + 64, h // 2, :], start=True, stop=True)
                        return ins
                    S.op("pe", fsc, reads=[kT_r[sr], qT_r[sq]], writes=[ST_r[p]])
                    S.op("act", lambda q: q.activation(out=PT[p][:], in_=ST[p][:], func=AF.Exp, scale=0.125), reads=[ST_r[p]], writes=[PT_r[p]])
                    S.op("dve", lambda q: q.tensor_tensor(out=PT[p][:].rearrange("p (h t) -> p h t", h=4), in0=PT[p][:].rearrange("p (h t) -> p h t", h=4),
                                                          in1=E[:, di, 8 * (hq // 2) + (hq % 2):8 * (hq // 2) + 8:2, :], op=ALU.mult),
                         reads=[PT_r[p], E_r], writes=[PT_r[p]])
                    return p

                def back(t, p):
                    hq, jn = t
                    j, rects = plan[jn]
                    sr = j % NR
                    po = hq % 2
                    if jn == 0:
                        S.op("pe", lambda q: q.matmul(PO[po][:], lhsT=zb[:], rhs=E[:, 3, 0:4, :], start=True, stop=False),
                             reads=[zb_r, E_r], writes=[PO_r[po]])
                    def fpv(q):
                        ins = None
                        for hh in range(4):
                            h = 2 * (4 * (hq // 2) + hh) + (hq % 2)
                            for ri, (k0, kn, c0, cn) in enumerate(rects):
                                ins = q.matmul(PO[po][:, hh * 128 + c0:hh * 128 + c0 + cn], lhsT=vp[k0:k0 + kn, sr, h, :],
                                               rhs=PT[p][k0:k0 + kn, hh * 128 + c0:hh * 128 + c0 + cn],
                                               start=False, stop=(jn == nj - 1 and ri == len(rects) - 1))
                        return ins
                    S.op("pe", fpv, reads=[vp_r[sr], PT_r[p]], writes=[PO_r[po]])
                    if jn == nj - 1:
                        if hq % 2 == 0:
                            o_lo, d_lo = 0, 64
                        else:
                            o_lo, d_lo = 64, 0
                        S.op("act", lambda q: q.activation(out=rec[o_lo:o_lo + 64, :], in_=PO[po][d_lo:d_lo + 64, :], func=AF.Ln), reads=[PO_r[po]], writes=[rec_r])
                        S.op("act", lambda q: q.activation(out=rec[o_lo:o_lo + 64, :], in_=rec[o_lo:o_lo + 64, :], func=AF.Exp, scale=-1.0), reads=[rec_r], writes=[rec_r])
                        S.op("dve", lambda q: q.tensor_tensor(out=ocT[o2][o_lo:o_lo + 64, 4 * (hq // 2):4 * (hq // 2) + 4, :],
                                                              in0=PO[po][o_lo:o_lo + 64, :].rearrange("p (h t) -> p h t", h=4),
                                                              in1=rec[o_lo:o_lo + 64, :].rearrange("p (h t) -> p h t", h=4), op=ALU.mult),
                             reads=[PO_r[po], rec_r], writes=[ocT_r[o2]])

                DEP = NST - 1
                ps = {}
                for k in range(len(tasks) + DEP):
                    if fillers and k % 3 == 1:
                        fillers.pop(0)()
                    if k < len(tasks):
                        ps[k] = front(tasks[k])
                    if k >= DEP:
                        back(tasks[k - DEP], ps[k - DEP])
                while fillers:
                    fillers.pop(0)()
                S.tag = ""
                for dh in range(2):
                    p = pc[0] % 2; pc[0] += 1
                    Pb = P0 if p == 0 else P1
                    mm_group(S, Pb, [(ocT[o2][:, h, :], woC[:, h, dh * 512:(dh + 1) * 512]) for h in range(8)], reads=[ocT_r[o2], woC_r], writes=[P_r[p]])
                    S.op("dve", lambda q, dh=dh, Pb=Pb: q.tensor_tensor(out=ost[o2][:, dh * 512:(dh + 1) * 512], in0=Pb, in1=xin[sq][:, dh * 512:(dh + 1) * 512], op=ALU.add),
                         reads=[P_r[p], xin_r[sq]], writes=[ost_r[o2]])
                S.dma("sp", lambda q: q.dma_start(out=x_dst[blk * 128:(blk + 1) * 128, :], in_=ost[o2][:]), reads=[ost_r[o2]], writes=[dst_res[blk]], sem_res=ost_r[o2])

            dbg = 0
            for it in range(nb + 3):
                if dbg == 1:
                    break
                pieces = stage_a(it) if it < nb else []
                if it >= 3 and dbg != 2:
                    m = it - 3
                    if max(j for j, _ in make_plan(m)) >= it:
                        while pieces:
                            pieces.pop(0)()
                    stage_b(m, pieces)
                else:
                    while pieces:
                        pieces.pop(0)()
            blk0 += nb
        S.barrier()
        S.flush()


def build_nc(seqs=SEQS, nphase=6, only=None):
    nc = bass.Bass("TRN2", target_bir_lowering=False)
    print("sbuf bytes remaining at start", nc.sbuf_bytes_remaining)
    ntok = sum(seqs)
    nblk = ntok // 128
    x_in = nc.dram_tensor("x", [ntok, D], F32, kind="ExternalInput").ap()
    W = {n: nc.dram_tensor(n, list(s), F32, kind="ExternalInput").ap() for n, s in WSPECS}
    C = Ctx()
    ident_in = nc.dram_tensor("c_ident", [128, 128], F32, kind="ExternalInput").ap()
    C.rope_in = nc.dram_tensor("c_rope", [4096, 32], F32, kind="ExternalInput").ap()
    C.wmask_in = nc.dram_tensor("c_wmask", [128, 2, 512], F32, kind="ExternalInput").ap()
    C.cmask_in = nc.dram_tensor("c_cmask", [128, 128], F32, kind="ExternalInput").ap()
    y_out = nc.dram_tensor("y", [ntok, D], F32, kind="ExternalOutput").ap()
    xa = nc.dram_tensor("xa", [ntok, D], F32).ap()
    xb = nc.dram_tensor("xb", [ntok, D], F32).ap()
    C.rsc_t = nc.dram_tensor("rsc", [7440 + 128], F32)
    C.rsc = C.rsc_t.ap()
    C.gsc = nc.dram_tensor("gsc", [240, 64, 64], F32).ap()
    in_r = [Res(f"in{b}") for b in range(nblk)]
    xa_r = [Res(f"xa{b}") for b in range(nblk)]
    xb_r = [Res(f"xb{b}") for b in range(nblk)]
    y_r = [Res(f"y{b}") for b in range(nblk)]
    with ExitStack() as st:
        S = Sched(nc, st)
        sb = lambda n, sh, dt: st.enter_context(nc.sbuf_tensor(U(n), sh, dt))
        C.ident = sb("ident", [128, 128], BF16)
        C.identf = sb("identf", [128, 128], F32)
        C.ones_row = sb("ones_row", [1, 128], F32)
        C.ones_col = sb("ones_col", [128, 1], F32)
        C.eps = sb("eps", [128, 1], F32)
        C.vrow = sb("vrow", [1, 1024], F32); C.vrow_r = Res("vrow")
        C.junk = sb("junk", [128, 1024], BF16)
        C.const_r = Res("const")
        cl = Res("constload")
        S.dma("pool", lambda q: q.dma_start(out=C.ident[:], in_=ident_in), writes=[C.const_r], sem_res=cl)
        S.dma("sp", lambda q: q.dma_start(out=C.identf[:], in_=ident_in), writes=[C.const_r], sem_res=cl)
        S.op("dve", lambda q: q.memset(C.ones_row[:], 1.0), writes=[C.const_r])
        S.op("dve", lambda q: q.memset(C.ones_col[:], 1.0), writes=[C.const_r])
        S.op("dve", lambda q: q.memset(C.eps[:], EPS), writes=[C.const_r])
        S.barrier()

        def ffn(src, src_r, dst, dst_r, pre, final=False):
            ffn_phase(S, C, nc, src, src_r, dst, dst_r, W[pre + "_norm"], W[pre + "_w_gate"], W[pre + "_w_up"], W[pre + "_w_down"], ntok,
                      final_g=(W["final_norm"] if final else None), is_output=(dst is y_out))

        phases = [
            lambda s, sr, d, dr: ffn(s, sr, d, dr, "l0_ffn1"),
            lambda s, sr, d, dr: mixer_ab_phase(S, C, nc, W, s, sr, d, dr, seqs),
            lambda s, sr, d, dr: ffn(s, sr, d, dr, "l0_ffn2"),
            lambda s, sr, d, dr: ffn(s, sr, d, dr, "l1_ffn1"),
            lambda s, sr, d, dr: mixer_c_phase(S, C, nc, W, s, sr, d, dr, seqs),
            lambda s, sr, d, dr: ffn(s, sr, d, dr, "l1_ffn2", final=(nphase == 6)),
        ]
        cur, cur_r = x_in, in_r
        scr = [(xa, xa_r), (xb, xb_r)]
        for pi in range(nphase):
            if only is not None and pi != only:
                continue
            if pi == nphase - 1:
                dst, dst_r = y_out, y_r
            else:
                dst, dst_r = scr[pi % 2]
            phases[pi](cur, cur_r, dst, dst_r)
            cur, cur_r = dst, dst_r
        S.flush(final=True)
    return nc


def host_consts():
    ident = np.eye(128, dtype=np.float32)
    pos = np.arange(4096, dtype=np.float64)[:, None]
    inv = np.power(500000.0, -np.arange(0, 16, 2, dtype=np.float64) / 16.0)[None, :]
    ang = (pos.astype(np.float32) * inv.astype(np.float32)).astype(np.float32)
    cos = np.cos(ang).astype(np.float32); sin = np.sin(ang).astype(np.float32)
    rope = np.concatenate([cos, cos, -sin, sin], axis=1).astype(np.float32)
    j = np.arange(128)[:, None]; i = np.arange(128)[None, :]
    m0 = (j >= i).astype(np.float32); m1 = (j <= i).astype(np.float32)
    wmask = np.stack([np.tile(m0, (1, 4)), np.tile(m1, (1, 4))], axis=1).astype(np.float32)
    qc = np.arange(64)
    cs = np.clip(qc - 8, 0, 48)
    kc = np.arange(64)[:, None]
    cm64 = ((kc >= cs[None, :]) & (kc < cs[None, :] + 16)).astype(np.float32)
    cmask = np.tile(cm64, (2, 2)).astype(np.float32)
    return {"c_ident": ident, "c_rope": rope, "c_wmask": wmask, "c_cmask": cmask}


_NC_CACHE = {}


def kernel(**inputs):
    n = 8
    xp = np.asarray(inputs["x_prompt"], dtype=np.float32)
    xs = np.asarray(inputs["x_sample"], dtype=np.float32)
    if "nc" not in _NC_CACHE:
        _NC_CACHE["nc"] = build_nc(SEQS, 6)
    nc = _NC_CACHE["nc"]
    consts = host_consts()
    wmaps = {nm: np.ascontiguousarray(np.asarray(inputs[nm], dtype=np.float32)) for nm, _ in WSPECS}
    in_maps = []
    for c in range(n):
        xc = np.concatenate([xp[c], xs[2 * c], xs[2 * c + 1]], axis=0)
        m = {"x": np.ascontiguousarray(xc)}
        m.update(wmaps)
        m.update(consts)
        in_maps.append(m)
    res = run_bass_kernel_spmd(nc, in_maps, core_ids=list(range(n)))
    yp = np.empty_like(xp)
    ys = np.empty_like(xs)
    for c in range(n):
        y = np.asarray(res.results[c]["y"], dtype=np.float32)
        yp[c] = y[0:4096]
        ys[2 * c] = y[4096:6144]
        ys[2 * c + 1] = y[6144:8192]
    return (yp, ys)
```

```python
import numpy as np
from contextlib import ExitStack
import concourse.bass as bass
import concourse.mybir as mybir
from concourse.bass_utils import run_bass_kernel_spmd

F32 = mybir.dt.float32
BF16 = mybir.dt.bfloat16
AF = mybir.ActivationFunctionType
ALU = mybir.AluOpType

D = 1024
DFF = 2816
NFC = DFF // 128
EPS = 1e-6
SEQS = (4096, 2048, 2048)

WSPECS = [
    ("l0_ffn1_norm", (1024,)), ("l0_ffn1_w_gate", (1024, 2816)), ("l0_ffn1_w_up", (1024, 2816)),
    ("l0_ffn1_w_down", (2816, 1024)), ("l0_mix_norm", (1024,)), ("l0_w_in", (1024, 1792)),
    ("l0_sgu_ln_g", (512,)), ("l0_sgu_ln_b", (512,)), ("l0_sgu_w", (4, 128, 128)), ("l0_sgu_b", (4, 128)),
    ("l0_sink", (8,)), ("l0_w_out", (1024, 1024)), ("l0_ffn2_norm", (1024,)),
    ("l0_ffn2_w_gate", (1024, 2816)), ("l0_ffn2_w_up", (1024, 2816)), ("l0_ffn2_w_down", (2816, 1024)),
    ("l1_ffn1_norm", (1024,)), ("l1_ffn1_w_gate", (1024, 2816)), ("l1_ffn1_w_up", (1024, 2816)),
    ("l1_ffn1_w_down", (2816, 1024)), ("l1_mix_norm", (1024,)), ("l1_w_qkv", (1024, 3072)),
    ("l1_rpb", (16, 15, 31)), ("l1_w_out", (1024, 1024)), ("l1_ffn2_norm", (1024,)),
    ("l1_ffn2_w_gate", (1024, 2816)), ("l1_ffn2_w_up", (1024, 2816)), ("l1_ffn2_w_down", (2816, 1024)),
    ("final_norm", (1024,)),
]


class Res:
    __slots__ = ("name", "w", "r", "dsem", "dcnt")

    def __init__(self, name):
        self.name = name
        self.w = None
        self.r = {}
        self.dsem = None
        self.dcnt = 0


class Sched:
    ENGS = ("pe", "act", "dve", "pool", "sp")

    def __init__(self, nc, stack):
        self.nc = nc
        self.stack = stack
        self.q = {e: [] for e in self.ENGS}
        self.cnt = {e: 0 for e in self.ENGS}
        self.sem = {}
        self.seen = {e: {} for e in self.ENGS}
        self.nsem = 0
        for e in ("pe", "act", "dve", "pool"):
            self.sem[e] = self.new_sem("c_" + e)
        self.out_tokens = []
        self.free_d = {}
        self.live_d = []
        self.dma_hi = {}

    def new_sem(self, name):
        self.nsem += 1
        return self.stack.enter_context(self.nc.semaphore(f"{name}_{self.nsem}"))

    def _waits(self, eng, reads, writes, nowaw=False):
        waits = {}

        def need(tok):
            if tok is None:
                return
            sem, val = tok
            if eng == "pe" and sem is self.sem["pe"]:
                return
            k = id(sem)
            if self.seen[eng].get(k, 0) >= val:
                return
            if k not in waits or waits[k][1] < val:
                waits[k] = (sem, val)

        for r in reads:
            need(r.w)
        for w in writes:
            if not nowaw:
                need(w.w)
            for tok in w.r.values():
                need(tok)
        for k, (sem, val) in waits.items():
            self.seen[eng][k] = val
        return list(waits.values())

    def _commit(self, tok, reads, writes):
        k = id(tok[0])
        for r in reads:
            old = r.r.get(k)
            if old is None or old[1] < tok[1]:
                r.r[k] = tok
        for w in writes:
            w.w = tok
            w.r = {}

    tag = ""

    def op(self, eng, fn, reads=(), writes=()):
        waits = self._waits(eng, reads, writes)
        self.cnt[eng] += 1
        tok = (self.sem[eng], self.cnt[eng])
        self._commit(tok, reads, writes)
        self.q[eng].append((waits, fn, tok, 1))
        return tok

    def dma(self, eng, fn, reads=(), writes=(), sem_res=None, is_output=False, nowaw=False):
        waits = self._waits(eng, reads, writes, nowaw)
        if sem_res is None:
            sem_res = writes[0]
        if sem_res.dsem is None:
            sem_res.dsem = {}
        if eng not in sem_res.dsem:
            fl = self.free_d.setdefault(eng, [])
            if fl:
                sem_res.dsem[eng] = list(fl.pop())
            else:
                sem_res.dsem[eng] = [self.new_sem("d" + eng), 0]
            self.live_d.append((sem_res, eng))
        ent = sem_res.dsem[eng]
        ent[1] += 16
        tok = (ent[0], ent[1])
        self.dma_hi[id(tok[0])] = tok
        self._commit(tok, reads, writes)
        self.q[eng].append((waits, fn, tok, 16))
        if is_output:
            self.out_tokens.append(tok)
        return tok

    def barrier(self):
        toks = [(self.sem[e], self.cnt[e]) for e in ("pe", "act", "dve", "pool") if self.cnt[e] > 0]
        toks += list(self.dma_hi.values())
        for e in self.ENGS:
            ws = []
            for sem, val in toks:
                if self.seen[e].get(id(sem), 0) < val:
                    ws.append((sem, val))
                    self.seen[e][id(sem)] = val
            if ws:
                self.q[e].append((ws, None, None, 0))
        for r, eng in self.live_d:
            self.free_d.setdefault(eng, []).append(tuple(r.dsem.pop(eng)))
        self.live_d = []

    def flush(self, final=False):
        nc = self.nc
        q = self.q
        fin = []
        if final:
            d = {}
            for sem, val in self.out_tokens:
                if id(sem) not in d or d[id(sem)][1] < val:
                    d[id(sem)] = (sem, val)
            fin = list(d.values())
        csem = {id(self.sem[e]): e for e in ("pe", "act", "dve", "pool")}
        needed = {e: set() for e in csem.values()}
        for e in self.ENGS:
            for waits, fn, tok, inc in q[e]:
                for sem, val in waits:
                    if id(sem) in csem:
                        needed[csem[id(sem)]].add(val)
        if not hasattr(self, "base"):
            self.base = {e: 0 for e in csem.values()}
        remap = {}
        for e, vals in needed.items():
            for rank, v in enumerate(sorted(vals)):
                remap[(e, v)] = self.base[e] + rank + 1
            self.base[e] += len(vals)

        def tr(sem, val):
            if id(sem) in csem:
                return remap[(csem[id(sem)], val)]
            return val

        def run(e, items, extra=()):
            for waits, fn, tok, inc in items:
                for sem, val in waits:
                    e.wait_ge(sem, tr(sem, val))
                if fn is not None:
                    ins = fn(e)
                    if inc == 16:
                        ins.then_inc(tok[0], 16)
                    elif (csem[id(tok[0])], tok[1]) in remap:
                        ins.then_inc(tok[0], 1)
            for sem, val in extra:
                e.wait_ge(sem, val)

        with nc.Block() as block:
            @block.sync
            def _(e):
                run(e, q["sp"], fin)

            @block.gpsimd
            def _(e):
                run(e, q["pool"])

            @block.tensor
            def _(e):
                run(e, q["pe"])

            @block.scalar
            def _(e):
                run(e, q["act"])

            @block.vector
            def _(e):
                run(e, q["dve"])
        self.q = {e: [] for e in self.ENGS}
        self.n_inc = getattr(self, "n_inc", 0) + sum(len(v) for v in needed.values())


def mm_group(S, out_ap, pairs, reads, writes):
    def f(q, pairs=pairs, out_ap=out_ap):
        n = len(pairs)
        ins = None
        for i, (l, r) in enumerate(pairs):
            ins = q.matmul(out_ap, lhsT=l, rhs=r, start=(i == 0), stop=(i == n - 1))
        return ins
    S.op("pe", f, reads, writes)


class Ctx:
    pass


_UC = [0]


def U(n):
    _UC[0] += 1
    return f"{n}_{_UC[0]}"


def rmsnorm_block(S, C, x_ap, x_res, gt, gt_res, hb, hb_res, st, st_res, col):
    S.op("act", lambda q: q.activation(out=C.junk[:], in_=x_ap, func=AF.Square, scale=1.0 / 32.0,
                                       accum_out=st[:, col:col + 1]), reads=[x_res], writes=[st_res])
    S.op("act", lambda q: q.activation(out=st[:, col:col + 1], in_=st[:, col:col + 1], func=AF.Ln,
                                       bias=C.eps[:, 0:1], scale=1.0), reads=[st_res, C.const_r], writes=[st_res])
    S.op("act", lambda q: q.activation(out=st[:, col:col + 1], in_=st[:, col:col + 1], func=AF.Exp, scale=-0.5), reads=[st_res], writes=[st_res])
    S.op("dve", lambda q: q.scalar_tensor_tensor(out=hb, in0=x_ap, scalar=st[:, col:col + 1], in1=gt[:],
                                                 op0=ALU.mult, op1=ALU.mult),
         reads=[x_res, st_res, gt_res], writes=[hb_res])


def transpose_block(S, C, hb, hb_res, ptr, ptr_res, dst3, dst_res, eng="act"):
    def f(q):
        ins = None
        for kc in range(8):
            ins = q.transpose(out=ptr[:, kc * 128:(kc + 1) * 128], in_=hb[:, kc * 128:(kc + 1) * 128], identity=C.ident[:])
        return ins
    S.op("pe", f, reads=[hb_res, C.const_r], writes=[ptr_res])
    src = ptr[:, 0:1024].rearrange("p (k t) -> p k t", k=8)
    if eng == "act":
        S.op("act", lambda q: q.copy(out=dst3, in_=src), reads=[ptr_res], writes=[dst_res])
    else:
        S.op("dve", lambda q: q.tensor_copy(out=dst3, in_=src), reads=[ptr_res], writes=[dst_res])


def load_gain(S, C, ph, nc, g_ap, name):
    gt = ph.enter_context(nc.sbuf_tensor(U(name), [128, 1024], F32))
    gt_r = Res(name)
    S.dma("sp", lambda q: q.dma_start(out=C.vrow[:], in_=g_ap.unsqueeze(0)), writes=[C.vrow_r])
    def f(q):
        q.matmul(C.pbc[:, 0:512], lhsT=C.ones_row[:], rhs=C.vrow[:, 0:512], start=True, stop=True)
        return q.matmul(C.pbc[:, 512:1024], lhsT=C.ones_row[:], rhs=C.vrow[:, 512:1024], start=True, stop=True)
    S.op("pe", f, reads=[C.vrow_r, C.const_r], writes=[C.pbc_r])
    S.op("act", lambda q: q.copy(out=gt[:], in_=C.pbc[:]), reads=[C.pbc_r], writes=[gt_r])
    return gt, gt_r


def ffn_phase(S, C, nc, x_src, src_res, x_dst, dst_res, g_ap, wg, wu, wd, ntok, final_g=None, is_output=False):
    TM = 1024
    NB = TM // 128
    NX = 4
    NO = 3
    groups = [(0, 4), (4, 4), (8, 4), (12, 4), (16, 4), (20, 2)]
    with ExitStack() as ph:
        sb = lambda n, sh, dt: ph.enter_context(nc.sbuf_tensor(U(n), sh, dt))
        pst = lambda n, sh, dt: ph.enter_context(nc.psum_tensor(U(n), sh, dt))
        C.pbc = pst("pbc", [128, 1024], F32); C.pbc_r = Res("pbc")
        gt, gt_r = load_gain(S, C, ph, nc, g_ap, "gt")
        if final_g is not None:
            fgt, fgt_r = load_gain(S, C, ph, nc, final_g, "fgt")
        xt = sb("xt", [128, NX, 1024], F32); xt_r = [Res(f"xt{b}") for b in range(NX)]
        hb = [sb(f"hb{i}", [128, 1024], BF16) for i in range(2)]; hb_r = [Res(f"hb{i}") for i in range(2)]
        hT = [sb(f"hT{i}", [128, 8, TM], BF16) for i in range(2)]
        hT_r = [[Res(f"hT{i}_{b}") for b in range(NB)] for i in range(2)]
        aT = sb("aT", [128, NFC, TM], BF16); aT_r = [[Res(f"aT{j}_{s}") for s in range(2)] for j in range(NFC)]
        wdt = sb("wdt", [128, NFC, 1024], BF16); wd_r = [Res(f"wd{g}") for g in range(len(groups))]
        wgt = [sb(f"wgt{i}", [128, 8, 512], BF16) for i in range(2)]; wg_r = [Res(f"wg{i}") for i in range(2)]
        wut = [sb(f"wut{i}", [128, 8, 512], BF16) for i in range(2)]; wu_r = [Res(f"wu{i}") for i in range(2)]
        sg = [sb(f"sg{i}", [128, 512], F32) for i in range(2)]; sg_r = [Res(f"sg{i}") for i in range(2)]
        ost = [sb(f"ost{i}", [128, 1024], F32) for i in range(NO)]; ost_r = [Res(f"ost{i}") for i in range(NO)]
        stt = sb("stt", [128, 16], F32); stt_r = [Res(f"stt{b}") for b in range(16)]
        pg = [pst(f"pg{i}", [128, 512], F32) for i in range(2)]; pg_r = [Res(f"pg{i}") for i in range(2)]
        pu = [pst(f"pu{i}", [128, 512], F32) for i in range(2)]; pu_r = [Res(f"pu{i}") for i in range(2)]
        ptr = C.pbc.bitcast(BF16); ptr_r = C.pbc_r
        py = [pst(f"py{i}", [128, 512], F32) for i in range(2)]; py_r = [Res(f"py{i}") for i in range(2)]

        wg_v = wg.rearrange("(kc p) f -> p kc f", p=128)
        wu_v = wu.rearrange("(kc p) f -> p kc f", p=128)
        wd_v = wd.rearrange("(fc p) d -> p fc d", p=128)

        def issue_gu(gi):
            f0, nf = groups[gi]
            sl = gi % 2
            S.dma("pool", lambda q: q.dma_start(out=wgt[sl][:, :, 0:nf * 128], in_=wg_v[:, :, f0 * 128:(f0 + nf) * 128]), writes=[wg_r[sl]])
            S.dma("pool", lambda q: q.dma_start(out=wut[sl][:, :, 0:nf * 128], in_=wu_v[:, :, f0 * 128:(f0 + nf) * 128]), writes=[wu_r[sl]])

        def issue_wd(gi):
            f0, nf = groups[gi]
            S.dma("pool", lambda q: q.dma_start(out=wdt[:, f0:f0 + nf, :], in_=wd_v[:, f0:f0 + nf, :]), writes=[wd_r[gi]])

        nmt = ntok // TM
        pcnt = 0
        ycnt = 0
        xcnt = [0]

        def stage1_front(mt, b):
            blk = (mt * TM // 128) + b
            xs = xcnt[0] % NX; xcnt[0] += 1
            sl = b % 2
            S.dma("sp", lambda q: q.dma_start(out=xt[:, xs, :], in_=x_src[blk * 128:(blk + 1) * 128, :]), reads=[src_res[blk]], writes=[xt_r[xs]])
            rmsnorm_block(S, C, xt[:, xs, :], xt_r[xs], gt, gt_r, hb[sl][:], hb_r[sl], stt, stt_r[b], b)

        def stage1_back(mt, b):
            sl = b % 2
            hs = mt % 2
            transpose_block(S, C, hb[sl], hb_r[sl], ptr, ptr_r, hT[hs][:, :, b * 128:(b + 1) * 128], hT_r[hs][b])

        def load_res(mt, tb):
            blk = (mt * TM // 128) + tb
            o = blk % NO
            S.dma("sp", lambda q: q.dma_start(out=ost[o][:], in_=x_src[blk * 128:(blk + 1) * 128, :]), reads=[src_res[blk]], writes=[ost_r[o]])

        issue_gu(0); issue_gu(1)
        for b in range(NB):
            stage1_front(0, b)
            stage1_back(0, b)
        for mt in range(nmt):
            t0 = mt * TM
            hs = mt % 2
            if mt > 0:
                pass
            for gi, (f0, nf) in enumerate(groups):
                sl = gi % 2
                for jl in range(nf):
                    j = f0 + jl
                    for s in range(2):
                        p = pcnt % 2; pcnt += 1
                        hrd = hT_r[hs][s * 4:(s + 1) * 4]
                        mm_group(S, pg[p][:], [(wgt[sl][:, kc, jl * 128:(jl + 1) * 128], hT[hs][:, kc, s * 512:(s + 1) * 512]) for kc in range(8)],
                                 reads=hrd + [wg_r[sl]], writes=[pg_r[p]])
                        mm_group(S, pu[p][:], [(wut[sl][:, kc, jl * 128:(jl + 1) * 128], hT[hs][:, kc, s * 512:(s + 1) * 512]) for kc in range(8)],
                                 reads=hrd + [wu_r[sl]], writes=[pu_r[p]])
                        S.op("act", lambda q, p=p: q.activation(out=sg[p][:], in_=pg[p][:], func=AF.Silu), reads=[pg_r[p]], writes=[sg_r[p]])
                        S.op("dve", lambda q, p=p, j=j, s=s: q.tensor_tensor(out=aT[:, j, s * 512:(s + 1) * 512], in0=sg[p][:], in1=pu[p][:], op=ALU.mult),
                             reads=[sg_r[p], pu_r[p]], writes=[aT_r[j][s]])
                issue_wd(gi)
                if gi + 2 < len(groups):
                    issue_gu(gi + 2)
            if mt + 1 < nmt:
                issue_gu(0); issue_gu(1)
            load_res(mt, 0)
            for tb in range(NB):
                blk = (t0 // 128) + tb
                o = blk % NO
                if tb + 1 < NB:
                    load_res(mt, tb + 1)
                if mt + 1 < nmt:
                    stage1_front(mt + 1, tb)
                    if tb >= 1:
                        stage1_back(mt + 1, tb - 1)
                for dh in range(2):
                    y = ycnt % 2; ycnt += 1
                    mm_group(S, py[y][:], [(aT[:, j, tb * 128:(tb + 1) * 128], wdt[:, j, dh * 512:(dh + 1) * 512]) for j in range(NFC)],
                             reads=[aT_r[j][tb // 4] for j in range(NFC)] + wd_r, writes=[py_r[y]])
                    S.op("dve", lambda q, y=y, o=o, dh=dh: q.scalar_tensor_tensor(
                        out=ost[o][:, dh * 512:(dh + 1) * 512], in0=py[y][:], scalar=0.5, in1=ost[o][:, dh * 512:(dh + 1) * 512],
                        op0=ALU.mult, op1=ALU.add), reads=[py_r[y], ost_r[o]], writes=[ost_r[o]])
                if final_g is not None:
                    rmsnorm_block(S, C, ost[o][:], ost_r[o], fgt, fgt_r, ost[o][:], ost_r[o], stt, stt_r[8 + tb], 8 + tb)
                S.dma("sp", lambda q, o=o, blk=blk: q.dma_start(out=x_dst[blk * 128:(blk + 1) * 128, :], in_=ost[o][:]),
                      reads=[ost_r[o]], writes=[dst_res[blk]], sem_res=ost_r[o], is_output=is_output)
            if mt + 1 < nmt:
                stage1_back(mt + 1, NB - 1)
        S.barrier()
        S.flush()


def mixer_ab_phase(S, C, nc, W, x_src, src_res, x_dst, dst_res, seqs):
    with ExitStack() as ph:
        sb = lambda n, sh, dt: ph.enter_context(nc.sbuf_tensor(U(n), sh, dt))
        pst = lambda n, sh, dt: ph.enter_context(nc.psum_tensor(U(n), sh, dt))
        C.pbc = pst("pbc", [128, 1024], F32); C.pbc_r = Res("pbc")
        gt, gt_r = load_gain(S, C, ph, nc, W["l0_mix_norm"], "gt")
        A0 = pst("A0", [128, 512], F32); A0_r = Res("A0")
        A1 = pst("A1", [128, 512], F32); A1_r = Res("A1")
        T0 = A1.bitcast(BF16); T0_r = A1_r
        NST = 3
        ST = [pst(f"ST{i}", [128, 512], F32) for i in range(NST)]; ST_r = [Res(f"ST{i}") for i in range(NST)]
        PO = pst("PO", [128, 512], F32); PO_r = Res("PO")
        PQ = C.pbc; PQ_r = C.pbc_r

        win = sb("win", [128, 8, 1792], BF16); win_r = Res("win")
        win_v = W["l0_w_in"].rearrange("(kc p) f -> p kc f", p=128)
        for h in range(2):
            S.dma("pool", lambda q, h=h: q.dma_start(out=win[:, :, h * 896:(h + 1) * 896], in_=win_v[:, :, h * 896:(h + 1) * 896]),
                  writes=[win_r], sem_res=Res("winl"))
        woA = sb("woA", [128, 4, 1024], BF16); woA_r = Res("woA")
        woB = sb("woB", [64, 8, 1024], BF16); woB_r = Res("woB")
        S.dma("pool", lambda q: q.dma_start(out=woA[:], in_=W["l0_w_out"][0:512, :].rearrange("(c p) d -> p c d", p=128)), writes=[woA_r])
        S.dma("pool", lambda q: q.dma_start(out=woB[:], in_=W["l0_w_out"][512:1024, :].rearrange("(h p) d -> p h d", p=64)), writes=[woB_r])
        S.tag = "wmask"
        wmask = sb("wmask", [128, 2, 512], BF16); wmask_r = Res("wmask")
        S.dma("pool", lambda q: q.dma_start(out=wmask[:], in_=C.wmask_in), writes=[wmask_r])
        S.tag = "rope"
        rope = sb("rope", [128, 32, 32], F32); rope_r = Res("rope")
        S.dma("sp", lambda q: q.dma_start(out=rope[:], in_=C.rope_in.rearrange("(b p) c -> p b c", p=128)), writes=[rope_r])
        S.tag = "tr"
        sw = sb("sw", [128, 4, 128], F32); sw_r = Res("sw")
        S.dma("sp", lambda q: q.dma_start(out=sw[:], in_=W["l0_sgu_w"].rearrange("g i j -> i g j")), writes=[sw_r])
        WT = sb("WT", [128, 4, 128], BF16); WT_r = Res("WT")
        WTf = sb("WTf", [128, 4, 128], F32); WTf_r = Res("WTf")
        def ftr(q):
            ins = None
            for g in range(4):
                ins = q.transpose(out=A0[:, g * 128:(g + 1) * 128], in_=sw[:, g, :], identity=C.identf[:])
            return ins
        S.op("pe", ftr, reads=[sw_r, C.const_r], writes=[A0_r])
        S.op("act", lambda q: q.copy(out=WTf[:], in_=A0[:].rearrange("p (g i) -> p g i", g=4)), reads=[A0_r], writes=[WTf_r])
        S.op("dve", lambda q: q.tensor_copy(out=WT[:], in_=WTf[:]), reads=[WTf_r], writes=[WT_r])
        S.tag = "k2"
        k2l = sb("k2l", [2, 4, 128], F32); k2l_r = Res("k2l")
        k2r = sb("k2r", [2, 4, 128], F32); k2r_r = Res("k2r")
        S.op("dve", lambda q: q.memset(k2l[:], 1.0), writes=[k2l_r])
        S.dma("sp", lambda q: q.dma_start(out=k2l[0:1, :, :], in_=W["l0_sgu_ln_b"].rearrange("(o g c) -> o g c", o=1, g=4)), writes=[k2l_r])
        S.dma("sp", lambda q: q.dma_start(out=k2r[1:2, :, :], in_=W["l0_sgu_b"].unsqueeze(0)), writes=[k2r_r])
        S.op("pe", lambda q: q.matmul(A1[0:1, :], lhsT=C.ones_col[:, 0:1], rhs=WTf[:].rearrange("p g i -> p (g i)"), start=True, stop=True),
             reads=[WTf_r, C.const_r], writes=[A1_r])
        S.op("act", lambda q: q.copy(out=k2r[0:1, :, :], in_=A1[0:1, :].rearrange("p (g i) -> p g i", g=4)), reads=[A1_r], writes=[k2r_r])
        B2 = sb("B2", [128, 4, 128], F32); B2_r = Res("B2")
        def fb2(q):
            ins = None
            for g in range(4):
                ins = q.matmul(A0[:, g * 128:(g + 1) * 128], lhsT=k2l[:, g, :], rhs=k2r[:, g, :], start=True, stop=True)
            return ins
        S.op("pe", fb2, reads=[k2l_r, k2r_r], writes=[A0_r])
        S.op("act", lambda q: q.copy(out=B2[:], in_=A0[:].rearrange("p (g i) -> p g i", g=4)), reads=[A0_r], writes=[B2_r])
        S.tag = "lg"
        lg = sb("lg", [128, 4], F32); lg_r = Res("lg")
        S.dma("sp", lambda q: q.dma_start(out=lg[:], in_=W["l0_sgu_ln_g"].rearrange("(g c) -> c g", g=4), allow_slow_non_contiguous=True), writes=[lg_r])
        S.tag = "es"
        S.dma("sp", lambda q: q.dma_start(out=C.vrow[:, 0:8], in_=W["l0_sink"].unsqueeze(0)), writes=[C.vrow_r])
        S.op("pe", lambda q: q.matmul(A1[:, 0:8], lhsT=C.ones_row[:], rhs=C.vrow[:, 0:8], start=True, stop=True), reads=[C.vrow_r, C.const_r], writes=[A1_r])
        es8 = sb("es8", [128, 8], F32); es8_r = Res("es8")
        S.op("act", lambda q: q.activation(out=es8[:], in_=A1[:, 0:8], func=AF.Exp), reads=[A1_r], writes=[es8_r])
        es = sb("es", [128, 8, 128], F32); es_r = Res("es")
        S.op("dve", lambda q: q.tensor_copy(out=es[:], in_=es8[:].unsqueeze(2).broadcast_to([128, 8, 128])), reads=[es8_r], writes=[es_r])

        S.tag = ""
        NS = 4
        xin = [sb(f"xin{i}", [128, 1024], F32) for i in range(NS)]; xin_r = [Res(f"xin{i}") for i in range(NS)]
        hb = [sb(f"hb{i}", [128, 1024], BF16) for i in range(2)]; hb_r = [Res(f"hb{i}") for i in range(2)]
        hT = [sb(f"hT{i}", [128, 8, 128], BF16) for i in range(2)]; hT_r = [Res(f"hT{i}") for i in range(2)]
        uT = [sb(f"uT{i}", [128, 4, 128], BF16) for i in range(2)]; uT_r = [Res(f"uT{i}") for i in range(2)]
        vh2 = [sb(f"vh{i}", [128, 512], F32) for i in range(2)]; vh2_r = [Res(f"vh{i}") for i in range(2)]
        nrm = sb("nrm", [128, 512], BF16); nrm_r = Res("nrm")
        bst = sb("bst", [128, 8], F32); bst_r = Res("bst")
        tsg = sb("tsg", [128, 4, 128], F32); tsg_r = Res("tsg")
        aoT = [sb(f"aoT{i}", [128, 4, 128], BF16) for i in range(NS)]; aoT_r = [Res(f"aoT{i}") for i in range(NS)]
        qk = sb("qk", [128, 10, 64], BF16); qk_r = Res("qk")
        qkf2 = [sb(f"qkf{i}", [128, 640], F32) for i in range(2)]; qkf2_r = [Res(f"qkf{i}") for i in range(2)]
        rt = sb("rt", [128, 2, 10, 16], F32); rt_r = Res("rt")
        qT = [sb(f"qT{i}", [64, 8, 128], BF16) for i in range(NS)]; qT_r = [Res(f"qT{i}") for i in range(NS)]
        kT = sb("kT", [64, 2, 4096], BF16); kT_r = [Res(f"kT{b}") for b in range(32)]
        vp = sb("vp", [128, 32, 2, 128], BF16); vp_r = [Res(f"vp{b}") for b in range(32)]
        PT = [sb(f"PT{i}", [128, 512], BF16) for i in range(NST)]; PT_r = [Res(f"PT{i}") for i in range(NST)]
        den = sb("den", [128, 512], F32); den_r = Res("den")
        rec = sb("rec", [64, 512], F32); rec_r = Res("rec")
        boT = [sb(f"boT{i}", [64, 8, 128], BF16) for i in range(2)]; boT_r = [Res(f"boT{i}") for i in range(2)]
        ost = [sb(f"ost{i}", [128, 1024], F32) for i in range(2)]; ost_r = [Res(f"ost{i}") for i in range(2)]
        stt = sb("stt", [128, 4], F32); stt_r = [Res(f"stt{i}") for i in range(4)]
        S.tag = "vp"
        S.op("pool", lambda q: q.memset(vp[:], 1.0), writes=vp_r)
        S.tag = ""

        pcnt = [0]
        blk0 = 0
        for L in seqs:
            nb = L // 128

            def stage_a(i, blk0=blk0):
              blk = blk0 + i
              s3 = i % NS; s2 = i % 2
              h_ = hT[s2]
              vh = vh2[s2]; vh_r = vh2_r[s2]
              qkf = qkf2[s2]; qkf_r = qkf2_r[s2]
              pieces = []
              late = []

              def piece(f):
                  pieces.append(f)
                  return f

              def latep(f):
                  late.append(f)
                  return f

              @piece
              def _p0():
                S.dma("sp", lambda q: q.dma_start(out=xin[s3][:], in_=x_src[blk * 128:(blk + 1) * 128, :]), reads=[src_res[blk]], writes=[xin_r[s3]])
                rmsnorm_block(S, C, xin[s3][:], xin_r[s3], gt, gt_r, hb[s2][:], hb_r[s2], stt, stt_r[s2], s2)
                transpose_block(S, C, hb[s2], hb_r[s2], T0, T0_r, hT[s2][:], hT_r[s2])

              @piece
              def _p1():
                pass
                S.tag = "u"
                def fu(q):
                    ins = None
                    for c in range(4):
                        for kc in range(8):
                            ins = q.matmul(A0[:, c * 128:(c + 1) * 128], lhsT=win[:, kc, c * 128:(c + 1) * 128], rhs=h_[:, kc, :], start=(kc == 0), stop=(kc == 7))
                    return ins
                S.op("pe", fu, reads=[hT_r[s2], win_r], writes=[A0_r])
                S.op("act", lambda q: q.activation(out=uT[s2][:], in_=A0[:].rearrange("p (c t) -> p c t", c=4), func=AF.Gelu), reads=[A0_r], writes=[uT_r[s2]])
              @piece
              def _p2():
                S.tag = "v"
                mm_group(S, A1[:], [(h_[:, kc, :], win[:, kc, 512:1024]) for kc in range(8)], reads=[hT_r[s2], win_r], writes=[A1_r])
                S.op("act", lambda q: q.activation(out=vh[:], in_=A1[:], func=AF.Gelu), reads=[A1_r], writes=[vh_r])
              @latep
              def _p2b():
                S.tag = "v"
                S.op("dve", lambda q: q.bn_stats(out=bst[:, 0:6], in_=vh[:]), reads=[vh_r], writes=[bst_r])
                S.op("dve", lambda q: q.bn_aggr(out=bst[:, 6:8], in_=bst[:, 0:6]), reads=[bst_r], writes=[bst_r])
                S.op("act", lambda q: q.activation(out=bst[:, 7:8], in_=bst[:, 7:8], func=AF.Ln, bias=C.eps[:, 0:1], scale=1.0), reads=[bst_r, C.const_r], writes=[bst_r])
                S.op("act", lambda q: q.activation(out=bst[:, 7:8], in_=bst[:, 7:8], func=AF.Exp, scale=-0.5), reads=[bst_r], writes=[bst_r])
                S.op("dve", lambda q: q.scalar_tensor_tensor(out=bst[:, 6:7], in0=bst[:, 6:7], scalar=-1.0, in1=bst[:, 7:8], op0=ALU.mult, op1=ALU.mult), reads=[bst_r], writes=[bst_r])
                S.op("act", lambda q: q.activation(out=nrm[:], in_=vh[:], func=AF.Identity, bias=bst[:, 6:7], scale=bst[:, 7:8]), reads=[vh_r, bst_r], writes=[nrm_r])
              @piece
              def _p3():
                S.tag = "qkv"
                mm_group(S, PQ[:, 0:512], [(h_[:, kc, :], win[:, kc, 1024:1536]) for kc in range(8)], reads=[hT_r[s2], win_r], writes=[PQ_r])
                mm_group(S, PQ[:, 512:768], [(h_[:, kc, :], win[:, kc, 1536:1792]) for kc in range(8)], reads=[hT_r[s2], win_r], writes=[PQ_r])
                S.op("act", lambda q: q.copy(out=qkf[:], in_=PQ[:, 0:640]), reads=[PQ_r], writes=[qkf_r])
                S.op("act", lambda q: q.copy(out=vp[:, i, :, 0:64], in_=PQ[:, 640:768].rearrange("p (g d) -> p g d", g=2)), reads=[PQ_r], writes=[vp_r[i]])
              @latep
              def _p3b():
                qk_ps = qkf[:].rearrange("p (h d) -> p h d", h=10)
                cc = rope[:, i, 0:16].unsqueeze(1).broadcast_to([128, 10, 16])
                sn = rope[:, i, 16:32]
                S.tag = "c1"
                S.op("act", lambda q: q.copy(out=qk[:, :, 16:64], in_=qk_ps[:, :, 16:64]), reads=[qkf_r], writes=[qk_r])
                S.tag = "r1"
                S.op("dve", lambda q: q.tensor_tensor(out=rt[:, 0, :, :], in0=qk_ps[:, :, 0:16], in1=cc, op=ALU.mult), reads=[qkf_r, rope_r], writes=[rt_r])
                S.tag = "r2"
                S.op("dve", lambda q: q.tensor_tensor(out=rt[:, 1, :, 0:8], in0=qk_ps[:, :, 8:16], in1=sn[:, 0:8].unsqueeze(1).broadcast_to([128, 10, 8]), op=ALU.mult),
                     reads=[qkf_r, rope_r], writes=[rt_r])
                S.op("dve", lambda q: q.tensor_tensor(out=rt[:, 1, :, 8:16], in0=qk_ps[:, :, 0:8], in1=sn[:, 8:16].unsqueeze(1).broadcast_to([128, 10, 8]), op=ALU.mult),
                     reads=[qkf_r, rope_r], writes=[rt_r])
                S.op("dve", lambda q: q.tensor_tensor(out=qk[:, :, 0:16], in0=rt[:, 0, :, :], in1=rt[:, 1, :, :], op=ALU.add), reads=[rt_r], writes=[qk_r])
              @latep
              def _p4():
                S.tag = "sg"
                def fs(q):
                    ins = None
                    for g in range(4):
                        ins = q.matmul(A1[:, g * 128:(g + 1) * 128], lhsT=nrm[:, g * 128:(g + 1) * 128], rhs=WT[:, g, :], start=True, stop=True)
                    return ins
                S.op("pe", fs, reads=[nrm_r, WT_r], writes=[A1_r])
                for g in range(4):
                    S.op("dve", lambda q, g=g: q.scalar_tensor_tensor(out=tsg[:, g, :], in0=A1[:, g * 128:(g + 1) * 128], scalar=lg[:, g:g + 1], in1=B2[:, g, :],
                                                                     op0=ALU.mult, op1=ALU.add), reads=[A1_r, lg_r, B2_r], writes=[tsg_r])
                S.op("dve", lambda q: q.tensor_tensor(out=aoT[s3][:], in0=tsg[:], in1=uT[s2][:], op=ALU.mult), reads=[tsg_r, uT_r[s2]], writes=[aoT_r[s3]])
              @latep
              def _p5():
                S.tag = "tq"
                def ftq(q):
                    ins = None
                    for h in range(8):
                        ins = q.transpose(out=T0[0:64, h * 128:(h + 1) * 128], in_=qk[:, h, :], identity=C.ident[:])
                    return ins
                S.op("pe", ftq, reads=[qk_r, C.const_r], writes=[T0_r])
                S.op("dve", lambda q: q.tensor_copy(out=qT[s3][:], in_=T0[0:64, :].rearrange("p (h t) -> p h t", h=8)), reads=[T0_r], writes=[qT_r[s3]])
                def ftk(q):
                    ins = None
                    for g in range(2):
                        ins = q.transpose(out=T0[0:64, g * 128:(g + 1) * 128], in_=qk[:, 8 + g, :], identity=C.ident[:])
                    return ins
                S.op("pe", ftk, reads=[qk_r, C.const_r], writes=[T0_r])
                S.op("dve", lambda q: q.tensor_copy(out=kT[:, :, i * 128:(i + 1) * 128], in_=T0[0:64, 0:256].rearrange("p (g t) -> p g t", g=2)),
                     reads=[T0_r], writes=[kT_r[i]])

              pu_, pv2_ = pieces[1], pieces[2]
              pieces[1:3] = [lambda: (pu_(), pv2_())]
              return pieces, late

            def stage_b(m, fillers, blk0=blk0, nb=nb):
                S.tag = ""
                blk = blk0 + m
                s3 = m % NS; s2 = m % 2
                kbs = [kb for kb in (m, m - 1, m + 1) if 0 <= kb < nb]
                tasks = [(g, n_, kb) for g in range(2) for n_, kb in enumerate(kbs)]

                def front(t):
                    g, n_, kb = t
                    p = pcnt[0] % NST; pcnt[0] += 1
                    S.op("pe", lambda q: q.matmul(ST[p][:], lhsT=kT[:, g, kb * 128:(kb + 1) * 128],
                                                  rhs=qT[s3][:, 4 * g:4 * g + 4, :], start=True, stop=True),
                         reads=[kT_r[kb], qT_r[s3]], writes=[ST_r[p]])
                    S.op("act", lambda q: q.activation(out=PT[p][:], in_=ST[p][:], func=AF.Exp, scale=0.125), reads=[ST_r[p]], writes=[PT_r[p]])
                    if kb != m:
                        mi = 0 if kb < m else 1
                        S.op("dve", lambda q: q.tensor_tensor(out=PT[p][:], in0=PT[p][:], in1=wmask[:, mi, :], op=ALU.mult),
                             reads=[PT_r[p], wmask_r], writes=[PT_r[p]])
                    return p

                def back(t, p):
                    g, n_, kb = t
                    S.op("pe", lambda q: q.matmul(PO[:], lhsT=vp[:, kb, g, :], rhs=PT[p][:], start=(n_ == 0), stop=(n_ == len(kbs) - 1)),
                         reads=[vp_r[kb], PT_r[p]], writes=[PO_r])
                    if n_ == len(kbs) - 1:
                        S.op("dve", lambda q: q.tensor_tensor(out=den[64:128, :].rearrange("p (h t) -> p h t", h=4), in0=PO[64:128, :].rearrange("p (h t) -> p h t", h=4),
                                                              in1=es[64:128, 4 * g:4 * g + 4, :], op=ALU.add),
                             reads=[PO_r, es_r], writes=[den_r])
                        S.op("act", lambda q: q.activation(out=rec[:], in_=den[64:128, :], func=AF.Ln), reads=[den_r], writes=[rec_r])
                        S.op("act", lambda q: q.activation(out=rec[:], in_=rec[:], func=AF.Exp, scale=-1.0), reads=[rec_r], writes=[rec_r])
                        S.op("dve", lambda q: q.tensor_tensor(out=boT[s2][:, 4 * g:4 * g + 4, :], in0=PO[0:64, :].rearrange("p (h t) -> p h t", h=4),
                                                              in1=rec[:].rearrange("p (h t) -> p h t", h=4), op=ALU.mult),
                             reads=[PO_r, rec_r], writes=[boT_r[s2]])

                ps = {}
                DEP = NST - 1
                for k in range(len(tasks) + DEP):
                    if fillers:
                        fillers.pop(0)()
                    if k < len(tasks):
                        ps[k] = front(tasks[k])
                    if k >= DEP:
                        back(tasks[k - DEP], ps[k - DEP])
                while fillers:
                    fillers.pop(0)()
                S.tag = ""
                o = m % 2
                for dh, (PY, PY_r) in enumerate(((A0, A0_r), (A1, A1_r))):
                    pairs = [(aoT[s3][:, c, :], woA[:, c, dh * 512:(dh + 1) * 512]) for c in range(4)]
                    pairs += [(boT[s2][:, h, :], woB[:, h, dh * 512:(dh + 1) * 512]) for h in range(8)]
                    mm_group(S, PY[:], pairs, reads=[aoT_r[s3], boT_r[s2], woA_r, woB_r], writes=[PY_r])
                    S.op("dve", lambda q, dh=dh, PY=PY: q.tensor_tensor(out=ost[o][:, dh * 512:(dh + 1) * 512], in0=PY[:], in1=xin[s3][:, dh * 512:(dh + 1) * 512], op=ALU.add),
                         reads=[PY_r, xin_r[s3]], writes=[ost_r[o]])
                S.dma("sp", lambda q: q.dma_start(out=x_dst[blk * 128:(blk + 1) * 128, :], in_=ost[o][:]), reads=[ost_r[o]], writes=[dst_res[blk]], sem_res=ost_r[o])

            dbg = 0
            late_prev = []
            for it in range(nb + 3):
                S.tag = ""
                if dbg == 1:
                    break
                early, late_new = stage_a(it) if it < nb else ([], [])
                pieces = []
                while early or late_prev:
                    if early:
                        pieces.append(early.pop(0))
                    if late_prev:
                        pieces.append(late_prev.pop(0))
                late_prev = late_new
                if it >= 3 and dbg != 2:
                    stage_b(it - 3, pieces)
                else:
                    while pieces:
                        pieces.pop(0)()
            blk0 += nb
        S.barrier()
        S.flush()


def mixer_c_phase(S, C, nc, W, x_src, src_res, x_dst, dst_res, seqs):
    with ExitStack() as ph:
        sb = lambda n, sh, dt: ph.enter_context(nc.sbuf_tensor(U(n), sh, dt))
        pst = lambda n, sh, dt: ph.enter_context(nc.psum_tensor(U(n), sh, dt))
        C.pbc = pst("pbc", [128, 1024], F32); C.pbc_r = Res("pbc")
        gt, gt_r = load_gain(S, C, ph, nc, W["l1_mix_norm"], "gt")
        P0 = C.pbc[:, 0:512]; P1 = C.pbc[:, 512:1024]
        P_r = [Res("P0"), Res("P1")]
        T0 = C.pbc.bitcast(BF16); T0_r = P_r[0]
        NST = 4
        ST = [pst(f"ST{i}", [128, 512], F32) for i in range(NST)]; ST_r = [Res(f"ST{i}") for i in range(NST)]
        PO = [pst(f"PO{i}", [128, 512], F32) for i in range(2)]; PO_r = [Res(f"PO{i}") for i in range(2)]

        wq = sb("wq", [128, 8, 3072], BF16); wq_r = Res("wq")
        wq_v = W["l1_w_qkv"].rearrange("(kc p) f -> p kc f", p=128)
        for h in range(3):
            S.dma("pool", lambda q, h=h: q.dma_start(out=wq[:, :, h * 1024:(h + 1) * 1024], in_=wq_v[:, :, h * 1024:(h + 1) * 1024]),
                  writes=[wq_r], sem_res=Res("wql"))
        woC = sb("woC", [128, 8, 1024], BF16); woC_r = Res("woC")
        S.dma("pool", lambda q: q.dma_start(out=woC[:], in_=W["l1_w_out"].rearrange("(c p) d -> p c d", p=128)), writes=[woC_r])
        E = sb("E", [128, 7, 16, 128], BF16); E_r = Res("E")
        if True:
            sb2 = sb
            rp = sb2("rp", [16, 15, 31], F32); rp_r = Res("rp")
            rr = sb2("rr", [16, 15, 31], F32); rr_r = Res("rr")
            zt = sb2("zt", [1, 64], F32); zt_r = Res("zt")
            cm = sb2("cm", [128, 128], F32); cm_r = Res("cm")
            est = sb2("est", [128, 16, 128], F32); est_r = Res("est")
            rsc_r = Res("rsc"); gsc_r = Res("gsc")
            S.dma("sp", lambda q: q.dma_start(out=rp[:], in_=W["l1_rpb"]), writes=[rp_r])
            S.dma("sp", lambda q: q.dma_start(out=cm[:], in_=C.cmask_in), writes=[cm_r])
            S.op("dve", lambda q: q.memset(zt[:], 0.0), writes=[zt_r])
            for t in range(31):
                S.op("dve" if t % 2 else "act",
                     (lambda t: (lambda q: q.tensor_copy(out=rr[:, :, 30 - t:31 - t], in_=rp[:, :, t:t + 1])) if t % 2 else
                      (lambda q: q.copy(out=rr[:, :, 30 - t:31 - t], in_=rp[:, :, t:t + 1])))(t),
                     reads=[rp_r], writes=[rr_r])
            rsc = C.rsc
            S.dma("sp", lambda q: q.dma_start(out=rsc[0:64].unsqueeze(0), in_=zt[:]), reads=[zt_r], writes=[rsc_r], sem_res=rsc_r)
            S.dma("sp", lambda q: q.dma_start(out=rsc[64 + 7440:128 + 7440].unsqueeze(0), in_=zt[:]), reads=[zt_r], writes=[rsc_r], sem_res=rsc_r)
            S.dma("sp", lambda q: q.dma_start(out=rsc[64:64 + 7440].rearrange("(h x) -> h x", h=16), in_=rr[:].rearrange("p a b -> p (a b)")),
                  reads=[rr_r], writes=[rsc_r], sem_res=rsc_r)
            gsc = C.gsc
            for kc in range(64):
                src = bass.AP(C.rsc_t, 64 + 15 - kc, [[31, 240], [1, 64]])
                S.dma("sp", lambda q, kc=kc, src=src: q.dma_start(out=gsc[:, kc, :], in_=src), reads=[rsc_r], writes=[gsc_r], sem_res=gsc_r, nowaw=(kc % 8 != 0))
            g4 = gsc.rearrange("(h r) k c -> h r k c", h=16)
            for di, dl in enumerate(range(-3, 4)):
                S.op("dve", lambda q: q.memset(est[:], 0.0), writes=[est_r])
                for kr in range(2):
                    for qr in range(2):
                        dr = 2 * dl + kr - qr + 7
                        if 0 <= dr <= 14:
                            S.dma("sp", lambda q, kr=kr, qr=qr, dr=dr: q.dma_start(
                                out=est[kr * 64:(kr + 1) * 64, :, qr * 64:(qr + 1) * 64], in_=g4[:, dr, :, :].rearrange("h k c -> k h c")),
                                reads=[gsc_r], writes=[est_r], nowaw=(kr + qr > 0))
                S.op("act", lambda q: q.activation(out=est[:], in_=est[:], func=AF.Exp), reads=[est_r], writes=[est_r])
                S.op("dve", lambda q, di=di: q.tensor_tensor(out=E[:, di, :, :], in0=est[:], in1=cm[:].unsqueeze(1).broadcast_to([128, 16, 128]), op=ALU.mult),
                     reads=[est_r, cm_r], writes=[E_r])

        NQ = 4
        NR = 7
        xin = [sb(f"xin{i}", [128, 1024], F32) for i in range(NQ)]; xin_r = [Res(f"xin{i}") for i in range(NQ)]
        hb = [sb("hb0", [128, 1024], BF16)] * 2; hb_r = [Res("hb0")] * 2
        hT = [sb("hT0", [128, 8, 128], BF16)] * 2; hT_r = [Res("hT0")] * 2
        qT = [sb(f"qT{i}", [128, 8, 128], BF16) for i in range(NQ)]; qT_r = [Res(f"qT{i}") for i in range(NQ)]
        kT = sb("kT", [128, 8, NR * 128], BF16); kT_r = [Res(f"kT{i}") for i in range(NR)]
        vp = sb("vp", [128, NR, 16, 128], BF16); vp_r = [Res(f"vp{i}") for i in range(NR)]
        PT = [sb(f"PT{i}", [128, 512], BF16) for i in range(NST)]; PT_r = [Res(f"PT{i}") for i in range(NST)]
        rec = sb("rec", [128, 512], F32); rec_r = Res("rec")
        ocT = [sb("ocT0", [128, 8, 128], BF16)] * 2; ocT_r = [Res("ocT0")] * 2
        ost = [sb(f"ost{i}", [128, 1024], F32) for i in range(2)]; ost_r = [Res(f"ost{i}") for i in range(2)]
        stt = sb("stt", [128, 4], F32); stt_r = [Res(f"stt{i}") for i in range(4)]
        S.op("pool", lambda q: q.memset(vp[:], 1.0), writes=vp_r)
        zb = sb("zb", [128, 128], BF16); zb_r = Res("zb")
        S.op("dve", lambda q: q.memset(zb[:], 0.0), writes=[zb_r])

        pc = [0]
        sc = [0]
        oc = [0]
        blk0 = 0
        for L in seqs:
            nb = L // 128
            R = L // 64

            def stage_a(i, blk0=blk0):
                blk = blk0 + i
                sq = i % NQ; s2 = i % 2; sr = i % NR
                h_ = hT[s2]
                pieces = []

                def p0():
                    S.dma("sp", lambda q: q.dma_start(out=xin[sq][:], in_=x_src[blk * 128:(blk + 1) * 128, :]), reads=[src_res[blk]], writes=[xin_r[sq]])
                    rmsnorm_block(S, C, xin[sq][:], xin_r[sq], gt, gt_r, hb[s2][:], hb_r[s2], stt, stt_r[s2], s2)
                    transpose_block(S, C, hb[s2], hb_r[s2], T0, T0_r, hT[s2][:], hT_r[s2])
                pieces.append(p0)

                def pq(bq):
                    p = pc[0] % 2; pc[0] += 1
                    Pb = P0 if p == 0 else P1
                    def fq(q):
                        ins = None
                        for c4 in range(4):
                            ch = bq * 4 + c4
                            for kc in range(8):
                                ins = q.matmul(Pb[:, c4 * 128:(c4 + 1) * 128], lhsT=wq[:, kc, ch * 128:(ch + 1) * 128], rhs=h_[:, kc, :], start=(kc == 0), stop=(kc == 7))
                        return ins
                    S.op("pe", fq, reads=[hT_r[s2], wq_r], writes=[P_r[p]])
                    src = Pb.rearrange("p (c t) -> p c t", c=4)
                    if bq < 2:
                        S.op("act", lambda q: q.copy(out=qT[sq][:, bq * 4:(bq + 1) * 4, :], in_=src), reads=[P_r[p]], writes=[qT_r[sq]])
                    else:
                        S.op("act", lambda q: q.copy(out=kT[:, (bq - 2) * 4:(bq - 1) * 4, sr * 128:(sr + 1) * 128], in_=src), reads=[P_r[p]], writes=[kT_r[sr]])
                for bq in range(4):
                    pieces.append(lambda bq=bq: pq(bq))

                def pv_(dh):
                    p = pc[0] % 2; pc[0] += 1
                    Pb = P0 if p == 0 else P1
                    mm_group(S, Pb, [(h_[:, kc, :], wq[:, kc, 2048 + dh * 512:2048 + (dh + 1) * 512]) for kc in range(8)], reads=[hT_r[s2], wq_r], writes=[P_r[p]])
                    pv3 = Pb.rearrange("p (h d) -> p h d", h=8)
                    S.op("act", lambda q: q.copy(out=vp[:, sr, dh * 8:(dh + 1) * 8:2, 0:64], in_=pv3[:, 0:8:2, :]), reads=[P_r[p]], writes=[vp_r[sr]])
                    S.op("act", lambda q: q.copy(out=vp[:, sr, dh * 8 + 1:(dh + 1) * 8:2, 64:128], in_=pv3[:, 1:8:2, :]), reads=[P_r[p]], writes=[vp_r[sr]])
                for dh in range(2):
                    pieces.append(lambda dh=dh: pv_(dh))
                return pieces

            def make_plan(m, R=R):
                rs = [min(max(2 * m + qr - 4, 0), R - 8) for qr in range(2)]
                js = sorted({(rs[qr] + a) // 2 for qr in range(2) for a in range(8)})
                js = [m] + [j for j in js if j != m]
                plan = []
                for j in js:
                    v = [[rs[qr] <= 2 * j + kr < rs[qr] + 8 for qr in range(2)] for kr in range(2)]
                    rects = []
                    if all(v[kr][qr] for kr in range(2) for qr in range(2)):
                        rects.append((0, 128, 0, 128))
                    else:
                        for qr in range(2):
                            ks = [kr for kr in range(2) if v[kr][qr]]
                            if len(ks) == 2:
                                rects.append((0, 128, qr * 64, 64))
                            elif len(ks) == 1:
                                rects.append((ks[0] * 64, 64, qr * 64, 64))
                    plan.append((j, rects))
                assert plan[0][1] == [(0, 128, 0, 128)]
                return plan

            def stage_b(m, fillers, blk0=blk0, nb=nb, R=R):
                blk = blk0 + m
                sq = m % NQ
                o2 = oc[0] % 2; oc[0] += 1
                plan = make_plan(m)
                nj = len(plan)
                tasks = [(hq, jn) for hq in range(4) for jn in range(nj)]

                def front(t):
                    hq, jn = t
                    j, rects = plan[jn]
                    sr = j % NR
                    di = j - m + 3
                    p = sc[0] % NST; sc[0] += 1
                    def fsc(q):
                        ins = None
                        for hh in range(4):
                            h = 2 * (4 * (hq // 2) + hh) + (hq % 2)
                            pb = (h % 2) * 64
                            ins = q.matmul(ST[p][:, hh * 128:(hh + 1) * 128], lhsT=kT[pb:pb + 64, h // 2, sr * 128:(sr + 1) * 128],
                                           rhs=qT[sq][pb:pb + 64, h // 2, :], start=True, stop=True)
                        return ins
                    S.op("pe", fsc, reads=[kT_r[sr], qT_r[sq]], writes=[ST_r[p]])
                    S.op("act", lambda q: q.activation(out=PT[p][:], in_=ST[p][:], func=AF.Exp, scale=0.125), reads=[ST_r[p]], writes=[PT_r[p]])
                    S.op("dve", lambda q: q.tensor_tensor(out=PT[p][:].rearrange("p (h t) -> p h t", h=4), in0=PT[p][:].rearrange("p (h t) -> p h t", h=4),
                                                          in1=E[:, di, 8 * (hq // 2) + (hq % 2):8 * (hq // 2) + 8:2, :], op=ALU.mult),
                         reads=[PT_r[p], E_r], writes=[PT_r[p]])
                    return p

                def back(t, p):
                    hq, jn = t
                    j, rects = plan[jn]
                    sr = j % NR
                    po = hq % 2
                    if jn == 0:
                        S.op("pe", lambda q: q.matmul(PO[po][:], lhsT=zb[:], rhs=E[:, 3, 0:4, :], start=True, stop=False),
                             reads=[zb_r, E_r], writes=[PO_r[po]])
                    def fpv(q):
                        ins = None
                        for hh in range(4):
                            h = 2 * (4 * (hq // 2) + hh) + (hq % 2)
                            for ri, (k0, kn, c0, cn) in enumerate(rects):
                                ins = q.matmul(PO[po][:, hh * 128 + c0:hh * 128 + c0 + cn], lhsT=vp[k0:k0 + kn, sr, h, :],
                                               rhs=PT[p][k0:k0 + kn, hh * 128 + c0:hh * 128 + c0 + cn],
                                               start=False, stop=(jn == nj - 1 and ri == len(rects) - 1))
                        return ins
                    S.op("pe", fpv, reads=[vp_r[sr], PT_r[p]], writes=[PO_r[po]])
                    if jn == nj - 1:
                        if hq % 2 == 0:
                            o_lo, d_lo = 0, 64
                        else:
                            o_lo, d_lo = 64, 0
                        S.op("act", lambda q: q.activation(out=rec[o_lo:o_lo + 64, :], in_=PO[po][d_lo:d_lo + 64, :], func=AF.Ln), reads=[PO_r[po]], writes=[rec_r])
                        S.op("act", lambda q: q.activation(out=rec[o_lo:o_lo + 64, :], in_=rec[o_lo:o_lo + 64, :], func=AF.Exp, scale=-1.0), reads=[rec_r], writes=[rec_r])
                        S.op("dve", lambda q: q.tensor_tensor(out=ocT[o2][o_lo:o_lo + 64, 4 * (hq // 2):4 * (hq // 2) + 4, :],
                                                              in0=PO[po][o_lo:o_lo + 64, :].rearrange("p (h t) -> p h t", h=4),
                                                              in1=rec[o_lo:o_lo + 64, :].rearrange("p (h t) -> p h t", h=4), op=ALU.mult),
                             reads=[PO_r[po], rec_r], writes=[ocT_r[o2]])

                DEP = NST - 1
                ps = {}
                for k in range(len(tasks) + DEP):
                    if fillers and k % 3 == 1:
                        fillers.pop(0)()
                    if k < len(tasks):
                        ps[k] = front(tasks[k])
                    if k >= DEP:
                        back(tasks[k - DEP], ps[k - DEP])
                while fillers:
                    fillers.pop(0)()
                S.tag = ""
                for dh in range(2):
                    p = pc[0] % 2; pc[0] += 1
                    Pb = P0 if p == 0 else P1
                    mm_group(S, Pb, [(ocT[o2][:, h, :], woC[:, h, dh * 512:(dh + 1) * 512]) for h in range(8)], reads=[ocT_r[o2], woC_r], writes=[P_r[p]])
                    S.op("dve", lambda q, dh=dh, Pb=Pb: q.tensor_tensor(out=ost[o2][:, dh * 512:(dh + 1) * 512], in0=Pb, in1=xin[sq][:, dh * 512:(dh + 1) * 512], op=ALU.add),
                         reads=[P_r[p], xin_r[sq]], writes=[ost_r[o2]])
                S.dma("sp", lambda q: q.dma_start(out=x_dst[blk * 128:(blk + 1) * 128, :], in_=ost[o2][:]), reads=[ost_r[o2]], writes=[dst_res[blk]], sem_res=ost_r[o2])

            dbg = 0
            for it in range(nb + 3):
                if dbg == 1:
                    break
                pieces = stage_a(it) if it < nb else []
                if it >= 3 and dbg != 2:
                    m = it - 3
                    if max(j for j, _ in make_plan(m)) >= it:
                        while pieces:
                            pieces.pop(0)()
                    stage_b(m, pieces)
                else:
                    while pieces:
                        pieces.pop(0)()
            blk0 += nb
        S.barrier()
        S.flush()


def build_nc(seqs=SEQS, nphase=6, only=None):
    nc = bass.Bass("TRN2", target_bir_lowering=False)
    print("sbuf bytes remaining at start", nc.sbuf_bytes_remaining)
    ntok = sum(seqs)
    nblk = ntok // 128
    x_in = nc.dram_tensor("x", [ntok, D], F32, kind="ExternalInput").ap()
    W = {n: nc.dram_tensor(n, list(s), F32, kind="ExternalInput").ap() for n, s in WSPECS}
    C = Ctx()
    ident_in = nc.dram_tensor("c_ident", [128, 128], F32, kind="ExternalInput").ap()
    C.rope_in = nc.dram_tensor("c_rope", [4096, 32], F32, kind="ExternalInput").ap()
    C.wmask_in = nc.dram_tensor("c_wmask", [128, 2, 512], F32, kind="ExternalInput").ap()
    C.cmask_in = nc.dram_tensor("c_cmask", [128, 128], F32, kind="ExternalInput").ap()
    y_out = nc.dram_tensor("y", [ntok, D], F32, kind="ExternalOutput").ap()
    xa = nc.dram_tensor("xa", [ntok, D], F32).ap()
    xb = nc.dram_tensor("xb", [ntok, D], F32).ap()
    C.rsc_t = nc.dram_tensor("rsc", [7440 + 128], F32)
    C.rsc = C.rsc_t.ap()
    C.gsc = nc.dram_tensor("gsc", [240, 64, 64], F32).ap()
    in_r = [Res(f"in{b}") for b in range(nblk)]
    xa_r = [Res(f"xa{b}") for b in range(nblk)]
    xb_r = [Res(f"xb{b}") for b in range(nblk)]
    y_r = [Res(f"y{b}") for b in range(nblk)]
    with ExitStack() as st:
        S = Sched(nc, st)
        sb = lambda n, sh, dt: st.enter_context(nc.sbuf_tensor(U(n), sh, dt))
        C.ident = sb("ident", [128, 128], BF16)
        C.identf = sb("identf", [128, 128], F32)
        C.ones_row = sb("ones_row", [1, 128], F32)
        C.ones_col = sb("ones_col", [128, 1], F32)
        C.eps = sb("eps", [128, 1], F32)
        C.vrow = sb("vrow", [1, 1024], F32); C.vrow_r = Res("vrow")
        C.junk = sb("junk", [128, 1024], BF16)
        C.const_r = Res("const")
        cl = Res("constload")
        S.dma("pool", lambda q: q.dma_start(out=C.ident[:], in_=ident_in), writes=[C.const_r], sem_res=cl)
        S.dma("sp", lambda q: q.dma_start(out=C.identf[:], in_=ident_in), writes=[C.const_r], sem_res=cl)
        S.op("dve", lambda q: q.memset(C.ones_row[:], 1.0), writes=[C.const_r])
        S.op("dve", lambda q: q.memset(C.ones_col[:], 1.0), writes=[C.const_r])
        S.op("dve", lambda q: q.memset(C.eps[:], EPS), writes=[C.const_r])
        S.barrier()

        def ffn(src, src_r, dst, dst_r, pre, final=False):
            ffn_phase(S, C, nc, src, src_r, dst, dst_r, W[pre + "_norm"], W[pre + "_w_gate"], W[pre + "_w_up"], W[pre + "_w_down"], ntok,
                      final_g=(W["final_norm"] if final else None), is_output=(dst is y_out))

        phases = [
            lambda s, sr, d, dr: ffn(s, sr, d, dr, "l0_ffn1"),
            lambda s, sr, d, dr: mixer_ab_phase(S, C, nc, W, s, sr, d, dr, seqs),
            lambda s, sr, d, dr: ffn(s, sr, d, dr, "l0_ffn2"),
            lambda s, sr, d, dr: ffn(s, sr, d, dr, "l1_ffn1"),
            lambda s, sr, d, dr: mixer_c_phase(S, C, nc, W, s, sr, d, dr, seqs),
            lambda s, sr, d, dr: ffn(s, sr, d, dr, "l1_ffn2", final=(nphase == 6)),
        ]
        cur, cur_r = x_in, in_r
        scr = [(xa, xa_r), (xb, xb_r)]
        for pi in range(nphase):
            if only is not None and pi != only:
                continue
            if pi == nphase - 1:
                dst, dst_r = y_out, y_r
            else:
                dst, dst_r = scr[pi % 2]
            phases[pi](cur, cur_r, dst, dst_r)
            cur, cur_r = dst, dst_r
        S.flush(final=True)
    return nc


def host_consts():
    ident = np.eye(128, dtype=np.float32)
    pos = np.arange(4096, dtype=np.float64)[:, None]
    inv = np.power(500000.0, -np.arange(0, 16, 2, dtype=np.float64) / 16.0)[None, :]
    ang = (pos.astype(np.float32) * inv.astype(np.float32)).astype(np.float32)
    cos = np.cos(ang).astype(np.float32); sin = np.sin(ang).astype(np.float32)
    rope = np.concatenate([cos, cos, -sin, sin], axis=1).astype(np.float32)
    j = np.arange(128)[:, None]; i = np.arange(128)[None, :]
    m0 = (j >= i).astype(np.float32); m1 = (j <= i).astype(np.float32)
    wmask = np.stack([np.tile(m0, (1, 4)), np.tile(m1, (1, 4))], axis=1).astype(np.float32)
    qc = np.arange(64)
    cs = np.clip(qc - 8, 0, 48)
    kc = np.arange(64)[:, None]
    cm64 = ((kc >= cs[None, :]) & (kc < cs[None, :] + 16)).astype(np.float32)
    cmask = np.tile(cm64, (2, 2)).astype(np.float32)
    return {"c_ident": ident, "c_rope": rope, "c_wmask": wmask, "c_cmask": cmask}


_NC_CACHE = {}


def kernel(**inputs):
    n = 8
    xp = np.asarray(inputs["x_prompt"], dtype=np.float32)
    xs = np.asarray(inputs["x_sample"], dtype=np.float32)
    if "nc" not in _NC_CACHE:
        _NC_CACHE["nc"] = build_nc(SEQS, 6)
    nc = _NC_CACHE["nc"]
    consts = host_consts()
    wmaps = {nm: np.ascontiguousarray(np.asarray(inputs[nm], dtype=np.float32)) for nm, _ in WSPECS}
    in_maps = []
    for c in range(n):
        xc = np.concatenate([xp[c], xs[2 * c], xs[2 * c + 1]], axis=0)
        m = {"x": np.ascontiguousarray(xc)}
        m.update(wmaps)
        m.update(consts)
        in_maps.append(m)
    res = run_bass_kernel_spmd(nc, in_maps, core_ids=list(range(n)))
    yp = np.empty_like(xp)
    ys = np.empty_like(xs)
    for c in range(n):
        y = np.asarray(res.results[c]["y"], dtype=np.float32)
        yp[c] = y[0:4096]
        ys[2 * c] = y[4096:6144]
        ys[2 * c + 1] = y[6144:8192]
    return (yp, ys)
```
